# Optimizing a Trainium2 kernel written in Bass

```python
import jax, jax.numpy as jnp
from jax import lax
import numpy as np

D_MODEL = 1024
BATCH = 16
SEQ = 256
DEPTH = 2
DEC_BATCH = 8
DEC_SEQ = 1024
PAST_LEN = 512

GRID_W = 64
D_RNN = 1024
RNN_BLOCKS = 8
RNN_BW = D_RNN // RNN_BLOCKS
RG_C = 8.0
LRU_CONV_W = 4
LRU_PAD = (2, 1)
N_FOURIER_GROUPS = 4
FOURIER_GW = 128
D_FOURIER = N_FOURIER_GROUPS * FOURIER_GW
D_CONF = 512
CONF_CONV_W = 31
CONF_PAD = (15, 15)
D_SC = 512
SC_CONV_W = 3
SC_PAD = (1, 1)
N_BRANCH = 4
D_FF = 2816
FFN_CONV_W = 3
EPS = 1e-6
F32 = jnp.float32

D_IN = 2 * D_RNN + D_FOURIER + 2 * D_CONF + 3 * D_SC + N_BRANCH * D_MODEL
SPLIT_POINTS = (
    D_RNN,
    2 * D_RNN,
    2 * D_RNN + D_FOURIER,
    2 * D_RNN + D_FOURIER + 2 * D_CONF,
    2 * D_RNN + D_FOURIER + 2 * D_CONF + 3 * D_SC,
)

kernel_name = "hybrid_lru_fourier_conv_dit_step"


def rmsnorm(x, g):
    xf = x.astype(F32)
    y = xf * lax.rsqrt(jnp.mean(xf * xf, axis=-1, keepdims=True) + EPS)
    return (y * g.astype(F32)).astype(x.dtype)


def layernorm(x, g, b):
    xf = x.astype(F32)
    mu = jnp.mean(xf, axis=-1, keepdims=True)
    var = jnp.mean(jnp.square(xf - mu), axis=-1, keepdims=True)
    y = (xf - mu) * lax.rsqrt(var + EPS)
    return (y * g.astype(F32) + b.astype(F32)).astype(x.dtype)


def dwconv1d(x, w, b, pad):
    C = x.shape[-1]
    y = lax.conv_general_dilated(
        x, w[:, None, :].astype(x.dtype), window_strides=(1,), padding=[pad],
        dimension_numbers=("NWC", "WIO", "NWC"), feature_group_count=C)
    return y + b.astype(x.dtype)


def dwconv2d(x, w, b):
    C = x.shape[-1]
    y = lax.conv_general_dilated(
        x, w[:, :, None, :].astype(x.dtype), window_strides=(1, 1), padding=[(1, 1), (1, 1)],
        dimension_numbers=("NHWC", "HWIO", "NHWC"), feature_group_count=C)
    return y + b.astype(x.dtype)


def block_diag(x, w, b):
    xb = x.reshape(x.shape[:-1] + (RNN_BLOCKS, RNN_BW))
    y = jnp.einsum("blhi,hij->blhj", xb, w)
    return y.reshape(x.shape) + b


def _lin_combine(left, right):
    a_l, u_l = left
    a_r, u_r = right
    return a_r * a_l, a_r * u_l + u_r


def rglru_direction(xc, wa, ba, wx, bx, lam, h0, reverse):
    r = jax.nn.sigmoid(block_diag(xc, wa, ba).astype(F32))
    i = jax.nn.sigmoid(block_diag(xc, wx, bx).astype(F32))
    log_a = -RG_C * r * jax.nn.softplus(-lam.astype(F32))
    a = jnp.exp(log_a)
    u = jnp.sqrt(-jnp.expm1(2.0 * log_a)) * (i * xc)
    a_cum, u_cum = lax.associative_scan(_lin_combine, (a, u), reverse=reverse, axis=1)
    h = a_cum * h0[:, None, :] + u_cum
    final = h[:, 0] if reverse else h[:, -1]
    return h, final


def trunk_layer(x, cond, h0, p, grid_rows):
    B, L, _ = x.shape
    dt = x.dtype
    mod = (jax.nn.silu(cond.astype(F32)) @ p["w_mod"].astype(F32) + p["b_mod"].astype(F32)).astype(dt)
    mod = mod.reshape(B, 6, D_MODEL)
    sh1, sc1, g1, sh2, sc2, g2 = [mod[:, k, None, :] for k in range(6)]

    h = rmsnorm(x, p["norm1_g"]) * (1 + sc1) + sh1
    z = h @ p["w_in"]
    zx, zg, zf, zc, zs, zgate = jnp.split(z, SPLIT_POINTS, axis=-1)

    xc = dwconv1d(zx, p["lru_conv_w"], p["lru_conv_b"], LRU_PAD).astype(F32)
    h_f, s_f = rglru_direction(xc, p["lru_wa"][0], p["lru_ba"][0], p["lru_wx"][0], p["lru_bx"][0],
                               p["lru_lam"][0], h0[:, 0], False)
    h_b, s_b = rglru_direction(xc, p["lru_wa"][1], p["lru_ba"][1], p["lru_wx"][1], p["lru_bx"][1],
                               p["lru_lam"][1], h0[:, 1], True)
    y_a = (jax.nn.gelu(zg) * (h_f + h_b).astype(dt)) @ p["lru_out"]

    zf4 = zf.reshape(B, L, N_FOURIER_GROUPS, FOURIER_GW).astype(F32)
    yf = jnp.fft.fft2(zf4, axes=(1, 3), norm="ortho").real.astype(dt)
    y_b = yf.reshape(B, L, D_FOURIER) @ p["fourier_out"]

    za, zb = jnp.split(zc, 2, axis=-1)
    uc = dwconv1d(za * jax.nn.sigmoid(zb), p["conf_dw_w"], p["conf_dw_b"], CONF_PAD)
    y_c = jax.nn.silu(layernorm(uc, p["conf_ln_g"], p["conf_ln_b"])) @ p["conf_out"]

    xs, bs, cs = jnp.split(zs, 3, axis=-1)
    y_d = (bs * dwconv1d(cs * xs, p["sc_conv_w"], p["sc_conv_b"], SC_PAD)) @ p["sc_out"]

    gates = jax.nn.sigmoid(zgate.reshape(B, L, N_BRANCH, D_MODEL) + p["b_gate"])
    branches = jnp.stack([y_a, y_b, y_c, y_d], axis=2)
    merged = jnp.sum(gates * branches, axis=2)
    x = x + g1 * (merged @ p["w_o"])

    h2 = rmsnorm(x, p["norm2_g"]) * (1 + sc2) + sh2
    uf = h2 @ p["ffn_up"]
    if grid_rows is None:
        uf = dwconv1d(uf, p["ffn_conv_w"][1], p["ffn_conv_b"], (1, 1))
    else:
        uf = dwconv2d(uf.reshape(B, grid_rows, GRID_W, 2 * D_FF), p["ffn_conv_w"], p["ffn_conv_b"])
        uf = uf.reshape(B, L, 2 * D_FF)
    ff_a, ff_v = jnp.split(uf, 2, axis=-1)
    x = x + g2 * ((jax.nn.gelu(ff_a) * ff_v) @ p["ffn_down"])
    return x, jnp.stack([s_f, s_b], axis=1)


def setup_inputs(seed: int = 0) -> dict:
    key = jax.random.key(seed)
    ks = iter(jax.random.split(key, 40))
    nrm = lambda shape, s: jax.random.normal(next(ks), shape, F32) * s
    gain = lambda shape: 1.0 + 0.01 * jax.random.normal(next(ks), shape, F32)
    u = jax.random.uniform(next(ks), (DEPTH, 2, D_RNN), F32, 0.9, 0.999)
    a0 = u ** (1.0 / RG_C)
    lam = jnp.log(a0) - jnp.log1p(-a0)
    return {
        "x_prompt": nrm((BATCH, SEQ, D_MODEL), 1.0),
        "x_sample": nrm((DEC_BATCH, DEC_SEQ, D_MODEL), 1.0),
        "state_lru": nrm((DEC_BATCH, DEPTH, 2, D_RNN), 0.5),
        "c": nrm((DEC_BATCH, D_MODEL), 1.0),
        "c_ctx": nrm((D_MODEL,), 1.0),
        "norm1_g": gain((DEPTH, D_MODEL)),
        "norm2_g": gain((DEPTH, D_MODEL)),
        "w_mod": nrm((DEPTH, D_MODEL, 6 * D_MODEL), 0.5 * D_MODEL ** -0.5),
        "b_mod": nrm((DEPTH, 6 * D_MODEL), 0.01),
        "w_in": nrm((DEPTH, D_MODEL, D_IN), D_MODEL ** -0.5),
        "b_gate": nrm((DEPTH, N_BRANCH, D_MODEL), 0.01),
        "lru_conv_w": nrm((DEPTH, LRU_CONV_W, D_RNN), LRU_CONV_W ** -0.5),
        "lru_conv_b": nrm((DEPTH, D_RNN), 0.01),
        "lru_wa": nrm((DEPTH, 2, RNN_BLOCKS, RNN_BW, RNN_BW), RNN_BW ** -0.5),
        "lru_ba": nrm((DEPTH, 2, D_RNN), 0.01),
        "lru_wx": nrm((DEPTH, 2, RNN_BLOCKS, RNN_BW, RNN_BW), RNN_BW ** -0.5),
        "lru_bx": nrm((DEPTH, 2, D_RNN), 0.01),
        "lru_lam": lam,
        "lru_out": nrm((DEPTH, D_RNN, D_MODEL), D_RNN ** -0.5),
        "fourier_out": nrm((DEPTH, D_FOURIER, D_MODEL), D_FOURIER ** -0.5),
        "conf_dw_w": nrm((DEPTH, CONF_CONV_W, D_CONF), CONF_CONV_W ** -0.5),
        "conf_dw_b": nrm((DEPTH, D_CONF), 0.01),
        "conf_ln_g": gain((DEPTH, D_CONF)),
        "conf_ln_b": nrm((DEPTH, D_CONF), 0.01),
        "conf_out": nrm((DEPTH, D_CONF, D_MODEL), D_CONF ** -0.5),
        "sc_conv_w": nrm((DEPTH, SC_CONV_W, D_SC), SC_CONV_W ** -0.5),
        "sc_conv_b": nrm((DEPTH, D_SC), 0.01),
        "sc_out": nrm((DEPTH, D_SC, D_MODEL), D_SC ** -0.5),
        "w_o": nrm((DEPTH, D_MODEL, D_MODEL), D_MODEL ** -0.5),
        "ffn_up": nrm((DEPTH, D_MODEL, 2 * D_FF), D_MODEL ** -0.5),
        "ffn_conv_w": nrm((DEPTH, FFN_CONV_W, FFN_CONV_W, 2 * D_FF), 1.0 / FFN_CONV_W),
        "ffn_conv_b": nrm((DEPTH, 2 * D_FF), 0.01),
        "ffn_down": nrm((DEPTH, D_FF, D_MODEL), D_FF ** -0.5),
        "final_g": gain((D_MODEL,)),
    }


def reference(x_prompt, x_sample, state_lru, c, c_ctx, norm1_g, norm2_g, w_mod, b_mod, w_in, b_gate,
              lru_conv_w, lru_conv_b, lru_wa, lru_ba, lru_wx, lru_bx, lru_lam, lru_out, fourier_out,
              conf_dw_w, conf_dw_b, conf_ln_g, conf_ln_b, conf_out, sc_conv_w, sc_conv_b, sc_out, w_o,
              ffn_up, ffn_conv_w, ffn_conv_b, ffn_down, final_g):
    stacked = {
        "norm1_g": norm1_g, "norm2_g": norm2_g, "w_mod": w_mod, "b_mod": b_mod, "w_in": w_in,
        "b_gate": b_gate, "lru_conv_w": lru_conv_w, "lru_conv_b": lru_conv_b, "lru_wa": lru_wa,
        "lru_ba": lru_ba, "lru_wx": lru_wx, "lru_bx": lru_bx, "lru_lam": lru_lam, "lru_out": lru_out,
        "fourier_out": fourier_out, "conf_dw_w": conf_dw_w, "conf_dw_b": conf_dw_b,
        "conf_ln_g": conf_ln_g, "conf_ln_b": conf_ln_b, "conf_out": conf_out, "sc_conv_w": sc_conv_w,
        "sc_conv_b": sc_conv_b, "sc_out": sc_out, "w_o": w_o, "ffn_up": ffn_up,
        "ffn_conv_w": ffn_conv_w, "ffn_conv_b": ffn_conv_b, "ffn_down": ffn_down,
    }
    n_req = x_prompt.shape[0]
    cond_ctx = jnp.broadcast_to(c_ctx, (n_req, D_MODEL))
    rows = x_sample.shape[1] // GRID_W
    h0_ctx = jnp.zeros((n_req, 2, D_RNN), F32)

    xp = x_prompt
    xs = x_sample
    ctx_states = []
    for l in range(DEPTH):
        p = {k: v[l] for k, v in stacked.items()}
        xp, s_ctx = trunk_layer(xp, cond_ctx, h0_ctx, p, None)
        ctx_states.append(s_ctx)
        xs, _ = trunk_layer(xs, c, state_lru[:, l].astype(F32), p, rows)

    y_prompt = rmsnorm(xp, final_g)
    y_sample = rmsnorm(xs, final_g)
    new_state_lru = jnp.stack(ctx_states, axis=1)
    return (y_prompt, y_sample, new_state_lru)
```

```python
import numpy as np
from contextlib import ExitStack
import concourse.bass as bass
import concourse.mybir as mybir
from concourse.bass_utils import run_bass_kernel_spmd

F32 = mybir.dt.float32
BF16 = mybir.dt.bfloat16
AF = mybir.ActivationFunctionType
ALU = mybir.AluOpType

D = 1024
DEPTH = 2
D_IN = 9216
D_FF = 2816
NCORES = 8
EPS = 1e-6

_PLIST = [("norm1_g", 8), ("norm2_g", 8), ("b_mod", 48), ("b_gate", 32), ("lru_conv_w", 32), ("lru_conv_b", 8),
          ("lru_ba", 16), ("lru_bx", 16), ("lru_lam", 16), ("conf_dw_w", 124), ("conf_dw_b", 4), ("conf_ln_g", 4),
          ("conf_ln_b", 4), ("sc_conv_w", 12), ("sc_conv_b", 4), ("ffn_conv_w", 396), ("ffn_conv_b", 44)]
PL = {}
_o = 0
for _n, _c in _PLIST:
    PL[_n] = (_o, _c)
    _o += _c
NP = _o
_DLIST = [("hba", 16), ("hbx", 16), ("hkk", 16), ("hbg", 32), ("cwh", 124)]
DL = {}
_o = 0
for _n, _c in _DLIST:
    DL[_n] = (_o, _c)
    _o += _c
ND = _o
NG = 56


class Buf:
    __slots__ = ("name", "w", "r", "sem", "semv")

    def __init__(self, name):
        self.name = name
        self.w = None
        self.r = []
        self.sem = None
        self.semv = 0


class Sched:
    ENG = ("pe", "act", "dve", "pool", "sp")

    def __init__(self, nc, ctx):
        self.nc = nc
        self.ctx = ctx
        self.q = {e: [] for e in self.ENG}
        self.cnt = {e: 0 for e in self.ENG}
        self.esem = {e: ctx.enter_context(nc.semaphore("s_" + e)) for e in self.ENG}
        self.waited = {}
        self.dma_bufs = []
        self.pending = {e: [] for e in self.ENG}
        self.phase = "init"
        self.pe_log = []

    def _need(self, eng, tok, waits):
        sem, val, teng = tok
        if teng == eng and eng == "pe":
            return
        key = (eng, sem.name)
        if self.waited.get(key, 0) >= val:
            return
        self.waited[key] = val
        waits.append((sem, val))

    def op(self, eng, fn, reads=(), writes=(), dma_buf=None, n_dma=1):
        waits = self.pending[eng]
        self.pending[eng] = []
        for b in reads:
            if b.w is not None:
                self._need(eng, b.w, waits)
        for b in writes:
            if b.w is not None:
                self._need(eng, b.w, waits)
            for t in b.r:
                self._need(eng, t, waits)
        if dma_buf is not None:
            if dma_buf.sem is None:
                dma_buf.sem = self.ctx.enter_context(self.nc.semaphore("d_" + dma_buf.name))
                self.dma_bufs.append(dma_buf)
            dma_buf.semv += 16 * n_dma
            tok = (dma_buf.sem, dma_buf.semv, "dma")
            self.q[eng].append((waits, fn, dma_buf.sem))
        else:
            self.cnt[eng] += 1
            tok = (self.esem[eng], self.cnt[eng], eng)
            self.q[eng].append((waits, fn, None))
        for b in writes:
            b.w = tok
            b.r = []
        for b in reads:
            b.r.append(tok)
        return tok

    def fence(self):
        for e in self.ENG:
            for e2 in self.ENG:
                if self.cnt[e2] > 0:
                    self._need(e, (self.esem[e2], self.cnt[e2], e2 if e2 != "pe" else "x"), self.pending[e])
            for b in self.dma_bufs:
                self._need(e, (b.sem, b.semv, "dma"), self.pending[e])

    def emit(self):
        nc = self.nc
        handles = {"pe": "tensor", "act": "scalar", "dve": "vector", "pool": "gpsimd", "sp": "sync"}
        self.fence()
        with nc.Block() as block:
            for e in self.ENG:
                def body(h, e=e):
                    sem_e = self.esem[e]
                    for waits, fn, dsem in self.q[e]:
                        for (s, v) in waits:
                            h.wait_ge(s, v)
                        if dsem is not None:
                            for ins in fn(h):
                                ins.then_inc(dsem, 16)
                        else:
                            fn(h).then_inc(sem_e, 1)
                    for (s, v) in self.pending[e]:
                        h.wait_ge(s, v)
                getattr(block, handles[e])(body)


class Slot:
    __slots__ = ("b", "t", "name")

    def __init__(self, b, t, name):
        self.b = b
        self.t = t
        self.name = name


class FifoPool:
    def __init__(self, items):
        self.free_list = list(items)

    def alloc(self):
        assert self.free_list, "pool exhausted"
        return self.free_list.pop(0)

    def free(self, it):
        self.free_list.append(it)


def build_nc(order=None):
    nc = bass.Bass("TRN2", target_bir_lowering=False)
    dram = {}

    def din(name, shape):
        dram[name] = nc.dram_tensor(name, list(shape), F32, kind="ExternalInput").ap()
        return dram[name]

    xP = din("xP", [D, 512])
    xS = din("xS", [D, 1024])
    pp_d = din("pp", [DEPTH, 128, NP])
    pg_d = din("pg", [128, NG])
    w_mod = din("w_mod", [DEPTH, D, 6 * D])
    w_in = din("w_in", [DEPTH, D, D_IN])
    lru_wa = din("lru_wa", [DEPTH, 2, 8, 128, 128])
    lru_wx = din("lru_wx", [DEPTH, 2, 8, 128, 128])
    lru_out = din("lru_out", [DEPTH, D, D])
    fourier_out = din("fourier_out", [DEPTH, 512, D])
    conf_out = din("conf_out", [DEPTH, 512, D])
    sc_out = din("sc_out", [DEPTH, 512, D])
    w_o = din("w_o", [DEPTH, D, D])
    ffn_up = din("ffn_up", [DEPTH, D, 2 * D_FF])
    ffn_down = din("ffn_down", [DEPTH, D_FF, D])
    cs128_d = din("cs128", [128, 256])
    dft256_d = din("dft256", [256, 512])
    dftc_d = din("dftc", [1024, 512])
    dfts_d = din("dfts", [1024, 512])
    ident_d = din("ident", [128, 128])
    yP = nc.dram_tensor("yP", [128, 8, 512], F32, kind="ExternalOutput").ap()
    yS = nc.dram_tensor("yS", [128, 8, 1024], F32, kind="ExternalOutput").ap()
    stO = nc.dram_tensor("stO", [128, 64], F32, kind="ExternalOutput").ap()

    with ExitStack() as ctx:
        S = Sched(nc, ctx)

        def sb(name, shape, dt):
            return ctx.enter_context(nc.sbuf_tensor(name, list(shape), dt))

        xT = sb("xT", [128, 8, 1024], F32)
        hT = sb("hT", [128, 8, 1024], BF16)
        R1 = sb("R1", [128, 22 * 1024], BF16)
        merged = R1[:, 0:8192].rearrange("p (c t) -> p c t", c=8)
        act1 = R1[:, 8192:16384].rearrange("p (c t) -> p c t", c=8)
        act1f = R1[:, 8192:16384].bitcast(F32).rearrange("p (c t) -> p c t", c=4)
        act2 = R1[:, 16384:20480].rearrange("p (c t) -> p c t", c=4)
        ffact = R1[:, :].rearrange("p (c t) -> p c t", c=22)
        NWS = 5
        wsl = [sb("ws%d" % i, [128, 8, 512], BF16) for i in range(NWS)]
        NTMP = 18
        tmps = [sb("tmp%d" % i, [128, 512], F32) for i in range(NTMP)]
        NPAD = 5
        pads = [sb("pad%d" % i, [128, 1200], BF16) for i in range(NPAD)]
        NDG = 4
        dgs = [sb("dg%d" % i, [128, 9, 128], BF16) for i in range(NDG)]
        UTs = [sb("ut%d" % i, [128, 8, 256], BF16) for i in range(1)]
        pp = sb("ppt", [128, DEPTH, NP], F32)
        dpl = sb("dpl", [128, DEPTH, ND], F32)
        pg = sb("pgt", [128, NG], F32)
        der = sb("der", [128, 32], F32)
        scond = sb("scond", [128, 8, 2], BF16)
        modTall = sb("modTall", [128, DEPTH, 2, 48], F32)
        consts = sb("consts", [128, 8], F32)
        identf = sb("identf", [128, 128], F32)
        ident = sb("identb", [128, 128], BF16)
        onesb = sb("onesb", [128, 128], BF16)
        cs128 = sb("cs128t", [128, 256], BF16)
        d256 = sb("d256t", [128, 2, 512], BF16)
        stT = sb("stT", [128, 64], F32)
        small = sb("small", [128, 64], F32)
        psb = [ctx.enter_context(nc.psum_tensor("ps%d" % i, [128, 512], F32)) for i in range(8)]

        ws_pool = FifoPool([Slot(Buf("ws%d" % i), wsl[i], "ws%d" % i) for i in range(NWS)])
        tmp_pool = FifoPool([Slot(Buf("tmp%d" % i), tmps[i], "tmp%d" % i) for i in range(NTMP)])
        pad_pool = FifoPool([Slot(Buf("pad%d" % i), pads[i], "pad%d" % i) for i in range(NPAD)])
        dg_pool = FifoPool([Slot(Buf("dg%d" % i), dgs[i], "dg%d" % i) for i in range(NDG)])
        ut_pool = FifoPool([Slot(Buf("ut%d" % i), UTs[i], "ut%d" % i) for i in range(1)])
        ps_pool = FifoPool([Slot(Buf("ps%d" % i), psb[i], "ps%d" % i) for i in range(8)])
        _xr = [R1[:, 1024 * j:1024 * (j + 1)].bitcast(F32) for j in list(range(8)) + list(range(20, 22))]
        lru_extra = [Slot(Buf("xtmp%d" % i), _xr[i], "xtmp%d" % i) for i in range(len(_xr))]
        b_xT = [[Buf("xT%d_%d" % (c, t)) for t in range(2)] for c in range(8)]
        b_hT = [[Buf("hT%d_%d" % (c, t)) for t in range(2)] for c in range(8)]
        b_mg = [[Buf("mg%d_%d" % (c, t)) for t in range(2)] for c in range(8)]
        b_a1 = [[Buf("a1%d_%d" % (c, t)) for t in range(2)] for c in range(8)]
        b_a2 = [[Buf("a2%d_%d" % (c, t)) for t in range(2)] for c in range(4)]
        b_ffx = [[Buf("ff%d_%d" % (c, t)) for t in range(2)] for c in range(2)]
        b_ff = [b_mg[j] if j < 8 else (b_a1[j - 8] if j < 16 else (b_a2[j - 16] if j < 20 else b_ffx[j - 20]))
                for j in range(22)]

        _xal = [b_mg[j] for j in range(8)] + [b_ffx[j] for j in range(2)]

        def alias_import():
            for sl, bufs in zip(lru_extra, _xal):
                toks = []
                for b in bufs:
                    if b.w is not None:
                        toks.append(b.w)
                    toks += b.r
                sl.b.w = None
                sl.b.r = toks

        def alias_export():
            for sl, bufs in zip(lru_extra, _xal):
                toks = list(sl.b.r) + ([sl.b.w] if sl.b.w is not None else [])
                for b in bufs:
                    b.r = b.r + toks


        def r1f_bufs(c, t):
            idx = 2 * c + t
            lst = b_mg[idx] if idx < 8 else b_a1[idx - 8]
            return [lst[0], lst[1]]
        b_pp, b_dpl, b_pg, b_der, b_scond = Buf("pp"), Buf("dpl"), Buf("pg"), Buf("der"), Buf("scond")
        b_modl = [Buf("mod0"), Buf("mod1")]
        b_const, b_ident, b_identf, b_cs, b_d256, b_st, b_small = (Buf("const"), Buf("ident"), Buf("identf"), Buf("cs"),
                                                                    Buf("d256"), Buf("st"), Buf("small"))
        b_out = Buf("out")
        b_st2 = Buf("st2")

        def act(out, in_, func, bias=None, scale=1.0, R=(), W=()):
            def f(h):
                if bias is None:
                    return h.activation(out=out, in_=in_, func=func, scale=scale)
                return h.activation(out=out, in_=in_, func=func, bias=bias, scale=scale)
            S.op("act", f, reads=R, writes=W)

        def tt(out, in0, in1, op, R=(), W=(), eng="dve"):
            S.op(eng, lambda h: h.tensor_tensor(out=out, in0=in0, in1=in1, op=op), reads=R, writes=W)

        def ts(out, in0, s1, s2, op0, op1=None, R=(), W=(), eng="dve"):
            def f(h):
                if op1 is None:
                    return h.tensor_scalar(out=out, in0=in0, scalar1=s1, scalar2=None, op0=op0)
                return h.tensor_scalar(out=out, in0=in0, scalar1=s1, scalar2=s2, op0=op0, op1=op1)
            S.op(eng, f, reads=R, writes=W)

        def stt(out, in0, scalar, in1, op0, op1, R=(), W=()):
            S.op("dve", lambda h: h.scalar_tensor_tensor(out=out, in0=in0, scalar=scalar, in1=in1, op0=op0, op1=op1),
                 reads=R, writes=W)

        def mm(ps, pairs, R, start=True, stop=True):
            def f(h):
                n = len(pairs)
                ins = None
                for i, (l, r) in enumerate(pairs):
                    ins = h.matmul(ps.t[:, :] if not isinstance(ps, tuple) else ps[1], lhsT=l, rhs=r,
                                   start=(start and i == 0), stop=(stop and i == n - 1))
                return ins
            b = ps.b if not isinstance(ps, tuple) else ps[0].b
            S.pe_log.append((S.phase, len(pairs)))
            S.op("pe", f, reads=R, writes=[b])

        def dma_load(eng, dst_buf, pairs):
            def f(h):
                return [h.dma_start(out=o, in_=i) for (o, i) in pairs]
            S.op(eng, f, writes=[dst_buf], dma_buf=dst_buf, n_dma=len(pairs))

        cz, c1, ceps, cq, chalf = (consts[:, 0:1], consts[:, 1:2], consts[:, 2:3], consts[:, 3:4], consts[:, 4:5])

        for i, v in enumerate([0.0, 1.0, EPS, 0.25, 0.5]):
            S.op("dve", lambda h, i=i, v=v: h.memset(consts[:, i:i + 1], v), writes=[b_const])
        S.op("dve", lambda h: h.memset(onesb[:, :], 1.0), writes=[b_ident])
        dma_load("sp", b_identf, [(identf[:, :], ident_d)])
        S.op("dve", lambda h: h.tensor_copy(out=ident[:, :], in_=identf[:, :]), reads=[b_identf], writes=[b_ident])
        dma_load("sp", b_pp, [(pp[:, l, :], pp_d[l]) for l in range(DEPTH)])
        dma_load("sp", b_pg, [(pg[:, :], pg_d)])
        dma_load("pool", b_cs, [(cs128[:, :], cs128_d)])
        dma_load("pool", b_d256, [(d256[:, :, :], dft256_d.rearrange("(t p) n -> p t n", p=128))])
        for i in range(NPAD):
            S.op("pool", lambda h, i=i: h.memset(pads[i][:, :], 0.0), writes=[pad_pool.free_list[i].b])

        def pcol(l, name, a=0, n=None):
            o, c = PL[name]
            if n is None:
                n = c - a
            return pp[:, l, o + a:o + a + n]

        def dcol(l, name, a=0, n=None):
            o, c = DL[name]
            if n is None:
                n = c - a
            return dpl[:, l, o + a:o + a + n]

        for l in range(DEPTH):
            ts(dcol(l, "hba"), pcol(l, "lru_ba"), 0.5, None, ALU.mult, R=[b_pp], W=[b_dpl])
            ts(dcol(l, "hbx"), pcol(l, "lru_bx"), 0.5, None, ALU.mult, R=[b_pp], W=[b_dpl])
            ts(dcol(l, "hbg"), pcol(l, "b_gate"), 0.5, None, ALU.mult, R=[b_pp], W=[b_dpl])
            ts(dcol(l, "cwh"), pcol(l, "conf_dw_w"), 0.5, None, ALU.mult, R=[b_pp], W=[b_dpl])
            e_ = small[:, 0:16]
            p_ = small[:, 16:32]
            act(e_, pcol(l, "lru_lam"), AF.Exp, scale=-1.0, R=[b_pp], W=[b_small])
            ts(p_, e_, -0.2, 0.25, ALU.mult, ALU.add, R=[b_small], W=[b_small])
            for cst in (1.0 / 3.0, 0.5, 1.0):
                tt(p_, p_, e_, ALU.mult, R=[b_small], W=[b_small])
                ts(p_, p_, -1.0, cst, ALU.mult, ALU.add, R=[b_small], W=[b_small])
            tt(p_, p_, e_, ALU.mult, R=[b_small], W=[b_small])
            ts(dcol(l, "hkk"), p_, -4.0, None, ALU.mult, R=[b_small], W=[b_dpl])

        for ci in range(2):
            act(scond[:, :, ci], pg[:, 8 + 8 * ci:16 + 8 * ci], AF.Silu, R=[b_pg], W=[b_scond])

        wq = []
        wstate = {"next_issue": 0, "next_get": 0, "loaded": {}}

        def w_issue():
            while wstate["next_issue"] < len(wq) and ws_pool.free_list and \
                    wstate["next_issue"] - wstate["next_get"] < NWS:
                i = wstate["next_issue"]
                slot = ws_pool.alloc()
                name, pf = wq[i]
                dma_load("pool", slot.b, pf(slot.t))
                wstate["loaded"][i] = slot
                wstate["next_issue"] += 1

        def w_get(name):
            recorded.append(name)
            if order is None:
                slot = ws_pool.alloc()
                dma_load("pool", slot.b, wq_defs[name](slot.t))
                wstate["next_get"] += 1
                return slot
            i = wstate["next_get"]
            assert wq[i][0] == name, (wq[i][0], name)
            if i not in wstate["loaded"]:
                w_issue()
            assert i in wstate["loaded"], "no free weight slot for " + name
            wstate["next_get"] += 1
            slot = wstate["loaded"].pop(i)
            w_issue()
            return slot

        def w_free(slot):
            ws_pool.free(slot)
            if order is not None:
                w_issue()

        def kview(w2d):
            return w2d.rearrange("(k p) n -> p k n", p=128)

        def q_std(name, w2d, c0, ncols, nk=8, k0=0):
            def pf(t, w2d=w2d):
                return [(t[:, 0:nk, 0:ncols], kview(w2d)[:, k0:k0 + nk, c0:c0 + ncols])]
            wq.append((name, pf))

        def q_wide(name, w2d):
            def pf(t, w2d=w2d):
                tv = t[:, :, :].rearrange("p a b -> p (a b)").rearrange("p (k n) -> p k n", k=4)
                return [(tv, kview(w2d))]
            wq.append((name, pf))

        def build_wq(l, kind):
            tag = "%s%d_" % (kind, l)
            if kind == "P" and l == 0:
                for g in range(12):
                    q_std("mod0_%d" % g, w_mod[0], g * 512, 512)
            for g in (2, 3, 0, 1):
                q_std(tag + "in%d" % g, w_in[l], g * 512, 512)

            def pf_bd(t, l=l):
                tv = t[:, :, :].rearrange("p a b -> p (a b)").rearrange("p (k n) -> p k n", k=32)
                return [(tv[:, 0:16, :], lru_wa[l].rearrange("d h i j -> i (d h) j")),
                        (tv[:, 16:32, :], lru_wx[l].rearrange("d h i j -> i (d h) j"))]
            wq.insert(len(wq) - 2, (tag + "bd", pf_bd))
            q_std(tag + "in4", w_in[l], 4 * 512, 512)
            if kind == "S":
                q_std(tag + "dfc", dftc_d, 0, 512)
                q_std(tag + "dfs", dfts_d, 0, 512)
            for hh in range(2):
                q_std(tag + "lruout%d" % hh, lru_out[l], hh * 512, 512)
                q_std(tag + "in%d" % (10 + hh), w_in[l], (10 + hh) * 512, 512)
            q_wide(tag + "fourier_out", fourier_out[l])
            for hh in range(2):
                q_std(tag + "in%d" % (12 + hh), w_in[l], (12 + hh) * 512, 512)
            for g in (6, 5):
                q_std(tag + "in%d" % g, w_in[l], g * 512, 512)
            for g in (9, 7):
                q_std(tag + "in%d" % g, w_in[l], g * 512, 512)
            q_wide(tag + "conf_out", conf_out[l])
            for hh in range(2):
                q_std(tag + "in%d" % (14 + hh), w_in[l], (14 + hh) * 512, 512)
            q_std(tag + "in8", w_in[l], 8 * 512, 512)
            q_wide(tag + "sc_out", sc_out[l])
            for hh in range(2):
                q_std(tag + "in%d" % (16 + hh), w_in[l], (16 + hh) * 512, 512)
            for hh in range(2):
                q_std(tag + "wo%d" % hh, w_o[l], hh * 512, 512)
            for q in range(11):
                def pf_up(t, l=l, q=q):
                    v = kview(ffn_up[l])
                    return [(t[:, :, 0:256], v[:, :, 256 * q:256 * q + 256]),
                            (t[:, :, 256:512], v[:, :, D_FF + 256 * q:D_FF + 256 * q + 256])]
                wq.append((tag + "up%d" % q, pf_up))
                if kind == "P" and l == 0:
                    q_std("mod1_%d" % q, w_mod[1], q * 512, 512)
            if kind == "P" and l == 0:
                q_std("mod1_11", w_mod[1], 11 * 512, 512)
            for hh in range(2):
                for kg in range(3):
                    nk = 8 if kg < 2 else 6
                    q_std(tag + "down%d_%d" % (hh, kg), ffn_down[l], hh * 512, 512, nk=nk, k0=8 * kg)

        for kind in ("P", "S"):
            for l in range(DEPTH):
                build_wq(l, kind)
        wq_defs = dict(wq)
        recorded = []
        if order is not None:
            wq[:] = [(n_, wq_defs[n_]) for n_ in order]
            assert len(wq) == len(wq_defs)

        def mod_steps(l):
            tag = "mod%d_" % l

            def finish(row, g):
                ps2 = ps_pool.alloc()

                def f(h, row=row, ps2=ps2):
                    ins = None
                    for j in range(4):
                        ins = h.matmul(ps2.t[:, 2 * j:2 * j + 2], lhsT=row.t[0:2, j * 128:(j + 1) * 128],
                                       rhs=identf[0:2, 0:2], start=True, stop=True)
                    return ins
                S.pe_log.append((S.phase, 4))
                S.op("pe", f, reads=[row.b, b_identf], writes=[ps2.b])
                tt(modTall[:, l, :, 4 * g:4 * g + 4], ps2.t[:, 0:8].rearrange("p (j n) -> p n j", n=2),
                   pcol(l, "b_mod", 4 * g, 4).unsqueeze(1).to_broadcast([128, 2, 4]), ALU.add,
                   R=[ps2.b, b_pp], W=[b_modl[l]])
                ps_pool.free(ps2)
                tmp_pool.free(row)

            prev = None
            for g in range(12):
                slot = w_get(tag + "%d" % g)
                ps = ps_pool.alloc()
                mm((ps, ps.t[0:2, :]), [(scond[:, k, :], slot.t[:, k, :]) for k in range(8)], R=[slot.b, b_scond])
                w_free(slot)
                row = tmp_pool.alloc()
                act(row.t[0:2, :], ps.t[0:2, :], AF.Copy, R=[ps.b], W=[row.b])
                ps_pool.free(ps)
                if prev is not None:
                    finish(*prev)
                prev = (row, g)
                if g == 11:
                    finish(*prev)
                yield g

        def run_pass(kind):
            P = (kind == "P")
            NT = 1 if P else 2
            NTOK = 512 * NT
            L = 256 if P else 1024
            nseq = 2 if P else 1
            xsrc = xP if P else xS

            def v(ap):
                return ap.rearrange("p (s l) -> p s l", s=2) if P else ap

            def padv(pad_t, pl, pr, t, k):
                W = pl + L + pr
                if P:
                    return pad_t[:, 0:2 * W].rearrange("p (s w) -> p s w", s=2)[:, :, k:k + 256]
                return pad_t[:, t * 512 + k:t * 512 + k + 512]

            def v2(ap):
                if P:
                    return ap.rearrange("p (s l) -> p s l", s=2)
                return ap.rearrange("p (r c) -> p r c", c=64)

            def padv2(pad_t, t, kr, kc):
                if P:
                    return pad_t[:, 0:2 * 258].rearrange("p (s w) -> p s w", s=2)[:, :, kc:kc + 256]
                return pad_t[:, 0:18 * 66].rearrange("p (r c) -> p r c", c=66)[:, 8 * t + kr:8 * t + kr + 8, kc:kc + 64]

            ffn_taps = [(1, kc) for kc in range(3)] if P else [(kr, kc) for kr in range(3) for kc in range(3)]

            def tsl(t):
                return slice(t * 512, (t + 1) * 512)

            for c in range(8):
                for t in range(NT):
                    if (not P) and t == 1:
                        continue
                    dma_load("sp", b_xT[c][t], [(xT[:, c, tsl(t)], xsrc[c * 128:(c + 1) * 128, tsl(t)])])
            if P:
                for c in range(8):
                    dma_load("sp", b_xT[c][1], [(xT[:, c, tsl(1)], xS[c * 128:(c + 1) * 128, tsl(1)])])

            def zmm(ps, wslot, jj, t):
                mm(ps, [(wslot.t[:, k, jj * 128:(jj + 1) * 128], hT[:, k, tsl(t)]) for k in range(8)],
                   R=[wslot.b] + [b_hT[k][t] for k in range(8)])

            def build_diag(dg, wap, ntaps):
                S.op("pool", lambda h: h.tensor_tensor(
                    out=dg.t[:, 0:ntaps, :],
                    in0=ident[:, :].unsqueeze(1).to_broadcast([128, ntaps, 128]),
                    in1=wap.unsqueeze(2).to_broadcast([128, ntaps, 128]), op=ALU.mult),
                    reads=[b_ident, b_pp, b_dpl], writes=[dg.b])

            def stats_begin():
                return {"ps": [ps_pool.alloc() for _ in range(NT)], "n": [0] * NT}

            def stats_add(st, c, t):
                sq = tmp_pool.alloc()
                sqb = sq.t[:, :].bitcast(BF16)[:, 0:512]
                act(sqb, xT[:, c, tsl(t)], AF.Square, R=[b_xT[c][t]], W=[sq.b])
                mm(st["ps"][t], [(onesb[:, :], sqb)], R=[b_ident, sq.b], start=(st["n"][t] == 0), stop=(st["n"][t] == 7))
                st["n"][t] += 1
                tmp_pool.free(sq)

            def stats_add_delayed(st, c, t, lag=3):
                q_ = st.setdefault("q", [])
                q_.append((c, t))
                while len(q_) > lag:
                    stats_add(st, *q_.pop(0))

            def stats_flush(st):
                for ct in st.pop("q", []):
                    stats_add(st, *ct)

            def rmsnorm_to(l, gcol, shcol, dst_fn, dst_bufs, dst_buf_fn=None, torder=None, st=None, per_tile_cb=None):
                order_ = list(torder) if torder is not None else list(range(NT))
                if st is None:
                    st = stats_begin()
                    for t in order_:
                        for c in range(8):
                            stats_add(st, c, t)
                else:
                    stats_flush(st)
                assert all(n_ == 8 for n_ in st["n"])
                rss = {}
                for t in order_:
                    ps = st["ps"][t]
                    rs = tmp_pool.alloc()
                    act(rs.t[:, :], ps.t[:, :], AF.Ln, bias=ceps, scale=1.0 / D, R=[ps.b, b_const], W=[rs.b])
                    ps_pool.free(ps)
                    act(rs.t[:, :], rs.t[:, :], AF.Exp, scale=-0.5, R=[rs.b], W=[rs.b])
                    rss[t] = rs
                for t in order_:
                    rs = rss[t]
                    for c in range(8):
                        tm = tmp_pool.alloc()
                        tt(tm.t[:, :], xT[:, c, tsl(t)], rs.t[:, :], ALU.mult, R=[b_xT[c][t], rs.b], W=[tm.b])
                        if shcol is None:
                            act(dst_fn(c, t), tm.t[:, :], AF.Identity, bias=cz, scale=gcol(c),
                                R=[tm.b, b_pg, b_der, b_const], W=dst_buf_fn(c, t))
                        else:
                            act(dst_fn(c, t), tm.t[:, :], AF.Identity, bias=shcol(c), scale=gcol(c),
                                R=[tm.b, b_der, b_modl[l]], W=[dst_bufs[c][t]])
                        tmp_pool.free(tm)
                    tmp_pool.free(rs)
                    if per_tile_cb is not None:
                        per_tile_cb(t)

            def out_and_gate(l, tag, b, get_out, free_out, wcol, kchunks, act_ap, act_bufs):
                for hh in range(2):
                    wout = get_out(hh)
                    gslot = w_get(tag + "in%d" % (10 + 2 * b + hh))
                    wflat = wout.t[:, :, :].rearrange("p a b -> p (a b)")
                    tgs = {}
                    for t in range(NT):
                        for mq in range(4):
                            m = hh * 4 + mq
                            psg = ps_pool.alloc()
                            zmm(psg, gslot, mq, t)
                            tg = tmp_pool.alloc()
                            act(tg.t[:, :], psg.t[:, :], AF.Tanh, bias=dcol(l, "hbg", b * 8 + m, 1), scale=0.5,
                                R=[psg.b, b_dpl], W=[tg.b])
                            ps_pool.free(psg)
                            tgs[(mq, t)] = tg
                    w_free(gslot)
                    for t in range(NT):
                        for mq in range(4):
                            m = hh * 4 + mq
                            tg = tgs.pop((mq, t))
                            psy = ps_pool.alloc()
                            mm(psy, [(wflat[:, wcol(k, m):wcol(k, m) + 128], act_ap(k, t)) for k in range(kchunks)],
                               R=[wout.b] + [act_bufs[k][t] for k in range(kchunks)])
                            if b == 0:
                                stt(merged[:, m, tsl(t)], tg.t[:, :], 1.0, psy.t[:, :], ALU.add, ALU.mult,
                                    R=[tg.b, psy.b], W=[b_mg[m][t]])
                            else:
                                stt(tg.t[:, :], tg.t[:, :], 1.0, psy.t[:, :], ALU.add, ALU.mult,
                                    R=[tg.b, psy.b], W=[tg.b])
                                tt(merged[:, m, tsl(t)], merged[:, m, tsl(t)], tg.t[:, :], ALU.add,
                                   R=[tg.b, b_mg[m][t]], W=[b_mg[m][t]])
                            ps_pool.free(psy)
                            tmp_pool.free(tg)
                    free_out(hh)

            def wide_out(tag, name):
                hold = {}

                def get_out(hh):
                    if "w" not in hold:
                        hold["w"] = w_get(tag + name)
                    return hold["w"]

                def free_out(hh):
                    if hh == 1:
                        w_free(hold.pop("w"))
                return get_out, free_out, (lambda k, m: k * 1024 + m * 128)

            def layer(l):
                tag = "%s%d_" % (kind, l)
                if P and l == 0:
                    S.phase = "P0:mod"
                    for _ in mod_steps(0):
                        pass
                modT = modTall[:, l, 0 if P else 1, :]
                mgen = mod_steps(1) if (P and l == 0) else iter(())
                stt(der[:, 0:8], modT[:, 8:16], 1.0, pcol(l, "norm1_g"), ALU.add, ALU.mult, R=[b_modl[l], b_pp], W=[b_der])
                stt(der[:, 8:16], modT[:, 32:40], 1.0, pcol(l, "norm2_g"), ALU.add, ALU.mult, R=[b_modl[l], b_pp], W=[b_der])
                ts(der[:, 16:24], modT[:, 16:24], 0.5, None, ALU.mult, R=[b_modl[l]], W=[b_der])

                S.phase = kind + str(l) + ":norm1"
                rmsnorm_to(l, lambda c: der[:, c:c + 1], lambda c: modT[:, c:c + 1],
                           lambda c, t: hT[:, c, tsl(t)], b_hT,
                           torder=((1, 0) if (not P and l == 0) else None), st=pend["st"])
                pend["st"] = None

                S.phase = kind + str(l) + ":geluzg"
                for g in (2, 3):
                    slot = w_get(tag + "in%d" % g)
                    for jj in range(4):
                        c = (g - 2) * 4 + jj
                        for t in range(NT):
                            ps = ps_pool.alloc()
                            zmm(ps, slot, jj, t)
                            act(act1[:, c, tsl(t)], ps.t[:, :], AF.Gelu, R=[ps.b], W=[b_a1[c][t]])
                            ps_pool.free(ps)
                    w_free(slot)
                def fft_gen():
                    ph_ = [None]

                    def enter():
                        ph_[0] = S.phase
                        S.phase = kind + str(l) + ":fft"

                    def leave():
                        S.phase = ph_[0]
                    enter()
                    slot = w_get(tag + "in4")
                    zfs = []
                    for g in range(4):
                        zf = tmp_pool.alloc()
                        zfb = zf.t[:, :].bitcast(BF16)
                        for t in range(NT):
                            ps = ps_pool.alloc()
                            zmm(ps, slot, g, t)
                            act(zfb[:, tsl(t)], ps.t[:, :], AF.Copy, R=[ps.b], W=[zf.b])
                            ps_pool.free(ps)
                        zfs.append(zf)
                    w_free(slot)
                    leave()
                    yield
                    nlt = NTOK // 128
                    st_ = {}

                    def stage1(g):
                        zf = zfs[g]
                        zfb = zf.t[:, :].bitcast(BF16)
                        ut = ut_pool.alloc()
                        for lp in range(nlt // 2):
                            ps = ps_pool.alloc()

                            def f(h, ps=ps, lp=lp, zfb=zfb):
                                ins = None
                                for q in range(2):
                                    lt = 2 * lp + q
                                    if P:
                                        lh = zfb[:, lt * 128:(lt + 1) * 128]
                                    else:
                                        par_, blk_ = divmod(lt, 4)
                                        lh = zfb[:, 256 * blk_ + par_:256 * blk_ + 256:2]
                                    ins = h.matmul(ps.t[:, q * 256:(q + 1) * 256], lhsT=lh,
                                                   rhs=cs128[:, :], start=True, stop=True)
                                return ins
                            S.pe_log.append((S.phase, 2))
                            S.op("pe", f, reads=[zf.b, b_cs], writes=[ps.b])
                            act(ut.t[:, 2 * lp:2 * lp + 2, :], ps.t[:, :].rearrange("p (a b) -> p a b", a=2), AF.Copy,
                                R=[ps.b], W=[ut.b])
                            ps_pool.free(ps)
                        st_["ut"] = ut

                    def stage2(k):
                        ut = st_.pop("ut")
                        if P:
                            g = k
                            ps = ps_pool.alloc()
                            for sq in range(2):
                                pairs = []
                                for lt in range(2):
                                    pairs.append((ut.t[:, 2 * sq + lt, 0:128], d256[:, lt, 0:256]))
                                    pairs.append((ut.t[:, 2 * sq + lt, 128:256], d256[:, lt, 256:512]))
                                mm((ps, ps.t[:, sq * 256:(sq + 1) * 256]), pairs, R=[ut.b, b_d256])
                            act(act2[:, g, 0:512], ps.t[:, :], AF.Copy, R=[ps.b], W=[b_a2[g][0]])
                            ps_pool.free(ps)
                        else:
                            g = k
                            if g == 0:
                                st_["dc"] = w_get(tag + "dfc")
                                st_["ds"] = w_get(tag + "dfs")
                            dc, ds = st_["dc"], st_["ds"]
                            psE = ps_pool.alloc()
                            psO = ps_pool.alloc()
                            for ps_, base_ in ((psE, 0), (psO, 4)):
                                pairs = []
                                for lt in range(base_, base_ + 4):
                                    pairs.append((ut.t[:, lt, 0:128], dc.t[:, lt, :]))
                                    pairs.append((ut.t[:, lt, 128:256], ds.t[:, lt, :]))
                                mm(ps_, pairs, R=[ut.b, dc.b, ds.b])
                            et = tmp_pool.alloc()
                            ot = tmp_pool.alloc()
                            act(et.t[:, :], psE.t[:, :], AF.Copy, R=[psE.b], W=[et.b])
                            ps_pool.free(psE)
                            act(ot.t[:, :], psO.t[:, :], AF.Copy, R=[psO.b], W=[ot.b])
                            ps_pool.free(psO)
                            tt(act2[:, g, tsl(0)], et.t[:, :], ot.t[:, :], ALU.add, R=[et.b, ot.b], W=[b_a2[g][0]], eng="pool")
                            tt(act2[:, g, tsl(1)], et.t[:, :], ot.t[:, :], ALU.subtract, R=[et.b, ot.b], W=[b_a2[g][1]], eng="pool")
                            tmp_pool.free(et)
                            tmp_pool.free(ot)
                            if g == 3:
                                w_free(st_.pop("dc"))
                                w_free(st_.pop("ds"))
                        ut_pool.free(ut)

                    nk = 4
                    if P:
                        for i in range(1, nk + 2):
                            enter()
                            if 0 <= i - 2 < nk:
                                stage2(i - 2)
                            if i - 1 < nk:
                                stage1((i - 1) % 4)
                            leave()
                            yield
                    else:
                        for g in range(4):
                            enter()
                            stage1(g)
                            leave()
                            yield
                            enter()
                            stage2(g)
                            leave()
                            yield
                    for g in range(4):
                        tmp_pool.free(zfs[g])

                S.phase = kind + str(l) + ":lru"
                fg = fft_gen()
                alias_import()
                tmp_pool.free_list.extend(lru_extra)
                bpads = [pad_pool.alloc() for _ in range(3)]
                bslots = [Slot(p_.b, p_.t[:, 0:1024].bitcast(F32), "bp") for p_ in bpads]
                tmp_pool.free_list.extend(bslots)
                bd = w_get(tag + "bd")
                bdv = bd.t[:, :, :].rearrange("p a b -> p (a b)").rearrange("p (k n) -> p k n", k=32)
                def lru_front(c, slot, jj):
                    pad = pad_pool.alloc()
                    S.op("pool", lambda h, pad=pad: h.memset(pad.t[:, :], 0.0), writes=[pad.b])
                    dg = dg_pool.alloc()
                    build_diag(dg, pcol(l, "lru_conv_w")[:, c::8], 4)
                    for t in range(NT):
                        ps = ps_pool.alloc()
                        zmm(ps, slot, jj, t)
                        act(padv(pad.t, 2, 1, t, 2), v(ps.t[:, :]), AF.Copy, R=[ps.b], W=[pad.b])
                        ps_pool.free(ps)
                    xc32 = []
                    xcbs = tmp_pool.alloc()
                    xcbv = xcbs.t[:, :].bitcast(BF16)
                    for t in range(NT):
                        ps = ps_pool.alloc()
                        mm((ps, v(ps.t[:, :])), [(dg.t[:, k, :], padv(pad.t, 2, 1, t, k)) for k in range(4)],
                           R=[dg.b, pad.b])
                        x32 = tmp_pool.alloc()
                        ts(x32.t[:, :], ps.t[:, :], pcol(l, "lru_conv_b", c, 1), None, ALU.add,
                           R=[ps.b, b_pp], W=[x32.b])
                        S.op("pool", lambda h, x32=x32, t=t: h.tensor_copy(out=xcbv[:, tsl(t)], in_=x32.t[:, :]),
                             reads=[x32.b], writes=[xcbs.b])
                        ps_pool.free(ps)
                        xc32.append(x32)
                    pad_pool.free(pad)
                    dg_pool.free(dg)
                    return (c, xc32, xcbs, xcbv)

                def lru_back(state_):
                    c, xc32, xcbs, xcbv = state_
                    A_, S_, I_ = {}, {}, {}
                    for d in (0, 1):
                        for t in range(NT):
                            xbv = xcbv[:, tsl(t)]
                            psa = ps_pool.alloc()
                            mm(psa, [(bdv[:, d * 8 + c, :], xbv)], R=[bd.b, xcbs.b])
                            psx = ps_pool.alloc()
                            mm(psx, [(bdv[:, 16 + d * 8 + c, :], xbv)], R=[bd.b, xcbs.b])
                            a_ = tmp_pool.alloc()
                            act(a_.t[:, :], psa.t[:, :], AF.Tanh, bias=dcol(l, "hba", d * 8 + c, 1), scale=0.5,
                                R=[psa.b, b_dpl], W=[a_.b])
                            ps_pool.free(psa)
                            act(a_.t[:, :], a_.t[:, :], AF.Exp, bias=dcol(l, "hkk", d * 8 + c, 1),
                                scale=dcol(l, "hkk", d * 8 + c, 1), R=[a_.b, b_dpl], W=[a_.b])
                            s_ = tmp_pool.alloc()
                            tt(s_.t[:, :], a_.t[:, :], a_.t[:, :], ALU.mult, R=[a_.b], W=[s_.b], eng="pool")
                            i_ = tmp_pool.alloc()
                            act(i_.t[:, :], psx.t[:, :], AF.Tanh, bias=dcol(l, "hbx", d * 8 + c, 1), scale=0.5,
                                R=[psx.b, b_dpl], W=[i_.b])
                            ps_pool.free(psx)
                            A_[(d, t)], S_[(d, t)], I_[(d, t)] = a_, s_, i_
                    for d in (0, 1):
                        for t in range(NT):
                            s_ = S_[(d, t)]
                            act(s_.t[:, :], s_.t[:, :], AF.Sqrt, bias=cq, scale=-0.25, R=[s_.b, b_const], W=[s_.b])
                    hf = [None] * NT
                    hb = []
                    for d in (0, 1):
                        order = list(range(NT)) if d == 0 else list(range(NT - 1, -1, -1))
                        prev_h = None
                        for t in order:
                            a_, s_, i_ = A_[(d, t)], S_[(d, t)], I_[(d, t)]
                            stt(i_.t[:, :], i_.t[:, :], 1.0, xc32[t].t[:, :], ALU.add, ALU.mult,
                                R=[i_.b, xc32[t].b], W=[i_.b])
                            tt(i_.t[:, :], i_.t[:, :], s_.t[:, :], ALU.mult, R=[i_.b, s_.b], W=[i_.b])
                            tmp_pool.free(s_)
                            h_ = tmp_pool.alloc()
                            for sq in range(nseq):
                                lo, hi = (sq * 256, (sq + 1) * 256) if P else (0, 512)
                                if P:
                                    init = 0.0
                                    rd = []
                                elif prev_h is None:
                                    init = pg[:, 24 + (l * 2 + d) * 8 + c:24 + (l * 2 + d) * 8 + c + 1]
                                    rd = [b_pg]
                                else:
                                    init = prev_h.t[:, 511:512] if d == 0 else prev_h.t[:, 0:1]
                                    rd = [prev_h.b]
                                if d == 0:
                                    o_, a0_, a1_ = h_.t[:, lo:hi], a_.t[:, lo:hi], i_.t[:, lo:hi]
                                else:
                                    o_, a0_, a1_ = (h_.t[:, lo:hi][:, ::-1], a_.t[:, lo:hi][:, ::-1],
                                                    i_.t[:, lo:hi][:, ::-1])
                                S.op("dve", lambda h, o_=o_, a0_=a0_, a1_=a1_, init=init: h.tensor_tensor_scan(
                                    out=o_, data0=a0_, data1=a1_, initial=init, op0=ALU.mult, op1=ALU.add),
                                    reads=[a_.b, i_.b] + rd, writes=[h_.b])
                                if P:
                                    col = ((l * 2 + d) * 8 + c) * 2 + sq
                                    src = h_.t[:, hi - 1:hi] if d == 0 else h_.t[:, lo:lo + 1]
                                    S.op("dve", lambda h, col=col, src=src: h.tensor_copy(out=stT[:, col:col + 1], in_=src),
                                         reads=[h_.b], writes=[b_st])
                            tmp_pool.free(a_)
                            tmp_pool.free(i_)
                            if d == 0:
                                hf[t] = h_
                            else:
                                hb.append(h_)
                                tt(hf[t].t[:, :], hf[t].t[:, :], h_.t[:, :], ALU.add, R=[h_.b, hf[t].b], W=[hf[t].b])
                                tt(act1[:, c, tsl(t)], act1[:, c, tsl(t)], hf[t].t[:, :], ALU.mult,
                                   R=[hf[t].b, b_a1[c][t]], W=[b_a1[c][t]])
                            prev_h = h_
                    for t in range(NT):
                        tmp_pool.free(xc32[t])
                        tmp_pool.free(hf[t])
                    tmp_pool.free(xcbs)
                    for h__ in hb:
                        tmp_pool.free(h__)

                prevst = None
                for g in (0, 1):
                    slot = w_get(tag + "in%d" % g)
                    for jj in range(4):
                        st_ = lru_front(g * 4 + jj, slot, jj)
                        if prevst is not None:
                            lru_back(prevst)
                            next(fg, None)
                        prevst = st_
                    w_free(slot)
                lru_back(prevst)
                next(fg, None)
                w_free(bd)
                for _ in fg:
                    pass
                for x_ in lru_extra + bslots:
                    tmp_pool.free_list.remove(x_)
                for p_ in bpads:
                    pad_pool.free(p_)
                alias_export()

                S.phase = kind + str(l) + ":outA"
                holder = {}

                def get_A(hh):
                    holder[hh] = w_get(tag + "lruout%d" % hh)
                    return holder[hh]

                def free_A(hh):
                    w_free(holder.pop(hh))
                out_and_gate(l, tag, 0, get_A, free_A, (lambda k, m: k * 512 + (m % 4) * 128), 8,
                             lambda k, t: act1[:, k, tsl(t)], b_a1)

                next(mgen, None)
                S.phase = kind + str(l) + ":outB"
                go, fo, wc = wide_out(tag, "fourier_out")
                out_and_gate(l, tag, 1, go, fo, wc, 4, lambda k, t: act2[:, k, tsl(t)], b_a2)

                next(mgen, None)
                S.phase = kind + str(l) + ":conf"
                cpads = [pad_pool.alloc() for _ in range(4)]
                slot = w_get(tag + "in6")
                for c in range(4):
                    S.op("pool", lambda h, c=c: h.memset(cpads[c].t[:, :], 0.0), writes=[cpads[c].b])
                    for t in range(NT):
                        ps = ps_pool.alloc()
                        zmm(ps, slot, c, t)
                        act(padv(cpads[c].t, 15, 15, t, 15), v(ps.t[:, :]), AF.Tanh, scale=0.5, R=[ps.b], W=[cpads[c].b])
                        ps_pool.free(ps)
                w_free(slot)
                slot = w_get(tag + "in5")
                for c in range(4):
                    for t in range(NT):
                        ps = ps_pool.alloc()
                        zmm(ps, slot, c, t)
                        stt(padv(cpads[c].t, 15, 15, t, 15), padv(cpads[c].t, 15, 15, t, 15), 1.0, v(ps.t[:, :]),
                            ALU.add, ALU.mult, R=[ps.b, cpads[c].b], W=[cpads[c].b])
                        ps_pool.free(ps)
                w_free(slot)
                b_uc = [[r1f_bufs(4 + c, t) for t in range(NT)] for c in range(4)]
                for c in range(4):
                    pss = [ps_pool.alloc() for _ in range(NT)]
                    groups = list(range(0, 31, 9))
                    for gi, k0 in enumerate(groups):
                        n_ = min(9, 31 - k0)
                        dg = dg_pool.alloc()
                        build_diag(dg, dcol(l, "cwh")[:, c::4][:, k0:k0 + n_], n_)
                        for t in range(NT):
                            mm((pss[t], v(pss[t].t[:, :])),
                               [(dg.t[:, k, :], padv(cpads[c].t, 15, 15, t, k0 + k)) for k in range(n_)],
                               R=[cpads[c].b, dg.b], start=(gi == 0), stop=(gi == len(groups) - 1))
                        dg_pool.free(dg)
                    for t in range(NT):
                        act(act1f[:, c, tsl(t)], pss[t].t[:, :], AF.Identity, bias=pcol(l, "conf_dw_b", c, 1),
                            R=[pss[t].b, b_pp], W=b_uc[c][t])
                        ps_pool.free(pss[t])
                for c in range(4):
                    pad_pool.free(cpads[c])
                def ln_gen():
                    ph0 = S.phase
                    for t in range(NT):
                        psm = ps_pool.alloc()
                        psq = ps_pool.alloc()
                        ub = []
                        for c in range(4):
                            u_ = tmp_pool.alloc()
                            uv = u_.t[:, :].bitcast(BF16)
                            act(uv[:, 0:512], act1f[:, c, tsl(t)], AF.Copy, R=b_uc[c][t], W=[u_.b])
                            act(uv[:, 512:1024], act1f[:, c, tsl(t)], AF.Square, R=b_uc[c][t], W=[u_.b])
                            ub.append(u_)
                        mm(psm, [(onesb[:, :], ub[c].t[:, :].bitcast(BF16)[:, 0:512]) for c in range(4)],
                           R=[b_ident] + [u_.b for u_ in ub])
                        mm(psq, [(onesb[:, :], ub[c].t[:, :].bitcast(BF16)[:, 512:1024]) for c in range(4)],
                           R=[b_ident] + [u_.b for u_ in ub])
                        for u_ in ub:
                            tmp_pool.free(u_)
                        yield
                        mean = tmp_pool.alloc()
                        act(mean.t[:, :], psm.t[:, :], AF.Copy, scale=1.0 / 512, R=[psm.b], W=[mean.b])
                        ps_pool.free(psm)
                        var = tmp_pool.alloc()
                        tt(var.t[:, :], mean.t[:, :], mean.t[:, :], ALU.mult, R=[mean.b], W=[var.b])
                        stt(var.t[:, :], psq.t[:, :], 1.0 / 512, var.t[:, :], ALU.mult, ALU.subtract,
                            R=[psq.b, var.b], W=[var.b])
                        ps_pool.free(psq)
                        yield
                        act(var.t[:, :], var.t[:, :], AF.Ln, bias=ceps, scale=1.0, R=[var.b, b_const], W=[var.b])
                        act(var.t[:, :], var.t[:, :], AF.Exp, scale=-0.5, R=[var.b], W=[var.b])
                        yield
                        for c in range(4):
                            d_ = tmp_pool.alloc()
                            tt(d_.t[:, :], act1f[:, c, tsl(t)], mean.t[:, :], ALU.subtract, R=b_uc[c][t] + [mean.b], W=[d_.b])
                            tt(d_.t[:, :], d_.t[:, :], var.t[:, :], ALU.mult, R=[d_.b, var.b], W=[d_.b])
                            act(act2[:, c, tsl(t)], d_.t[:, :], AF.Silu, bias=pcol(l, "conf_ln_b", c, 1),
                                scale=pcol(l, "conf_ln_g", c, 1), R=[d_.b, b_pp], W=[b_a2[c][t]])
                            tmp_pool.free(d_)
                            if c % 2 == 1:
                                yield
                        tmp_pool.free(mean)
                        tmp_pool.free(var)

                S.phase = kind + str(l) + ":scfront"
                lg = ln_gen()
                next(lg, None)
                spads = [pad_pool.alloc() for _ in range(4)]
                slot = w_get(tag + "in9")
                for c in range(4):
                    S.op("pool", lambda h, c=c: h.memset(spads[c].t[:, :], 0.0), writes=[spads[c].b])
                    for t in range(NT):
                        ps = ps_pool.alloc()
                        zmm(ps, slot, c, t)
                        act(padv(spads[c].t, 1, 1, t, 1), v(ps.t[:, :]), AF.Copy, R=[ps.b], W=[spads[c].b])
                        ps_pool.free(ps)
                        next(lg, None)
                w_free(slot)
                slot = w_get(tag + "in7")
                for c in range(4):
                    for t in range(NT):
                        ps = ps_pool.alloc()
                        zmm(ps, slot, c, t)
                        tt(padv(spads[c].t, 1, 1, t, 1), padv(spads[c].t, 1, 1, t, 1), v(ps.t[:, :]), ALU.mult,
                           R=[ps.b, spads[c].b], W=[spads[c].b])
                        ps_pool.free(ps)
                        next(lg, None)
                w_free(slot)
                for _ in lg:
                    pass
                next(mgen, None)
                S.phase = kind + str(l) + ":outC"
                go, fo, wc = wide_out(tag, "conf_out")
                out_and_gate(l, tag, 2, go, fo, wc, 4, lambda k, t: act2[:, k, tsl(t)], b_a2)

                next(mgen, None)
                S.phase = kind + str(l) + ":sc"
                for c in range(4):
                    dg = dg_pool.alloc()
                    build_diag(dg, pcol(l, "sc_conv_w")[:, c::4], 3)
                    for t in range(NT):
                        ps = ps_pool.alloc()
                        mm((ps, v(ps.t[:, :])), [(dg.t[:, k, :], padv(spads[c].t, 1, 1, t, k)) for k in range(3)],
                           R=[dg.b, spads[c].b])
                        act(act1f[:, c, tsl(t)], ps.t[:, :], AF.Identity, bias=pcol(l, "sc_conv_b", c, 1),
                            R=[ps.b, b_pp], W=b_uc[c][t])
                        ps_pool.free(ps)
                    dg_pool.free(dg)
                    pad_pool.free(spads[c])
                slot = w_get(tag + "in8")
                for c in range(4):
                    for t in range(NT):
                        ps = ps_pool.alloc()
                        zmm(ps, slot, c, t)
                        tt(act2[:, c, tsl(t)], act1f[:, c, tsl(t)], ps.t[:, :], ALU.mult,
                           R=[ps.b] + b_uc[c][t], W=[b_a2[c][t]])
                        ps_pool.free(ps)
                w_free(slot)
                next(mgen, None)
                S.phase = kind + str(l) + ":outD"
                go, fo, wc = wide_out(tag, "sc_out")
                out_and_gate(l, tag, 3, go, fo, wc, 4, lambda k, t: act2[:, k, tsl(t)], b_a2)

                next(mgen, None)
                S.phase = kind + str(l) + ":wo"
                st2 = stats_begin()
                for hh in range(2):
                    slot = w_get(tag + "wo%d" % hh)
                    for t in range(NT):
                        for mq in range(4):
                            m = hh * 4 + mq
                            ps = ps_pool.alloc()
                            mm(ps, [(slot.t[:, k, mq * 128:(mq + 1) * 128], merged[:, k, tsl(t)]) for k in range(8)],
                               R=[slot.b] + [b_mg[k][t] for k in range(8)])
                            stt(xT[:, m, tsl(t)], ps.t[:, :], der[:, 16 + m:17 + m], xT[:, m, tsl(t)], ALU.mult, ALU.add,
                                R=[ps.b, b_der, b_xT[m][t]], W=[b_xT[m][t]])
                            ps_pool.free(ps)
                            stats_add_delayed(st2, m, t)
                    w_free(slot)

                S.phase = kind + str(l) + ":norm2"
                rmsnorm_to(l, lambda c: der[:, 8 + c:9 + c], lambda c: modT[:, 24 + c:25 + c],
                           lambda c, t: hT[:, c, tsl(t)], b_hT, st=st2)
                S.phase = kind + str(l) + ":ffnup"
                ntap = len(ffn_taps)
                fstate = {}

                def ffn_front(j, slot, jq):
                    pa = pad_pool.alloc()
                    pv = pad_pool.alloc()
                    S.op("pool", lambda h, pa=pa: h.memset(pa.t[:, :], 0.0), writes=[pa.b])
                    S.op("pool", lambda h, pv=pv: h.memset(pv.t[:, :], 0.0), writes=[pv.b])
                    dga = dg_pool.alloc()
                    dgv = dg_pool.alloc()
                    wa_ = pcol(l, "ffn_conv_w")[:, j::44]
                    wv_ = pcol(l, "ffn_conv_w")[:, 22 + j::44]
                    if P:
                        wa_, wv_ = wa_[:, 3:6], wv_[:, 3:6]
                    if taps_for(j)[0]:
                        build_diag(dga, wa_, ntap)
                        build_diag(dgv, wv_, ntap)
                    for t in range(NT):
                        ps = ps_pool.alloc()
                        zmm(ps, slot, jq, t)
                        act(padv2(pa.t, t, 1, 1), v2(ps.t[:, :]), AF.Copy, R=[ps.b], W=[pa.b])
                        ps_pool.free(ps)
                        ps = ps_pool.alloc()
                        zmm(ps, slot, 2 + jq, t)
                        S.op("dve", lambda h, ps=ps, pv=pv, t=t: h.tensor_copy(out=padv2(pv.t, t, 1, 1), in_=v2(ps.t[:, :])),
                             reads=[ps.b], writes=[pv.b])
                        ps_pool.free(ps)
                    fstate[j] = (pa, pv, dga, dgv)

                def taps_for(j):
                    all_pe = (j >= 20) or (P and j % 3 == 2)
                    if all_pe:
                        return list(enumerate(ffn_taps)), []
                    if P:
                        return [], [(3 + kc_, (1, kc_)) for kc_ in range(3)]
                    return ([(i, kk_) for i, kk_ in enumerate(ffn_taps) if kk_[0] < 2],
                            [(i, kk_) for i, kk_ in enumerate(ffn_taps) if kk_[0] == 2])

                def ffn_back(j):
                    pa, pv, dga, dgv = fstate.pop(j)
                    pe_taps, ve_taps = taps_for(j)
                    wcol = pcol(l, "ffn_conv_w")
                    accs = {}
                    for t in range(NT):
                        for nm, pd, joff in (("a", pa, j), ("v", pv, 22 + j)):
                            if not ve_taps:
                                continue
                            acc = tmp_pool.alloc()
                            (i0, (r0, c0)), (i1, (r1, c1)), (i2, (r2, c2)) = ve_taps
                            act(v2(acc.t[:, :]), padv2(pd.t, t, r0, c0), AF.Copy, scale=wcol[:, i0 * 44 + joff:i0 * 44 + joff + 1],
                                R=[pd.b, b_pp], W=[acc.b])
                            stt(v2(acc.t[:, :]), padv2(pd.t, t, r1, c1), wcol[:, i1 * 44 + joff:i1 * 44 + joff + 1], v2(acc.t[:, :]),
                                ALU.mult, ALU.add, R=[pd.b, b_pp, acc.b], W=[acc.b])
                            stt(v2(acc.t[:, :]), padv2(pd.t, t, r2, c2), wcol[:, i2 * 44 + joff:i2 * 44 + joff + 1], v2(acc.t[:, :]),
                                ALU.mult, ALU.add, R=[pd.b, b_pp, acc.b], W=[acc.b])
                            accs[(nm, t)] = acc
                    for t in range(NT):
                        if not pe_taps:
                            aa, av = accs.pop(("a", t)), accs.pop(("v", t))
                            ga = tmp_pool.alloc()
                            act(ga.t[:, :], aa.t[:, :], AF.Gelu, bias=pcol(l, "ffn_conv_b", j, 1), R=[aa.b, b_pp], W=[ga.b])
                            tmp_pool.free(aa)
                            stt(ffact[:, j, tsl(t)], av.t[:, :], pcol(l, "ffn_conv_b", 22 + j, 1), ga.t[:, :], ALU.add, ALU.mult,
                                R=[av.b, ga.b, b_pp], W=[b_ff[j][t]])
                            tmp_pool.free(av)
                            tmp_pool.free(ga)
                            continue
                        psa = ps_pool.alloc()
                        mm((psa, v2(psa.t[:, :])), [(dga.t[:, i, :], padv2(pa.t, t, kr, kc)) for i, (kr, kc) in pe_taps],
                           R=[dga.b, pa.b])
                        psv = ps_pool.alloc()
                        mm((psv, v2(psv.t[:, :])), [(dgv.t[:, i, :], padv2(pv.t, t, kr, kc)) for i, (kr, kc) in pe_taps],
                           R=[dgv.b, pv.b])
                        ga = tmp_pool.alloc()
                        if ve_taps:
                            aa, av = accs.pop(("a", t)), accs.pop(("v", t))
                            tt(aa.t[:, :], aa.t[:, :], psa.t[:, :], ALU.add, R=[aa.b, psa.b], W=[aa.b])
                            ps_pool.free(psa)
                            act(ga.t[:, :], aa.t[:, :], AF.Gelu, bias=pcol(l, "ffn_conv_b", j, 1), R=[aa.b, b_pp], W=[ga.b])
                            tmp_pool.free(aa)
                            tt(av.t[:, :], av.t[:, :], psv.t[:, :], ALU.add, R=[av.b, psv.b], W=[av.b])
                            ps_pool.free(psv)
                            stt(ffact[:, j, tsl(t)], av.t[:, :], pcol(l, "ffn_conv_b", 22 + j, 1), ga.t[:, :], ALU.add, ALU.mult,
                                R=[av.b, ga.b, b_pp], W=[b_ff[j][t]])
                            tmp_pool.free(av)
                        else:
                            act(ga.t[:, :], psa.t[:, :], AF.Gelu, bias=pcol(l, "ffn_conv_b", j, 1), R=[psa.b, b_pp], W=[ga.b])
                            ps_pool.free(psa)
                            stt(ffact[:, j, tsl(t)], psv.t[:, :], pcol(l, "ffn_conv_b", 22 + j, 1), ga.t[:, :], ALU.add, ALU.mult,
                                R=[psv.b, ga.b, b_pp], W=[b_ff[j][t]])
                            ps_pool.free(psv)
                        tmp_pool.free(ga)
                    pad_pool.free(pa)
                    pad_pool.free(pv)
                    dg_pool.free(dga)
                    dg_pool.free(dgv)

                prevj = None
                for j in range(22):
                    q, jq = divmod(j, 2)
                    if jq == 0:
                        slot = w_get(tag + "up%d" % q)
                    ffn_front(j, slot, jq)
                    if jq == 1:
                        w_free(slot)
                        if q % 2 == 1:
                            next(mgen, None)
                    if prevj is not None:
                        ffn_back(prevj)
                    prevj = j
                ffn_back(prevj)
                for _ in mgen:
                    pass
                S.phase = kind + str(l) + ":ffndown"
                pend["st"] = stats_begin()
                for hh in range(2):
                    dsl = [w_get(tag + "down%d_%d" % (hh, kg)) for kg in range(3)]
                    for t in range(NT):
                        for mq in range(4):
                            m = hh * 4 + mq
                            ps = ps_pool.alloc()
                            mm(ps, [(dsl[k // 8].t[:, k % 8, mq * 128:(mq + 1) * 128], ffact[:, k, tsl(t)]) for k in range(22)],
                               R=[s_.b for s_ in dsl] + [b_ff[k][t] for k in range(22)])
                            stt(xT[:, m, tsl(t)], ps.t[:, :], modT[:, 40 + m:41 + m], xT[:, m, tsl(t)], ALU.mult, ALU.add,
                                R=[ps.b, b_modl[l], b_xT[m][t]], W=[b_xT[m][t]])
                            ps_pool.free(ps)
                            stats_add_delayed(pend["st"], m, t)
                    for s_ in dsl:
                        w_free(s_)

            pend = {"st": None}
            for l in range(DEPTH):
                layer(l)
            S.phase = kind + ":final"
            yview = R1[:, 0:16384].bitcast(F32).rearrange("p (c t) -> p c t", c=8)
            ydst = yP if P else yS

            def out_tile(t):
                S.op("sp", lambda h, t=t: [h.dma_start(out=ydst[:, :, tsl(t)], in_=yview[:, :, tsl(t)])],
                     reads=[b for c in range(8) for b in r1f_bufs(c, t)], dma_buf=b_out)
            rmsnorm_to(0, lambda c: pg[:, c:c + 1], None, lambda c, t: yview[:, c, tsl(t)], None,
                       dst_buf_fn=r1f_bufs, st=pend["st"], per_tile_cb=out_tile)
            if P:
                S.op("sp", lambda h: [h.dma_start(out=stO, in_=stT[:, :])], reads=[b_st], dma_buf=b_st2)

        run_pass("P")
        run_pass("S")
        assert wstate["next_get"] == len(wq), (wstate["next_get"], len(wq))
        build_nc.pe_log = S.pe_log
        build_nc.order = recorded
        S.emit()
    return nc


def _fm(v):
    v = np.asarray(v, dtype=np.float32).reshape(-1)
    return np.ascontiguousarray(v.reshape(-1, 128).T)


_NC_CACHE = {}


def kernel(x_prompt, x_sample, state_lru, c, c_ctx, norm1_g, norm2_g, w_mod, b_mod, w_in, b_gate,
           lru_conv_w, lru_conv_b, lru_wa, lru_ba, lru_wx, lru_bx, lru_lam, lru_out, fourier_out,
           conf_dw_w, conf_dw_b, conf_ln_g, conf_ln_b, conf_out, sc_conv_w, sc_conv_b, sc_out, w_o,
           ffn_up, ffn_conv_w, ffn_conv_b, ffn_down, final_g):
    f32 = lambda a: np.ascontiguousarray(np.asarray(a, dtype=np.float32))
    loc = dict(norm1_g=norm1_g, norm2_g=norm2_g, b_mod=b_mod, b_gate=b_gate, lru_conv_w=lru_conv_w,
               lru_conv_b=lru_conv_b, lru_ba=lru_ba, lru_bx=lru_bx, lru_lam=lru_lam, conf_dw_w=conf_dw_w,
               conf_dw_b=conf_dw_b, conf_ln_g=conf_ln_g, conf_ln_b=conf_ln_b, sc_conv_w=sc_conv_w,
               sc_conv_b=sc_conv_b, ffn_conv_w=ffn_conv_w, ffn_conv_b=ffn_conv_b)
    pp = np.zeros((DEPTH, 128, NP), np.float32)
    for l in range(DEPTH):
        for name, (o, n) in PL.items():
            pp[l, :, o:o + n] = _fm(np.asarray(loc[name])[l])
    m = np.arange(128)
    ang = 2.0 * np.pi * np.outer(m, m) / 128.0
    cs128 = np.concatenate([np.cos(ang), np.sin(ang)], axis=1).astype(np.float32)

    def dft(Lh):
        i = np.arange(Lh)
        a = 2.0 * np.pi * (np.outer(i, i) % Lh) / Lh
        s = 1.0 / np.sqrt(Lh * 128.0)
        return (np.cos(a) * s).astype(np.float32), (-np.sin(a) * s).astype(np.float32)
    c256, s256 = dft(256)
    dft256 = np.ascontiguousarray(np.concatenate([c256, s256], axis=1))
    li = np.arange(1024)
    ang_ = 2.0 * np.pi * (np.outer(li, np.arange(512)) % 1024) / 1024.0
    sc_ = 1.0 / np.sqrt(1024 * 128.0)
    cfull = (np.cos(ang_) * sc_).astype(np.float32)
    sfull = (-np.sin(ang_) * sc_).astype(np.float32)
    dftc = np.concatenate([cfull[0::2], cfull[1::2]], axis=0)
    dfts = np.concatenate([sfull[0::2], sfull[1::2]], axis=0)
    ident = np.eye(128, dtype=np.float32)

    shared = dict(pp=pp, w_mod=f32(w_mod), w_in=f32(w_in), lru_wa=f32(lru_wa), lru_wx=f32(lru_wx),
                  lru_out=f32(lru_out), fourier_out=f32(fourier_out), conf_out=f32(conf_out), sc_out=f32(sc_out),
                  w_o=f32(w_o), ffn_up=f32(ffn_up), ffn_down=f32(ffn_down), cs128=cs128, dft256=dft256,
                  dftc=np.ascontiguousarray(dftc), dfts=np.ascontiguousarray(dfts), ident=ident)
    x_prompt = np.asarray(x_prompt, np.float32)
    x_sample = np.asarray(x_sample, np.float32)
    state_lru = np.asarray(state_lru, np.float32)
    c = np.asarray(c, np.float32)
    in_maps = []
    for i in range(NCORES):
        pg = np.zeros((128, NG), np.float32)
        pg[:, 0:8] = _fm(final_g)
        pg[:, 8:16] = _fm(c_ctx)
        pg[:, 16:24] = _fm(c[i])
        pg[:, 24:56] = _fm(state_lru[i])
        d = dict(shared)
        d["xP"] = np.ascontiguousarray(x_prompt[2 * i:2 * i + 2].reshape(512, D).T)
        d["xS"] = np.ascontiguousarray(x_sample[i].T)
        d["pg"] = pg
        in_maps.append(d)
    if "nc" not in _NC_CACHE:
        build_nc()
        _NC_CACHE["nc"] = build_nc(order=list(build_nc.order))
    nc = _NC_CACHE["nc"]
    res = run_bass_kernel_spmd(nc, in_maps, core_ids=list(range(NCORES)))
    y_prompt = np.zeros((16, 256, D), np.float32)
    y_sample = np.zeros((8, 1024, D), np.float32)
    new_state = np.zeros((16, DEPTH, 2, D), np.float32)
    for i in range(NCORES):
        r = res.results[i]
        yp = np.asarray(r["yP"]).transpose(2, 1, 0).reshape(512, D)
        y_prompt[2 * i] = yp[0:256]
        y_prompt[2 * i + 1] = yp[256:512]
        y_sample[i] = np.asarray(r["yS"]).transpose(2, 1, 0).reshape(1024, D)
        st = np.asarray(r["stO"]).reshape(128, DEPTH, 2, 8, 2)
        for s in range(2):
            new_state[2 * i + s] = st[:, :, :, :, s].transpose(1, 2, 3, 0).reshape(DEPTH, 2, D)
    return (y_prompt, y_sample, new_state)
```

```python
import numpy as np
from contextlib import ExitStack
import concourse.bass as bass
import concourse.mybir as mybir
from concourse.bass_utils import run_bass_kernel_spmd

F32 = mybir.dt.float32
BF16 = mybir.dt.bfloat16
AF = mybir.ActivationFunctionType
ALU = mybir.AluOpType

D = 1024
DEPTH = 2
D_IN = 9216
D_FF = 2816
NCORES = 8
EPS = 1e-6

_PLIST = [("norm1_g", 8), ("norm2_g", 8), ("b_mod", 48), ("b_gate", 32), ("lru_conv_w", 32), ("lru_conv_b", 8),
          ("lru_ba", 16), ("lru_bx", 16), ("lru_lam", 16), ("conf_dw_w", 124), ("conf_dw_b", 4), ("conf_ln_g", 4),
          ("conf_ln_b", 4), ("sc_conv_w", 12), ("sc_conv_b", 4), ("ffn_conv_w", 396), ("ffn_conv_b", 44)]
PL = {}
_o = 0
for _n, _c in _PLIST:
    PL[_n] = (_o, _c)
    _o += _c
NP = _o
_DLIST = [("hba", 16), ("hbx", 16), ("hkk", 16), ("hbg", 32), ("cwh", 124)]
DL = {}
_o = 0
for _n, _c in _DLIST:
    DL[_n] = (_o, _c)
    _o += _c
ND = _o
NG = 56


class Buf:
    __slots__ = ("name", "w", "r", "sem", "semv")

    def __init__(self, name):
        self.name = name
        self.w = None
        self.r = []
        self.sem = None
        self.semv = 0


class Sched:
    ENG = ("pe", "act", "dve", "pool", "sp")

    def __init__(self, nc, ctx):
        self.nc = nc
        self.ctx = ctx
        self.q = {e: [] for e in self.ENG}
        self.cnt = {e: 0 for e in self.ENG}
        self.esem = {e: ctx.enter_context(nc.semaphore("s_" + e)) for e in self.ENG}
        self.waited = {}
        self.dma_bufs = []
        self.pending = {e: [] for e in self.ENG}
        self.phase = "init"
        self.pe_log = []

    def _need(self, eng, tok, waits):
        sem, val, teng = tok
        if teng == eng and eng == "pe":
            return
        key = (eng, sem.name)
        if self.waited.get(key, 0) >= val:
            return
        self.waited[key] = val
        waits.append((sem, val))

    def op(self, eng, fn, reads=(), writes=(), dma_buf=None, n_dma=1):
        waits = self.pending[eng]
        self.pending[eng] = []
        for b in reads:
            if b.w is not None:
                self._need(eng, b.w, waits)
        for b in writes:
            if b.w is not None:
                self._need(eng, b.w, waits)
            for t in b.r:
                self._need(eng, t, waits)
        if dma_buf is not None:
            if dma_buf.sem is None:
                dma_buf.sem = self.ctx.enter_context(self.nc.semaphore("d_" + dma_buf.name))
                self.dma_bufs.append(dma_buf)
            dma_buf.semv += 16 * n_dma
            tok = (dma_buf.sem, dma_buf.semv, "dma")
            self.q[eng].append((waits, fn, dma_buf.sem))
        else:
            self.cnt[eng] += 1
            tok = (self.esem[eng], self.cnt[eng], eng)
            self.q[eng].append((waits, fn, None))
        for b in writes:
            b.w = tok
            b.r = []
        for b in reads:
            b.r.append(tok)
        return tok

    def fence(self):
        for e in self.ENG:
            for e2 in self.ENG:
                if self.cnt[e2] > 0:
                    self._need(e, (self.esem[e2], self.cnt[e2], e2 if e2 != "pe" else "x"), self.pending[e])
            for b in self.dma_bufs:
                self._need(e, (b.sem, b.semv, "dma"), self.pending[e])

    def emit(self):
        nc = self.nc
        handles = {"pe": "tensor", "act": "scalar", "dve": "vector", "pool": "gpsimd", "sp": "sync"}
        self.fence()
        with nc.Block() as block:
            for e in self.ENG:
                def body(h, e=e):
                    sem_e = self.esem[e]
                    for waits, fn, dsem in self.q[e]:
                        for (s, v) in waits:
                            h.wait_ge(s, v)
                        if dsem is not None:
                            for ins in fn(h):
                                ins.then_inc(dsem, 16)
                        else:
                            fn(h).then_inc(sem_e, 1)
                    for (s, v) in self.pending[e]:
                        h.wait_ge(s, v)
                getattr(block, handles[e])(body)


class Slot:
    __slots__ = ("b", "t", "name")

    def __init__(self, b, t, name):
        self.b = b
        self.t = t
        self.name = name


class FifoPool:
    def __init__(self, items):
        self.free_list = list(items)

    def alloc(self):
        assert self.free_list, "pool exhausted"
        return self.free_list.pop(0)

    def free(self, it):
        self.free_list.append(it)


def build_nc(order=None):
    nc = bass.Bass("TRN2", target_bir_lowering=False)
    dram = {}

    def din(name, shape):
        dram[name] = nc.dram_tensor(name, list(shape), F32, kind="ExternalInput").ap()
        return dram[name]

    xP = din("xP", [D, 512])
    xS = din("xS", [D, 1024])
    pp_d = din("pp", [DEPTH, 128, NP])
    pg_d = din("pg", [128, NG])
    w_mod = din("w_mod", [DEPTH, D, 6 * D])
    w_in = din("w_in", [DEPTH, D, D_IN])
    lru_wa = din("lru_wa", [DEPTH, 2, 8, 128, 128])
    lru_wx = din("lru_wx", [DEPTH, 2, 8, 128, 128])
    lru_out = din("lru_out", [DEPTH, D, D])
    fourier_out = din("fourier_out", [DEPTH, 512, D])
    conf_out = din("conf_out", [DEPTH, 512, D])
    sc_out = din("sc_out", [DEPTH, 512, D])
    w_o = din("w_o", [DEPTH, D, D])
    ffn_up = din("ffn_up", [DEPTH, D, 2 * D_FF])
    ffn_down = din("ffn_down", [DEPTH, D_FF, D])
    cs128_d = din("cs128", [128, 256])
    dft256_d = din("dft256", [256, 512])
    dftc_d = din("dftc", [1024, 512])
    dfts_d = din("dfts", [1024, 512])
    ident_d = din("ident", [128, 128])
    yP = nc.dram_tensor("yP", [128, 8, 512], F32, kind="ExternalOutput").ap()
    yS = nc.dram_tensor("yS", [128, 8, 1024], F32, kind="ExternalOutput").ap()
    stO = nc.dram_tensor("stO", [128, 64], F32, kind="ExternalOutput").ap()

    with ExitStack() as ctx:
        S = Sched(nc, ctx)

        def sb(name, shape, dt):
            return ctx.enter_context(nc.sbuf_tensor(name, list(shape), dt))

        xT = sb("xT", [128, 8, 1024], F32)
        hT = sb("hT", [128, 8, 1024], BF16)
        R1 = sb("R1", [128, 22 * 1024], BF16)
        merged = R1[:, 0:8192].rearrange("p (c t) -> p c t", c=8)
        act1 = R1[:, 8192:16384].rearrange("p (c t) -> p c t", c=8)
        act1f = R1[:, 8192:16384].bitcast(F32).rearrange("p (c t) -> p c t", c=4)
        act2 = R1[:, 16384:20480].rearrange("p (c t) -> p c t", c=4)
        ffact = R1[:, :].rearrange("p (c t) -> p c t", c=22)
        NWS = 5
        wsl = [sb("ws%d" % i, [128, 8, 512], BF16) for i in range(NWS)]
        NTMP = 18
        tmps = [sb("tmp%d" % i, [128, 512], F32) for i in range(NTMP)]
        NPAD = 5
        pads = [sb("pad%d" % i, [128, 1200], BF16) for i in range(NPAD)]
        NDG = 4
        dgs = [sb("dg%d" % i, [128, 9, 128], BF16) for i in range(NDG)]
        UTs = [sb("ut%d" % i, [128, 8, 256], BF16) for i in range(1)]
        pp = sb("ppt", [128, DEPTH, NP], F32)
        dpl = sb("dpl", [128, DEPTH, ND], F32)
        pg = sb("pgt", [128, NG], F32)
        der = sb("der", [128, 32], F32)
        scond = sb("scond", [128, 8, 2], BF16)
        modTall = sb("modTall", [128, DEPTH, 2, 48], F32)
        consts = sb("consts", [128, 8], F32)
        identf = sb("identf", [128, 128], F32)
        ident = sb("identb", [128, 128], BF16)
        onesb = sb("onesb", [128, 128], BF16)
        cs128 = sb("cs128t", [128, 256], BF16)
        d256 = sb("d256t", [128, 2, 512], BF16)
        stT = sb("stT", [128, 64], F32)
        small = sb("small", [128, 64], F32)
        psb = [ctx.enter_context(nc.psum_tensor("ps%d" % i, [128, 512], F32)) for i in range(8)]

        ws_pool = FifoPool([Slot(Buf("ws%d" % i), wsl[i], "ws%d" % i) for i in range(NWS)])
        tmp_pool = FifoPool([Slot(Buf("tmp%d" % i), tmps[i], "tmp%d" % i) for i in range(NTMP)])
        pad_pool = FifoPool([Slot(Buf("pad%d" % i), pads[i], "pad%d" % i) for i in range(NPAD)])
        dg_pool = FifoPool([Slot(Buf("dg%d" % i), dgs[i], "dg%d" % i) for i in range(NDG)])
        ut_pool = FifoPool([Slot(Buf("ut%d" % i), UTs[i], "ut%d" % i) for i in range(1)])
        ps_pool = FifoPool([Slot(Buf("ps%d" % i), psb[i], "ps%d" % i) for i in range(8)])
        _xr = [R1[:, 1024 * j:1024 * (j + 1)].bitcast(F32) for j in list(range(8)) + list(range(20, 22))]
        lru_extra = [Slot(Buf("xtmp%d" % i), _xr[i], "xtmp%d" % i) for i in range(len(_xr))]
        b_xT = [[Buf("xT%d_%d" % (c, t)) for t in range(2)] for c in range(8)]
        b_hT = [[Buf("hT%d_%d" % (c, t)) for t in range(2)] for c in range(8)]
        b_mg = [[Buf("mg%d_%d" % (c, t)) for t in range(2)] for c in range(8)]
        b_a1 = [[Buf("a1%d_%d" % (c, t)) for t in range(2)] for c in range(8)]
        b_a2 = [[Buf("a2%d_%d" % (c, t)) for t in range(2)] for c in range(4)]
        b_ffx = [[Buf("ff%d_%d" % (c, t)) for t in range(2)] for c in range(2)]
        b_ff = [b_mg[j] if j < 8 else (b_a1[j - 8] if j < 16 else (b_a2[j - 16] if j < 20 else b_ffx[j - 20]))
                for j in range(22)]

        _xal = [b_mg[j] for j in range(8)] + [b_ffx[j] for j in range(2)]

        def alias_import():
            for sl, bufs in zip(lru_extra, _xal):
                toks = []
                for b in bufs:
                    if b.w is not None:
                        toks.append(b.w)
                    toks += b.r
                sl.b.w = None
                sl.b.r = toks

        def alias_export():
            for sl, bufs in zip(lru_extra, _xal):
                toks = list(sl.b.r) + ([sl.b.w] if sl.b.w is not None else [])
                for b in bufs:
                    b.r = b.r + toks


        def r1f_bufs(c, t):
            idx = 2 * c + t
            lst = b_mg[idx] if idx < 8 else b_a1[idx - 8]
            return [lst[0], lst[1]]
        b_pp, b_dpl, b_pg, b_der, b_scond = Buf("pp"), Buf("dpl"), Buf("pg"), Buf("der"), Buf("scond")
        b_modl = [Buf("mod0"), Buf("mod1")]
        b_const, b_ident, b_identf, b_cs, b_d256, b_st, b_small = (Buf("const"), Buf("ident"), Buf("identf"), Buf("cs"),
                                                                    Buf("d256"), Buf("st"), Buf("small"))
        b_out = Buf("out")
        b_st2 = Buf("st2")

        def act(out, in_, func, bias=None, scale=1.0, R=(), W=()):
            def f(h):
                if bias is None:
                    return h.activation(out=out, in_=in_, func=func, scale=scale)
                return h.activation(out=out, in_=in_, func=func, bias=bias, scale=scale)
            S.op("act", f, reads=R, writes=W)

        def tt(out, in0, in1, op, R=(), W=(), eng="dve"):
            S.op(eng, lambda h: h.tensor_tensor(out=out, in0=in0, in1=in1, op=op), reads=R, writes=W)

        def ts(out, in0, s1, s2, op0, op1=None, R=(), W=(), eng="dve"):
            def f(h):
                if op1 is None:
                    return h.tensor_scalar(out=out, in0=in0, scalar1=s1, scalar2=None, op0=op0)
                return h.tensor_scalar(out=out, in0=in0, scalar1=s1, scalar2=s2, op0=op0, op1=op1)
            S.op(eng, f, reads=R, writes=W)

        def stt(out, in0, scalar, in1, op0, op1, R=(), W=()):
            S.op("dve", lambda h: h.scalar_tensor_tensor(out=out, in0=in0, scalar=scalar, in1=in1, op0=op0, op1=op1),
                 reads=R, writes=W)

        def mm(ps, pairs, R, start=True, stop=True):
            def f(h):
                n = len(pairs)
                ins = None
                for i, (l, r) in enumerate(pairs):
                    ins = h.matmul(ps.t[:, :] if not isinstance(ps, tuple) else ps[1], lhsT=l, rhs=r,
                                   start=(start and i == 0), stop=(stop and i == n - 1))
                return ins
            b = ps.b if not isinstance(ps, tuple) else ps[0].b
            S.pe_log.append((S.phase, len(pairs)))
            S.op("pe", f, reads=R, writes=[b])

        def dma_load(eng, dst_buf, pairs):
            def f(h):
                return [h.dma_start(out=o, in_=i) for (o, i) in pairs]
            S.op(eng, f, writes=[dst_buf], dma_buf=dst_buf, n_dma=len(pairs))

        cz, c1, ceps, cq, chalf = (consts[:, 0:1], consts[:, 1:2], consts[:, 2:3], consts[:, 3:4], consts[:, 4:5])

        for i, v in enumerate([0.0, 1.0, EPS, 0.25, 0.5]):
            S.op("dve", lambda h, i=i, v=v: h.memset(consts[:, i:i + 1], v), writes=[b_const])
        S.op("dve", lambda h: h.memset(onesb[:, :], 1.0), writes=[b_ident])
        dma_load("sp", b_identf, [(identf[:, :], ident_d)])
        S.op("dve", lambda h: h.tensor_copy(out=ident[:, :], in_=identf[:, :]), reads=[b_identf], writes=[b_ident])
        dma_load("sp", b_pp, [(pp[:, l, :], pp_d[l]) for l in range(DEPTH)])
        dma_load("sp", b_pg, [(pg[:, :], pg_d)])
        dma_load("pool", b_cs, [(cs128[:, :], cs128_d)])
        dma_load("pool", b_d256, [(d256[:, :, :], dft256_d.rearrange("(t p) n -> p t n", p=128))])
        for i in range(NPAD):
            S.op("pool", lambda h, i=i: h.memset(pads[i][:, :], 0.0), writes=[pad_pool.free_list[i].b])

        def pcol(l, name, a=0, n=None):
            o, c = PL[name]
            if n is None:
                n = c - a
            return pp[:, l, o + a:o + a + n]

        def dcol(l, name, a=0, n=None):
            o, c = DL[name]
            if n is None:
                n = c - a
            return dpl[:, l, o + a:o + a + n]

        for l in range(DEPTH):
            ts(dcol(l, "hba"), pcol(l, "lru_ba"), 0.5, None, ALU.mult, R=[b_pp], W=[b_dpl])
            ts(dcol(l, "hbx"), pcol(l, "lru_bx"), 0.5, None, ALU.mult, R=[b_pp], W=[b_dpl])
            ts(dcol(l, "hbg"), pcol(l, "b_gate"), 0.5, None, ALU.mult, R=[b_pp], W=[b_dpl])
            ts(dcol(l, "cwh"), pcol(l, "conf_dw_w"), 0.5, None, ALU.mult, R=[b_pp], W=[b_dpl])
            e_ = small[:, 0:16]
            p_ = small[:, 16:32]
            act(e_, pcol(l, "lru_lam"), AF.Exp, scale=-1.0, R=[b_pp], W=[b_small])
            ts(p_, e_, -0.2, 0.25, ALU.mult, ALU.add, R=[b_small], W=[b_small])
            for cst in (1.0 / 3.0, 0.5, 1.0):
                tt(p_, p_, e_, ALU.mult, R=[b_small], W=[b_small])
                ts(p_, p_, -1.0, cst, ALU.mult, ALU.add, R=[b_small], W=[b_small])
            tt(p_, p_, e_, ALU.mult, R=[b_small], W=[b_small])
            ts(dcol(l, "hkk"), p_, -4.0, None, ALU.mult, R=[b_small], W=[b_dpl])

        for ci in range(2):
            act(scond[:, :, ci], pg[:, 8 + 8 * ci:16 + 8 * ci], AF.Silu, R=[b_pg], W=[b_scond])

        wq = []
        wstate = {"next_issue": 0, "next_get": 0, "loaded": {}}

        def w_issue():
            while wstate["next_issue"] < len(wq) and ws_pool.free_list and \
                    wstate["next_issue"] - wstate["next_get"] < NWS:
                i = wstate["next_issue"]
                slot = ws_pool.alloc()
                name, pf = wq[i]
                dma_load("pool", slot.b, pf(slot.t))
                wstate["loaded"][i] = slot
                wstate["next_issue"] += 1

        def w_get(name):
            recorded.append(name)
            if order is None:
                slot = ws_pool.alloc()
                dma_load("pool", slot.b, wq_defs[name](slot.t))
                wstate["next_get"] += 1
                return slot
            i = wstate["next_get"]
            assert wq[i][0] == name, (wq[i][0], name)
            if i not in wstate["loaded"]:
                w_issue()
            assert i in wstate["loaded"], "no free weight slot for " + name
            wstate["next_get"] += 1
            slot = wstate["loaded"].pop(i)
            w_issue()
            return slot

        def w_free(slot):
            ws_pool.free(slot)
            if order is not None:
                w_issue()

        def kview(w2d):
            return w2d.rearrange("(k p) n -> p k n", p=128)

        def q_std(name, w2d, c0, ncols, nk=8, k0=0):
            def pf(t, w2d=w2d):
                return [(t[:, 0:nk, 0:ncols], kview(w2d)[:, k0:k0 + nk, c0:c0 + ncols])]
            wq.append((name, pf))

        def q_wide(name, w2d):
            def pf(t, w2d=w2d):
                tv = t[:, :, :].rearrange("p a b -> p (a b)").rearrange("p (k n) -> p k n", k=4)
                return [(tv, kview(w2d))]
            wq.append((name, pf))

        def build_wq(l, kind):
            tag = "%s%d_" % (kind, l)
            if kind == "P" and l == 0:
                for g in range(12):
                    q_std("mod0_%d" % g, w_mod[0], g * 512, 512)
            for g in (2, 3, 0, 1):
                q_std(tag + "in%d" % g, w_in[l], g * 512, 512)

            def pf_bd(t, l=l):
                tv = t[:, :, :].rearrange("p a b -> p (a b)").rearrange("p (k n) -> p k n", k=32)
                return [(tv[:, 0:16, :], lru_wa[l].rearrange("d h i j -> i (d h) j")),
                        (tv[:, 16:32, :], lru_wx[l].rearrange("d h i j -> i (d h) j"))]
            wq.insert(len(wq) - 2, (tag + "bd", pf_bd))
            q_std(tag + "in4", w_in[l], 4 * 512, 512)
            if kind == "S":
                q_std(tag + "dfc", dftc_d, 0, 512)
                q_std(tag + "dfs", dfts_d, 0, 512)
            for hh in range(2):
                q_std(tag + "lruout%d" % hh, lru_out[l], hh * 512, 512)
                q_std(tag + "in%d" % (10 + hh), w_in[l], (10 + hh) * 512, 512)
            q_wide(tag + "fourier_out", fourier_out[l])
            for hh in range(2):
                q_std(tag + "in%d" % (12 + hh), w_in[l], (12 + hh) * 512, 512)
            for g in (6, 5):
                q_std(tag + "in%d" % g, w_in[l], g * 512, 512)
            for g in (9, 7):
                q_std(tag + "in%d" % g, w_in[l], g * 512, 512)
            q_wide(tag + "conf_out", conf_out[l])
            for hh in range(2):
                q_std(tag + "in%d" % (14 + hh), w_in[l], (14 + hh) * 512, 512)
            q_std(tag + "in8", w_in[l], 8 * 512, 512)
            q_wide(tag + "sc_out", sc_out[l])
            for hh in range(2):
                q_std(tag + "in%d" % (16 + hh), w_in[l], (16 + hh) * 512, 512)
            for hh in range(2):
                q_std(tag + "wo%d" % hh, w_o[l], hh * 512, 512)
            for q in range(11):
                def pf_up(t, l=l, q=q):
                    v = kview(ffn_up[l])
                    return [(t[:, :, 0:256], v[:, :, 256 * q:256 * q + 256]),
                            (t[:, :, 256:512], v[:, :, D_FF + 256 * q:D_FF + 256 * q + 256])]
                wq.append((tag + "up%d" % q, pf_up))
                if kind == "P" and l == 0:
                    q_std("mod1_%d" % q, w_mod[1], q * 512, 512)
            if kind == "P" and l == 0:
                q_std("mod1_11", w_mod[1], 11 * 512, 512)
            for hh in range(2):
                for kg in range(3):
                    nk = 8 if kg < 2 else 6
                    q_std(tag + "down%d_%d" % (hh, kg), ffn_down[l], hh * 512, 512, nk=nk, k0=8 * kg)

        for kind in ("P", "S"):
            for l in range(DEPTH):
                build_wq(l, kind)
        wq_defs = dict(wq)
        recorded = []
        if order is not None:
            wq[:] = [(n_, wq_defs[n_]) for n_ in order]
            assert len(wq) == len(wq_defs)

        def mod_steps(l):
            tag = "mod%d_" % l

            def finish(row, g):
                ps2 = ps_pool.alloc()

                def f(h, row=row, ps2=ps2):
                    ins = None
                    for j in range(4):
                        ins = h.matmul(ps2.t[:, 2 * j:2 * j + 2], lhsT=row.t[0:2, j * 128:(j + 1) * 128],
                                       rhs=identf[0:2, 0:2], start=True, stop=True)
                    return ins
                S.pe_log.append((S.phase, 4))
                S.op("pe", f, reads=[row.b, b_identf], writes=[ps2.b])
                tt(modTall[:, l, :, 4 * g:4 * g + 4], ps2.t[:, 0:8].rearrange("p (j n) -> p n j", n=2),
                   pcol(l, "b_mod", 4 * g, 4).unsqueeze(1).to_broadcast([128, 2, 4]), ALU.add,
                   R=[ps2.b, b_pp], W=[b_modl[l]])
                ps_pool.free(ps2)
                tmp_pool.free(row)

            prev = None
            for g in range(12):
                slot = w_get(tag + "%d" % g)
                ps = ps_pool.alloc()
                mm((ps, ps.t[0:2, :]), [(scond[:, k, :], slot.t[:, k, :]) for k in range(8)], R=[slot.b, b_scond])
                w_free(slot)
                row = tmp_pool.alloc()
                act(row.t[0:2, :], ps.t[0:2, :], AF.Copy, R=[ps.b], W=[row.b])
                ps_pool.free(ps)
                if prev is not None:
                    finish(*prev)
                prev = (row, g)
                if g == 11:
                    finish(*prev)
                yield g

        def run_pass(kind):
            P = (kind == "P")
            NT = 1 if P else 2
            NTOK = 512 * NT
            L = 256 if P else 1024
            nseq = 2 if P else 1
            xsrc = xP if P else xS

            def v(ap):
                return ap.rearrange("p (s l) -> p s l", s=2) if P else ap

            def padv(pad_t, pl, pr, t, k):
                W = pl + L + pr
                if P:
                    return pad_t[:, 0:2 * W].rearrange("p (s w) -> p s w", s=2)[:, :, k:k + 256]
                return pad_t[:, t * 512 + k:t * 512 + k + 512]

            def v2(ap):
                if P:
                    return ap.rearrange("p (s l) -> p s l", s=2)
                return ap.rearrange("p (r c) -> p r c", c=64)

            def padv2(pad_t, t, kr, kc):
                if P:
                    return pad_t[:, 0:2 * 258].rearrange("p (s w) -> p s w", s=2)[:, :, kc:kc + 256]
                return pad_t[:, 0:18 * 66].rearrange("p (r c) -> p r c", c=66)[:, 8 * t + kr:8 * t + kr + 8, kc:kc + 64]

            ffn_taps = [(1, kc) for kc in range(3)] if P else [(kr, kc) for kr in range(3) for kc in range(3)]

            def tsl(t):
                return slice(t * 512, (t + 1) * 512)

            for c in range(8):
                for t in range(NT):
                    if (not P) and t == 1:
                        continue
                    dma_load("sp", b_xT[c][t], [(xT[:, c, tsl(t)], xsrc[c * 128:(c + 1) * 128, tsl(t)])])
            if P:
                for c in range(8):
                    dma_load("sp", b_xT[c][1], [(xT[:, c, tsl(1)], xS[c * 128:(c + 1) * 128, tsl(1)])])

            def zmm(ps, wslot, jj, t):
                mm(ps, [(wslot.t[:, k, jj * 128:(jj + 1) * 128], hT[:, k, tsl(t)]) for k in range(8)],
                   R=[wslot.b] + [b_hT[k][t] for k in range(8)])

            def build_diag(dg, wap, ntaps):
                S.op("pool", lambda h: h.tensor_tensor(
                    out=dg.t[:, 0:ntaps, :],
                    in0=ident[:, :].unsqueeze(1).to_broadcast([128, ntaps, 128]),
                    in1=wap.unsqueeze(2).to_broadcast([128, ntaps, 128]), op=ALU.mult),
                    reads=[b_ident, b_pp, b_dpl], writes=[dg.b])

            def stats_begin():
                return {"ps": [ps_pool.alloc() for _ in range(NT)], "n": [0] * NT}

            def stats_add(st, c, t):
                sq = tmp_pool.alloc()
                sqb = sq.t[:, :].bitcast(BF16)[:, 0:512]
                act(sqb, xT[:, c, tsl(t)], AF.Square, R=[b_xT[c][t]], W=[sq.b])
                mm(st["ps"][t], [(onesb[:, :], sqb)], R=[b_ident, sq.b], start=(st["n"][t] == 0), stop=(st["n"][t] == 7))
                st["n"][t] += 1
                tmp_pool.free(sq)

            def stats_add_delayed(st, c, t, lag=3):
                q_ = st.setdefault("q", [])
                q_.append((c, t))
                while len(q_) > lag:
                    stats_add(st, *q_.pop(0))

            def stats_flush(st):
                for ct in st.pop("q", []):
                    stats_add(st, *ct)

            def rmsnorm_to(l, gcol, shcol, dst_fn, dst_bufs, dst_buf_fn=None, torder=None, st=None, per_tile_cb=None):
                order_ = list(torder) if torder is not None else list(range(NT))
                if st is None:
                    st = stats_begin()
                    for t in order_:
                        for c in range(8):
                            stats_add(st, c, t)
                else:
                    stats_flush(st)
                assert all(n_ == 8 for n_ in st["n"])
                rss = {}
                for t in order_:
                    ps = st["ps"][t]
                    rs = tmp_pool.alloc()
                    act(rs.t[:, :], ps.t[:, :], AF.Ln, bias=ceps, scale=1.0 / D, R=[ps.b, b_const], W=[rs.b])
                    ps_pool.free(ps)
                    act(rs.t[:, :], rs.t[:, :], AF.Exp, scale=-0.5, R=[rs.b], W=[rs.b])
                    rss[t] = rs
                for t in order_:
                    rs = rss[t]
                    for c in range(8):
                        tm = tmp_pool.alloc()
                        tt(tm.t[:, :], xT[:, c, tsl(t)], rs.t[:, :], ALU.mult, R=[b_xT[c][t], rs.b], W=[tm.b])
                        if shcol is None:
                            act(dst_fn(c, t), tm.t[:, :], AF.Identity, bias=cz, scale=gcol(c),
                                R=[tm.b, b_pg, b_der, b_const], W=dst_buf_fn(c, t))
                        else:
                            act(dst_fn(c, t), tm.t[:, :], AF.Identity, bias=shcol(c), scale=gcol(c),
                                R=[tm.b, b_der, b_modl[l]], W=[dst_bufs[c][t]])
                        tmp_pool.free(tm)
                    tmp_pool.free(rs)
                    if per_tile_cb is not None:
                        per_tile_cb(t)

            def out_and_gate(l, tag, b, get_out, free_out, wcol, kchunks, act_ap, act_bufs):
                for hh in range(2):
                    wout = get_out(hh)
                    gslot = w_get(tag + "in%d" % (10 + 2 * b + hh))
                    wflat = wout.t[:, :, :].rearrange("p a b -> p (a b)")
                    tgs = {}
                    for t in range(NT):
                        for mq in range(4):
                            m = hh * 4 + mq
                            psg = ps_pool.alloc()
                            zmm(psg, gslot, mq, t)
                            tg = tmp_pool.alloc()
                            act(tg.t[:, :], psg.t[:, :], AF.Tanh, bias=dcol(l, "hbg", b * 8 + m, 1), scale=0.5,
                                R=[psg.b, b_dpl], W=[tg.b])
                            ps_pool.free(psg)
                            tgs[(mq, t)] = tg
                    w_free(gslot)
                    for t in range(NT):
                        for mq in range(4):
                            m = hh * 4 + mq
                            tg = tgs.pop((mq, t))
                            psy = ps_pool.alloc()
                            mm(psy, [(wflat[:, wcol(k, m):wcol(k, m) + 128], act_ap(k, t)) for k in range(kchunks)],
                               R=[wout.b] + [act_bufs[k][t] for k in range(kchunks)])
                            if b == 0:
                                stt(merged[:, m, tsl(t)], tg.t[:, :], 1.0, psy.t[:, :], ALU.add, ALU.mult,
                                    R=[tg.b, psy.b], W=[b_mg[m][t]])
                            else:
                                stt(tg.t[:, :], tg.t[:, :], 1.0, psy.t[:, :], ALU.add, ALU.mult,
                                    R=[tg.b, psy.b], W=[tg.b])
                                tt(merged[:, m, tsl(t)], merged[:, m, tsl(t)], tg.t[:, :], ALU.add,
                                   R=[tg.b, b_mg[m][t]], W=[b_mg[m][t]])
                            ps_pool.free(psy)
                            tmp_pool.free(tg)
                    free_out(hh)

            def wide_out(tag, name):
                hold = {}

                def get_out(hh):
                    if "w" not in hold:
                        hold["w"] = w_get(tag + name)
                    return hold["w"]

                def free_out(hh):
                    if hh == 1:
                        w_free(hold.pop("w"))
                return get_out, free_out, (lambda k, m: k * 1024 + m * 128)

            def layer(l):
                tag = "%s%d_" % (kind, l)
                if P and l == 0:
                    S.phase = "P0:mod"
                    for _ in mod_steps(0):
                        pass
                modT = modTall[:, l, 0 if P else 1, :]
                mgen = mod_steps(1) if (P and l == 0) else iter(())
                stt(der[:, 0:8], modT[:, 8:16], 1.0, pcol(l, "norm1_g"), ALU.add, ALU.mult, R=[b_modl[l], b_pp], W=[b_der])
                stt(der[:, 8:16], modT[:, 32:40], 1.0, pcol(l, "norm2_g"), ALU.add, ALU.mult, R=[b_modl[l], b_pp], W=[b_der])
                ts(der[:, 16:24], modT[:, 16:24], 0.5, None, ALU.mult, R=[b_modl[l]], W=[b_der])

                S.phase = kind + str(l) + ":norm1"
                rmsnorm_to(l, lambda c: der[:, c:c + 1], lambda c: modT[:, c:c + 1],
                           lambda c, t: hT[:, c, tsl(t)], b_hT,
                           torder=((1, 0) if (not P and l == 0) else None), st=pend["st"])
                pend["st"] = None

                S.phase = kind + str(l) + ":geluzg"
                for g in (2, 3):
                    slot = w_get(tag + "in%d" % g)
                    for jj in range(4):
                        c = (g - 2) * 4 + jj
                        for t in range(NT):
                            ps = ps_pool.alloc()
                            zmm(ps, slot, jj, t)
                            act(act1[:, c, tsl(t)], ps.t[:, :], AF.Gelu, R=[ps.b], W=[b_a1[c][t]])
                            ps_pool.free(ps)
                    w_free(slot)
                def fft_gen():
                    ph_ = [None]

                    def enter():
                        ph_[0] = S.phase
                        S.phase = kind + str(l) + ":fft"

                    def leave():
                        S.phase = ph_[0]
                    enter()
                    slot = w_get(tag + "in4")
                    zfs = []
                    for g in range(4):
                        zf = tmp_pool.alloc()
                        zfb = zf.t[:, :].bitcast(BF16)
                        for t in range(NT):
                            ps = ps_pool.alloc()
                            zmm(ps, slot, g, t)
                            act(zfb[:, tsl(t)], ps.t[:, :], AF.Copy, R=[ps.b], W=[zf.b])
                            ps_pool.free(ps)
                        zfs.append(zf)
                    w_free(slot)
                    leave()
                    yield
                    nlt = NTOK // 128
                    st_ = {}

                    def stage1(g):
                        zf = zfs[g]
                        zfb = zf.t[:, :].bitcast(BF16)
                        ut = ut_pool.alloc()
                        for lp in range(nlt // 2):
                            ps = ps_pool.alloc()

                            def f(h, ps=ps, lp=lp, zfb=zfb):
                                ins = None
                                for q in range(2):
                                    lt = 2 * lp + q
                                    if P:
                                        lh = zfb[:, lt * 128:(lt + 1) * 128]
                                    else:
                                        par_, blk_ = divmod(lt, 4)
                                        lh = zfb[:, 256 * blk_ + par_:256 * blk_ + 256:2]
                                    ins = h.matmul(ps.t[:, q * 256:(q + 1) * 256], lhsT=lh,
                                                   rhs=cs128[:, :], start=True, stop=True)
                                return ins
                            S.pe_log.append((S.phase, 2))
                            S.op("pe", f, reads=[zf.b, b_cs], writes=[ps.b])
                            act(ut.t[:, 2 * lp:2 * lp + 2, :], ps.t[:, :].rearrange("p (a b) -> p a b", a=2), AF.Copy,
                                R=[ps.b], W=[ut.b])
                            ps_pool.free(ps)
                        st_["ut"] = ut

                    def stage2(k):
                        ut = st_.pop("ut")
                        if P:
                            g = k
                            ps = ps_pool.alloc()
                            for sq in range(2):
                                pairs = []
                                for lt in range(2):
                                    pairs.append((ut.t[:, 2 * sq + lt, 0:128], d256[:, lt, 0:256]))
                                    pairs.append((ut.t[:, 2 * sq + lt, 128:256], d256[:, lt, 256:512]))
                                mm((ps, ps.t[:, sq * 256:(sq + 1) * 256]), pairs, R=[ut.b, b_d256])
                            act(act2[:, g, 0:512], ps.t[:, :], AF.Copy, R=[ps.b], W=[b_a2[g][0]])
                            ps_pool.free(ps)
                        else:
                            g = k
                            if g == 0:
                                st_["dc"] = w_get(tag + "dfc")
                                st_["ds"] = w_get(tag + "dfs")
                            dc, ds = st_["dc"], st_["ds"]
                            psE = ps_pool.alloc()
                            psO = ps_pool.alloc()
                            for ps_, base_ in ((psE, 0), (psO, 4)):
                                pairs = []
                                for lt in range(base_, base_ + 4):
                                    pairs.append((ut.t[:, lt, 0:128], dc.t[:, lt, :]))
                                    pairs.append((ut.t[:, lt, 128:256], ds.t[:, lt, :]))
                                mm(ps_, pairs, R=[ut.b, dc.b, ds.b])
                            et = tmp_pool.alloc()
                            ot = tmp_pool.alloc()
                            act(et.t[:, :], psE.t[:, :], AF.Copy, R=[psE.b], W=[et.b])
                            ps_pool.free(psE)
                            act(ot.t[:, :], psO.t[:, :], AF.Copy, R=[psO.b], W=[ot.b])
                            ps_pool.free(psO)
                            tt(act2[:, g, tsl(0)], et.t[:, :], ot.t[:, :], ALU.add, R=[et.b, ot.b], W=[b_a2[g][0]], eng="pool")
                            tt(act2[:, g, tsl(1)], et.t[:, :], ot.t[:, :], ALU.subtract, R=[et.b, ot.b], W=[b_a2[g][1]], eng="pool")
                            tmp_pool.free(et)
                            tmp_pool.free(ot)
                            if g == 3:
                                w_free(st_.pop("dc"))
                                w_free(st_.pop("ds"))
                        ut_pool.free(ut)

                    nk = 4
                    if P:
                        for i in range(1, nk + 2):
                            enter()
                            if 0 <= i - 2 < nk:
                                stage2(i - 2)
                            if i - 1 < nk:
                                stage1((i - 1) % 4)
                            leave()
                            yield
                    else:
                        for g in range(4):
                            enter()
                            stage1(g)
                            leave()
                            yield
                            enter()
                            stage2(g)
                            leave()
                            yield
                    for g in range(4):
                        tmp_pool.free(zfs[g])

                S.phase = kind + str(l) + ":lru"
                fg = fft_gen()
                alias_import()
                tmp_pool.free_list.extend(lru_extra)
                bpads = [pad_pool.alloc() for _ in range(3)]
                bslots = [Slot(p_.b, p_.t[:, 0:1024].bitcast(F32), "bp") for p_ in bpads]
                tmp_pool.free_list.extend(bslots)
                bd = w_get(tag + "bd")
                bdv = bd.t[:, :, :].rearrange("p a b -> p (a b)").rearrange("p (k n) -> p k n", k=32)
                def lru_front(c, slot, jj):
                    pad = pad_pool.alloc()
                    S.op("pool", lambda h, pad=pad: h.memset(pad.t[:, :], 0.0), writes=[pad.b])
                    dg = dg_pool.alloc()
                    build_diag(dg, pcol(l, "lru_conv_w")[:, c::8], 4)
                    for t in range(NT):
                        ps = ps_pool.alloc()
                        zmm(ps, slot, jj, t)
                        act(padv(pad.t, 2, 1, t, 2), v(ps.t[:, :]), AF.Copy, R=[ps.b], W=[pad.b])
                        ps_pool.free(ps)
                    xc32 = []
                    xcbs = tmp_pool.alloc()
                    xcbv = xcbs.t[:, :].bitcast(BF16)
                    for t in range(NT):
                        ps = ps_pool.alloc()
                        mm((ps, v(ps.t[:, :])), [(dg.t[:, k, :], padv(pad.t, 2, 1, t, k)) for k in range(4)],
                           R=[dg.b, pad.b])
                        x32 = tmp_pool.alloc()
                        ts(x32.t[:, :], ps.t[:, :], pcol(l, "lru_conv_b", c, 1), None, ALU.add,
                           R=[ps.b, b_pp], W=[x32.b])
                        S.op("dve", lambda h, x32=x32, t=t: h.tensor_copy(out=xcbv[:, tsl(t)], in_=x32.t[:, :]),
                             reads=[x32.b], writes=[xcbs.b])
                        ps_pool.free(ps)
                        xc32.append(x32)
                    pad_pool.free(pad)
                    dg_pool.free(dg)
                    return (c, xc32, xcbs, xcbv)

                def lru_back(state_):
                    c, xc32, xcbs, xcbv = state_
                    A_, S_, I_ = {}, {}, {}
                    for d in (0, 1):
                        for t in range(NT):
                            xbv = xcbv[:, tsl(t)]
                            psa = ps_pool.alloc()
                            mm(psa, [(bdv[:, d * 8 + c, :], xbv)], R=[bd.b, xcbs.b])
                            psx = ps_pool.alloc()
                            mm(psx, [(bdv[:, 16 + d * 8 + c, :], xbv)], R=[bd.b, xcbs.b])
                            a_ = tmp_pool.alloc()
                            act(a_.t[:, :], psa.t[:, :], AF.Tanh, bias=dcol(l, "hba", d * 8 + c, 1), scale=0.5,
                                R=[psa.b, b_dpl], W=[a_.b])
                            ps_pool.free(psa)
                            act(a_.t[:, :], a_.t[:, :], AF.Exp, bias=dcol(l, "hkk", d * 8 + c, 1),
                                scale=dcol(l, "hkk", d * 8 + c, 1), R=[a_.b, b_dpl], W=[a_.b])
                            s_ = tmp_pool.alloc()
                            tt(s_.t[:, :], a_.t[:, :], a_.t[:, :], ALU.mult, R=[a_.b], W=[s_.b], eng="pool")
                            i_ = tmp_pool.alloc()
                            act(i_.t[:, :], psx.t[:, :], AF.Tanh, bias=dcol(l, "hbx", d * 8 + c, 1), scale=0.5,
                                R=[psx.b, b_dpl], W=[i_.b])
                            ps_pool.free(psx)
                            A_[(d, t)], S_[(d, t)], I_[(d, t)] = a_, s_, i_
                    for d in (0, 1):
                        for t in range(NT):
                            s_ = S_[(d, t)]
                            act(s_.t[:, :], s_.t[:, :], AF.Sqrt, bias=cq, scale=-0.25, R=[s_.b, b_const], W=[s_.b])
                    hf = [None] * NT
                    hb = []
                    for d in (0, 1):
                        order = list(range(NT)) if d == 0 else list(range(NT - 1, -1, -1))
                        prev_h = None
                        for t in order:
                            a_, s_, i_ = A_[(d, t)], S_[(d, t)], I_[(d, t)]
                            stt(i_.t[:, :], i_.t[:, :], 1.0, xc32[t].t[:, :], ALU.add, ALU.mult,
                                R=[i_.b, xc32[t].b], W=[i_.b])
                            tt(i_.t[:, :], i_.t[:, :], s_.t[:, :], ALU.mult, R=[i_.b, s_.b], W=[i_.b])
                            tmp_pool.free(s_)
                            h_ = tmp_pool.alloc()
                            for sq in range(nseq):
                                lo, hi = (sq * 256, (sq + 1) * 256) if P else (0, 512)
                                if P:
                                    init = 0.0
                                    rd = []
                                elif prev_h is None:
                                    init = pg[:, 24 + (l * 2 + d) * 8 + c:24 + (l * 2 + d) * 8 + c + 1]
                                    rd = [b_pg]
                                else:
                                    init = prev_h.t[:, 511:512] if d == 0 else prev_h.t[:, 0:1]
                                    rd = [prev_h.b]
                                if d == 0:
                                    o_, a0_, a1_ = h_.t[:, lo:hi], a_.t[:, lo:hi], i_.t[:, lo:hi]
                                else:
                                    o_, a0_, a1_ = (h_.t[:, lo:hi][:, ::-1], a_.t[:, lo:hi][:, ::-1],
                                                    i_.t[:, lo:hi][:, ::-1])
                                S.op("dve", lambda h, o_=o_, a0_=a0_, a1_=a1_, init=init: h.tensor_tensor_scan(
                                    out=o_, data0=a0_, data1=a1_, initial=init, op0=ALU.mult, op1=ALU.add),
                                    reads=[a_.b, i_.b] + rd, writes=[h_.b])
                                if P:
                                    col = ((l * 2 + d) * 8 + c) * 2 + sq
                                    src = h_.t[:, hi - 1:hi] if d == 0 else h_.t[:, lo:lo + 1]
                                    S.op("dve", lambda h, col=col, src=src: h.tensor_copy(out=stT[:, col:col + 1], in_=src),
                                         reads=[h_.b], writes=[b_st])
                            tmp_pool.free(a_)
                            tmp_pool.free(i_)
                            if d == 0:
                                hf[t] = h_
                            else:
                                hb.append(h_)
                                tt(hf[t].t[:, :], hf[t].t[:, :], h_.t[:, :], ALU.add, R=[h_.b, hf[t].b], W=[hf[t].b])
                                tt(act1[:, c, tsl(t)], act1[:, c, tsl(t)], hf[t].t[:, :], ALU.mult,
                                   R=[hf[t].b, b_a1[c][t]], W=[b_a1[c][t]])
                            prev_h = h_
                    for t in range(NT):
                        tmp_pool.free(xc32[t])
                        tmp_pool.free(hf[t])
                    tmp_pool.free(xcbs)
                    for h__ in hb:
                        tmp_pool.free(h__)

                def pe_keepwarm(n=20):
                    ps = ps_pool.alloc()
                    mm(ps, [(ident[:, :], hT[:, 0, 0:512]) for _ in range(n)], R=[b_ident, b_hT[0][0]])
                    ps_pool.free(ps)

                prevst = None
                for g in (0, 1):
                    slot = w_get(tag + "in%d" % g)
                    for jj in range(4):
                        st_ = lru_front(g * 4 + jj, slot, jj)
                        if prevst is not None:
                            lru_back(prevst)
                            next(fg, None)
                            pe_keepwarm()
                        prevst = st_
                    w_free(slot)
                lru_back(prevst)
                next(fg, None)
                w_free(bd)
                for _ in fg:
                    pass
                for x_ in lru_extra + bslots:
                    tmp_pool.free_list.remove(x_)
                for p_ in bpads:
                    pad_pool.free(p_)
                alias_export()

                S.phase = kind + str(l) + ":outA"
                holder = {}

                def get_A(hh):
                    holder[hh] = w_get(tag + "lruout%d" % hh)
                    return holder[hh]

                def free_A(hh):
                    w_free(holder.pop(hh))
                out_and_gate(l, tag, 0, get_A, free_A, (lambda k, m: k * 512 + (m % 4) * 128), 8,
                             lambda k, t: act1[:, k, tsl(t)], b_a1)

                next(mgen, None)
                S.phase = kind + str(l) + ":outB"
                go, fo, wc = wide_out(tag, "fourier_out")
                out_and_gate(l, tag, 1, go, fo, wc, 4, lambda k, t: act2[:, k, tsl(t)], b_a2)

                next(mgen, None)
                S.phase = kind + str(l) + ":conf"
                cpads = [pad_pool.alloc() for _ in range(4)]
                slot = w_get(tag + "in6")
                for c in range(4):
                    S.op("pool", lambda h, c=c: h.memset(cpads[c].t[:, :], 0.0), writes=[cpads[c].b])
                    for t in range(NT):
                        ps = ps_pool.alloc()
                        zmm(ps, slot, c, t)
                        act(padv(cpads[c].t, 15, 15, t, 15), v(ps.t[:, :]), AF.Tanh, scale=0.5, R=[ps.b], W=[cpads[c].b])
                        ps_pool.free(ps)
                w_free(slot)
                slot = w_get(tag + "in5")
                for c in range(4):
                    for t in range(NT):
                        ps = ps_pool.alloc()
                        zmm(ps, slot, c, t)
                        stt(padv(cpads[c].t, 15, 15, t, 15), padv(cpads[c].t, 15, 15, t, 15), 1.0, v(ps.t[:, :]),
                            ALU.add, ALU.mult, R=[ps.b, cpads[c].b], W=[cpads[c].b])
                        ps_pool.free(ps)
                w_free(slot)
                b_uc = [[r1f_bufs(4 + c, t) for t in range(NT)] for c in range(4)]
                for c in range(4):
                    pss = [ps_pool.alloc() for _ in range(NT)]
                    groups = list(range(0, 31, 9))
                    for gi, k0 in enumerate(groups):
                        n_ = min(9, 31 - k0)
                        dg = dg_pool.alloc()
                        build_diag(dg, dcol(l, "cwh")[:, c::4][:, k0:k0 + n_], n_)
                        for t in range(NT):
                            mm((pss[t], v(pss[t].t[:, :])),
                               [(dg.t[:, k, :], padv(cpads[c].t, 15, 15, t, k0 + k)) for k in range(n_)],
                               R=[cpads[c].b, dg.b], start=(gi == 0), stop=(gi == len(groups) - 1))
                        dg_pool.free(dg)
                    for t in range(NT):
                        act(act1f[:, c, tsl(t)], pss[t].t[:, :], AF.Identity, bias=pcol(l, "conf_dw_b", c, 1),
                            R=[pss[t].b, b_pp], W=b_uc[c][t])
                        ps_pool.free(pss[t])
                for c in range(4):
                    pad_pool.free(cpads[c])
                def ln_gen():
                    ph0 = S.phase
                    for t in range(NT):
                        psm = ps_pool.alloc()
                        psq = ps_pool.alloc()
                        ub = []
                        for c in range(4):
                            u_ = tmp_pool.alloc()
                            uv = u_.t[:, :].bitcast(BF16)
                            act(uv[:, 0:512], act1f[:, c, tsl(t)], AF.Copy, R=b_uc[c][t], W=[u_.b])
                            act(uv[:, 512:1024], act1f[:, c, tsl(t)], AF.Square, R=b_uc[c][t], W=[u_.b])
                            ub.append(u_)
                        mm(psm, [(onesb[:, :], ub[c].t[:, :].bitcast(BF16)[:, 0:512]) for c in range(4)],
                           R=[b_ident] + [u_.b for u_ in ub])
                        mm(psq, [(onesb[:, :], ub[c].t[:, :].bitcast(BF16)[:, 512:1024]) for c in range(4)],
                           R=[b_ident] + [u_.b for u_ in ub])
                        for u_ in ub:
                            tmp_pool.free(u_)
                        yield
                        mean = tmp_pool.alloc()
                        act(mean.t[:, :], psm.t[:, :], AF.Copy, scale=1.0 / 512, R=[psm.b], W=[mean.b])
                        ps_pool.free(psm)
                        var = tmp_pool.alloc()
                        tt(var.t[:, :], mean.t[:, :], mean.t[:, :], ALU.mult, R=[mean.b], W=[var.b])
                        stt(var.t[:, :], psq.t[:, :], 1.0 / 512, var.t[:, :], ALU.mult, ALU.subtract,
                            R=[psq.b, var.b], W=[var.b])
                        ps_pool.free(psq)
                        yield
                        act(var.t[:, :], var.t[:, :], AF.Ln, bias=ceps, scale=1.0, R=[var.b, b_const], W=[var.b])
                        act(var.t[:, :], var.t[:, :], AF.Exp, scale=-0.5, R=[var.b], W=[var.b])
                        yield
                        for c in range(4):
                            d_ = tmp_pool.alloc()
                            tt(d_.t[:, :], act1f[:, c, tsl(t)], mean.t[:, :], ALU.subtract, R=b_uc[c][t] + [mean.b], W=[d_.b])
                            tt(d_.t[:, :], d_.t[:, :], var.t[:, :], ALU.mult, R=[d_.b, var.b], W=[d_.b])
                            act(act2[:, c, tsl(t)], d_.t[:, :], AF.Silu, bias=pcol(l, "conf_ln_b", c, 1),
                                scale=pcol(l, "conf_ln_g", c, 1), R=[d_.b, b_pp], W=[b_a2[c][t]])
                            tmp_pool.free(d_)
                            if c % 2 == 1:
                                yield
                        tmp_pool.free(mean)
                        tmp_pool.free(var)

                S.phase = kind + str(l) + ":scfront"
                lg = ln_gen()
                next(lg, None)
                spads = [pad_pool.alloc() for _ in range(4)]
                slot = w_get(tag + "in9")
                for c in range(4):
                    S.op("pool", lambda h, c=c: h.memset(spads[c].t[:, :], 0.0), writes=[spads[c].b])
                    for t in range(NT):
                        ps = ps_pool.alloc()
                        zmm(ps, slot, c, t)
                        act(padv(spads[c].t, 1, 1, t, 1), v(ps.t[:, :]), AF.Copy, R=[ps.b], W=[spads[c].b])
                        ps_pool.free(ps)
                        next(lg, None)
                w_free(slot)
                slot = w_get(tag + "in7")
                for c in range(4):
                    for t in range(NT):
                        ps = ps_pool.alloc()
                        zmm(ps, slot, c, t)
                        tt(padv(spads[c].t, 1, 1, t, 1), padv(spads[c].t, 1, 1, t, 1), v(ps.t[:, :]), ALU.mult,
                           R=[ps.b, spads[c].b], W=[spads[c].b])
                        ps_pool.free(ps)
                        next(lg, None)
                w_free(slot)
                for _ in lg:
                    pass
                next(mgen, None)
                S.phase = kind + str(l) + ":outC"
                go, fo, wc = wide_out(tag, "conf_out")
                out_and_gate(l, tag, 2, go, fo, wc, 4, lambda k, t: act2[:, k, tsl(t)], b_a2)

                next(mgen, None)
                S.phase = kind + str(l) + ":sc"
                for c in range(4):
                    dg = dg_pool.alloc()
                    build_diag(dg, pcol(l, "sc_conv_w")[:, c::4], 3)
                    for t in range(NT):
                        ps = ps_pool.alloc()
                        mm((ps, v(ps.t[:, :])), [(dg.t[:, k, :], padv(spads[c].t, 1, 1, t, k)) for k in range(3)],
                           R=[dg.b, spads[c].b])
                        act(act1f[:, c, tsl(t)], ps.t[:, :], AF.Identity, bias=pcol(l, "sc_conv_b", c, 1),
                            R=[ps.b, b_pp], W=b_uc[c][t])
                        ps_pool.free(ps)
                    dg_pool.free(dg)
                    pad_pool.free(spads[c])
                slot = w_get(tag + "in8")
                for c in range(4):
                    for t in range(NT):
                        ps = ps_pool.alloc()
                        zmm(ps, slot, c, t)
                        tt(act2[:, c, tsl(t)], act1f[:, c, tsl(t)], ps.t[:, :], ALU.mult,
                           R=[ps.b] + b_uc[c][t], W=[b_a2[c][t]])
                        ps_pool.free(ps)
                w_free(slot)
                next(mgen, None)
                S.phase = kind + str(l) + ":outD"
                go, fo, wc = wide_out(tag, "sc_out")
                out_and_gate(l, tag, 3, go, fo, wc, 4, lambda k, t: act2[:, k, tsl(t)], b_a2)

                next(mgen, None)
                S.phase = kind + str(l) + ":wo"
                st2 = stats_begin()
                for hh in range(2):
                    slot = w_get(tag + "wo%d" % hh)
                    for t in range(NT):
                        for mq in range(4):
                            m = hh * 4 + mq
                            ps = ps_pool.alloc()
                            mm(ps, [(slot.t[:, k, mq * 128:(mq + 1) * 128], merged[:, k, tsl(t)]) for k in range(8)],
                               R=[slot.b] + [b_mg[k][t] for k in range(8)])
                            stt(xT[:, m, tsl(t)], ps.t[:, :], der[:, 16 + m:17 + m], xT[:, m, tsl(t)], ALU.mult, ALU.add,
                                R=[ps.b, b_der, b_xT[m][t]], W=[b_xT[m][t]])
                            ps_pool.free(ps)
                            stats_add_delayed(st2, m, t)
                    w_free(slot)

                S.phase = kind + str(l) + ":norm2"
                rmsnorm_to(l, lambda c: der[:, 8 + c:9 + c], lambda c: modT[:, 24 + c:25 + c],
                           lambda c, t: hT[:, c, tsl(t)], b_hT, st=st2)
                S.phase = kind + str(l) + ":ffnup"
                ntap = len(ffn_taps)
                fstate = {}

                def ffn_front(j, slot, jq):
                    pa = pad_pool.alloc()
                    pv = pad_pool.alloc()
                    S.op("pool", lambda h, pa=pa: h.memset(pa.t[:, :], 0.0), writes=[pa.b])
                    S.op("pool", lambda h, pv=pv: h.memset(pv.t[:, :], 0.0), writes=[pv.b])
                    dga = dg_pool.alloc()
                    dgv = dg_pool.alloc()
                    wa_ = pcol(l, "ffn_conv_w")[:, j::44]
                    wv_ = pcol(l, "ffn_conv_w")[:, 22 + j::44]
                    if P:
                        wa_, wv_ = wa_[:, 3:6], wv_[:, 3:6]
                    if taps_for(j)[0]:
                        build_diag(dga, wa_, ntap)
                        build_diag(dgv, wv_, ntap)
                    for t in range(NT):
                        ps = ps_pool.alloc()
                        zmm(ps, slot, jq, t)
                        act(padv2(pa.t, t, 1, 1), v2(ps.t[:, :]), AF.Copy, R=[ps.b], W=[pa.b])
                        ps_pool.free(ps)
                        ps = ps_pool.alloc()
                        zmm(ps, slot, 2 + jq, t)
                        S.op("dve", lambda h, ps=ps, pv=pv, t=t: h.tensor_copy(out=padv2(pv.t, t, 1, 1), in_=v2(ps.t[:, :])),
                             reads=[ps.b], writes=[pv.b])
                        ps_pool.free(ps)
                    fstate[j] = (pa, pv, dga, dgv)

                def taps_for(j):
                    all_pe = (j >= 20) or (P and j % 3 == 2)
                    if all_pe:
                        return list(enumerate(ffn_taps)), []
                    if P:
                        return [], [(3 + kc_, (1, kc_)) for kc_ in range(3)]
                    return ([(i, kk_) for i, kk_ in enumerate(ffn_taps) if kk_[0] < 2],
                            [(i, kk_) for i, kk_ in enumerate(ffn_taps) if kk_[0] == 2])

                def ffn_back(j):
                    pa, pv, dga, dgv = fstate.pop(j)
                    pe_taps, ve_taps = taps_for(j)
                    wcol = pcol(l, "ffn_conv_w")
                    accs = {}
                    for t in range(NT):
                        for nm, pd, joff in (("a", pa, j), ("v", pv, 22 + j)):
                            if not ve_taps:
                                continue
                            acc = tmp_pool.alloc()
                            (i0, (r0, c0)), (i1, (r1, c1)), (i2, (r2, c2)) = ve_taps
                            act(v2(acc.t[:, :]), padv2(pd.t, t, r0, c0), AF.Copy, scale=wcol[:, i0 * 44 + joff:i0 * 44 + joff + 1],
                                R=[pd.b, b_pp], W=[acc.b])
                            stt(v2(acc.t[:, :]), padv2(pd.t, t, r1, c1), wcol[:, i1 * 44 + joff:i1 * 44 + joff + 1], v2(acc.t[:, :]),
                                ALU.mult, ALU.add, R=[pd.b, b_pp, acc.b], W=[acc.b])
                            stt(v2(acc.t[:, :]), padv2(pd.t, t, r2, c2), wcol[:, i2 * 44 + joff:i2 * 44 + joff + 1], v2(acc.t[:, :]),
                                ALU.mult, ALU.add, R=[pd.b, b_pp, acc.b], W=[acc.b])
                            accs[(nm, t)] = acc
                    for t in range(NT):
                        if not pe_taps:
                            aa, av = accs.pop(("a", t)), accs.pop(("v", t))
                            ga = tmp_pool.alloc()
                            act(ga.t[:, :], aa.t[:, :], AF.Gelu, bias=pcol(l, "ffn_conv_b", j, 1), R=[aa.b, b_pp], W=[ga.b])
                            tmp_pool.free(aa)
                            stt(ffact[:, j, tsl(t)], av.t[:, :], pcol(l, "ffn_conv_b", 22 + j, 1), ga.t[:, :], ALU.add, ALU.mult,
                                R=[av.b, ga.b, b_pp], W=[b_ff[j][t]])
                            tmp_pool.free(av)
                            tmp_pool.free(ga)
                            continue
                        psa = ps_pool.alloc()
                        mm((psa, v2(psa.t[:, :])), [(dga.t[:, i, :], padv2(pa.t, t, kr, kc)) for i, (kr, kc) in pe_taps],
                           R=[dga.b, pa.b])
                        psv = ps_pool.alloc()
                        mm((psv, v2(psv.t[:, :])), [(dgv.t[:, i, :], padv2(pv.t, t, kr, kc)) for i, (kr, kc) in pe_taps],
                           R=[dgv.b, pv.b])
                        ga = tmp_pool.alloc()
                        if ve_taps:
                            aa, av = accs.pop(("a", t)), accs.pop(("v", t))
                            tt(aa.t[:, :], aa.t[:, :], psa.t[:, :], ALU.add, R=[aa.b, psa.b], W=[aa.b])
                            ps_pool.free(psa)
                            act(ga.t[:, :], aa.t[:, :], AF.Gelu, bias=pcol(l, "ffn_conv_b", j, 1), R=[aa.b, b_pp], W=[ga.b])
                            tmp_pool.free(aa)
                            tt(av.t[:, :], av.t[:, :], psv.t[:, :], ALU.add, R=[av.b, psv.b], W=[av.b])
                            ps_pool.free(psv)
                            stt(ffact[:, j, tsl(t)], av.t[:, :], pcol(l, "ffn_conv_b", 22 + j, 1), ga.t[:, :], ALU.add, ALU.mult,
                                R=[av.b, ga.b, b_pp], W=[b_ff[j][t]])
                            tmp_pool.free(av)
                        else:
                            act(ga.t[:, :], psa.t[:, :], AF.Gelu, bias=pcol(l, "ffn_conv_b", j, 1), R=[psa.b, b_pp], W=[ga.b])
                            ps_pool.free(psa)
                            stt(ffact[:, j, tsl(t)], psv.t[:, :], pcol(l, "ffn_conv_b", 22 + j, 1), ga.t[:, :], ALU.add, ALU.mult,
                                R=[psv.b, ga.b, b_pp], W=[b_ff[j][t]])
                            ps_pool.free(psv)
                        tmp_pool.free(ga)
                    pad_pool.free(pa)
                    pad_pool.free(pv)
                    dg_pool.free(dga)
                    dg_pool.free(dgv)

                prevj = None
                for j in range(22):
                    q, jq = divmod(j, 2)
                    if jq == 0:
                        slot = w_get(tag + "up%d" % q)
                    ffn_front(j, slot, jq)
                    if jq == 1:
                        w_free(slot)
                        if q % 2 == 1:
                            next(mgen, None)
                    if prevj is not None:
                        ffn_back(prevj)
                    prevj = j
                ffn_back(prevj)
                for _ in mgen:
                    pass
                S.phase = kind + str(l) + ":ffndown"
                pend["st"] = stats_begin()
                for hh in range(2):
                    dsl = [w_get(tag + "down%d_%d" % (hh, kg)) for kg in range(3)]
                    for t in range(NT):
                        for mq in range(4):
                            m = hh * 4 + mq
                            ps = ps_pool.alloc()
                            mm(ps, [(dsl[k // 8].t[:, k % 8, mq * 128:(mq + 1) * 128], ffact[:, k, tsl(t)]) for k in range(22)],
                               R=[s_.b for s_ in dsl] + [b_ff[k][t] for k in range(22)])
                            stt(xT[:, m, tsl(t)], ps.t[:, :], modT[:, 40 + m:41 + m], xT[:, m, tsl(t)], ALU.mult, ALU.add,
                                R=[ps.b, b_modl[l], b_xT[m][t]], W=[b_xT[m][t]])
                            ps_pool.free(ps)
                            stats_add_delayed(pend["st"], m, t)
                    for s_ in dsl:
                        w_free(s_)

            pend = {"st": None}
            for l in range(DEPTH):
                layer(l)
            S.phase = kind + ":final"
            yview = R1[:, 0:16384].bitcast(F32).rearrange("p (c t) -> p c t", c=8)
            ydst = yP if P else yS

            def out_tile(t):
                S.op("sp", lambda h, t=t: [h.dma_start(out=ydst[:, :, tsl(t)], in_=yview[:, :, tsl(t)])],
                     reads=[b for c in range(8) for b in r1f_bufs(c, t)], dma_buf=b_out)
            rmsnorm_to(0, lambda c: pg[:, c:c + 1], None, lambda c, t: yview[:, c, tsl(t)], None,
                       dst_buf_fn=r1f_bufs, st=pend["st"], per_tile_cb=out_tile)
            if P:
                S.op("sp", lambda h: [h.dma_start(out=stO, in_=stT[:, :])], reads=[b_st], dma_buf=b_st2)

        run_pass("P")
        run_pass("S")
        assert wstate["next_get"] == len(wq), (wstate["next_get"], len(wq))
        build_nc.pe_log = S.pe_log
        build_nc.order = recorded
        S.emit()
    return nc


def _fm(v):
    v = np.asarray(v, dtype=np.float32).reshape(-1)
    return np.ascontiguousarray(v.reshape(-1, 128).T)


_NC_CACHE = {}


def kernel(x_prompt, x_sample, state_lru, c, c_ctx, norm1_g, norm2_g, w_mod, b_mod, w_in, b_gate,
           lru_conv_w, lru_conv_b, lru_wa, lru_ba, lru_wx, lru_bx, lru_lam, lru_out, fourier_out,
           conf_dw_w, conf_dw_b, conf_ln_g, conf_ln_b, conf_out, sc_conv_w, sc_conv_b, sc_out, w_o,
           ffn_up, ffn_conv_w, ffn_conv_b, ffn_down, final_g):
    f32 = lambda a: np.ascontiguousarray(np.asarray(a, dtype=np.float32))
    loc = dict(norm1_g=norm1_g, norm2_g=norm2_g, b_mod=b_mod, b_gate=b_gate, lru_conv_w=lru_conv_w,
               lru_conv_b=lru_conv_b, lru_ba=lru_ba, lru_bx=lru_bx, lru_lam=lru_lam, conf_dw_w=conf_dw_w,
               conf_dw_b=conf_dw_b, conf_ln_g=conf_ln_g, conf_ln_b=conf_ln_b, sc_conv_w=sc_conv_w,
               sc_conv_b=sc_conv_b, ffn_conv_w=ffn_conv_w, ffn_conv_b=ffn_conv_b)
    pp = np.zeros((DEPTH, 128, NP), np.float32)
    for l in range(DEPTH):
        for name, (o, n) in PL.items():
            pp[l, :, o:o + n] = _fm(np.asarray(loc[name])[l])
    m = np.arange(128)
    ang = 2.0 * np.pi * np.outer(m, m) / 128.0
    cs128 = np.concatenate([np.cos(ang), np.sin(ang)], axis=1).astype(np.float32)

    def dft(Lh):
        i = np.arange(Lh)
        a = 2.0 * np.pi * (np.outer(i, i) % Lh) / Lh
        s = 1.0 / np.sqrt(Lh * 128.0)
        return (np.cos(a) * s).astype(np.float32), (-np.sin(a) * s).astype(np.float32)
    c256, s256 = dft(256)
    dft256 = np.ascontiguousarray(np.concatenate([c256, s256], axis=1))
    li = np.arange(1024)
    ang_ = 2.0 * np.pi * (np.outer(li, np.arange(512)) % 1024) / 1024.0
    sc_ = 1.0 / np.sqrt(1024 * 128.0)
    cfull = (np.cos(ang_) * sc_).astype(np.float32)
    sfull = (-np.sin(ang_) * sc_).astype(np.float32)
    dftc = np.concatenate([cfull[0::2], cfull[1::2]], axis=0)
    dfts = np.concatenate([sfull[0::2], sfull[1::2]], axis=0)
    ident = np.eye(128, dtype=np.float32)

    shared = dict(pp=pp, w_mod=f32(w_mod), w_in=f32(w_in), lru_wa=f32(lru_wa), lru_wx=f32(lru_wx),
                  lru_out=f32(lru_out), fourier_out=f32(fourier_out), conf_out=f32(conf_out), sc_out=f32(sc_out),
                  w_o=f32(w_o), ffn_up=f32(ffn_up), ffn_down=f32(ffn_down), cs128=cs128, dft256=dft256,
                  dftc=np.ascontiguousarray(dftc), dfts=np.ascontiguousarray(dfts), ident=ident)
    x_prompt = np.asarray(x_prompt, np.float32)
    x_sample = np.asarray(x_sample, np.float32)
    state_lru = np.asarray(state_lru, np.float32)
    c = np.asarray(c, np.float32)
    in_maps = []
    for i in range(NCORES):
        pg = np.zeros((128, NG), np.float32)
        pg[:, 0:8] = _fm(final_g)
        pg[:, 8:16] = _fm(c_ctx)
        pg[:, 16:24] = _fm(c[i])
        pg[:, 24:56] = _fm(state_lru[i])
        d = dict(shared)
        d["xP"] = np.ascontiguousarray(x_prompt[2 * i:2 * i + 2].reshape(512, D).T)
        d["xS"] = np.ascontiguousarray(x_sample[i].T)
        d["pg"] = pg
        in_maps.append(d)
    if "nc" not in _NC_CACHE:
        build_nc()
        _NC_CACHE["nc"] = build_nc(order=list(build_nc.order))
    nc = _NC_CACHE["nc"]
    res = run_bass_kernel_spmd(nc, in_maps, core_ids=list(range(NCORES)))
    y_prompt = np.zeros((16, 256, D), np.float32)
    y_sample = np.zeros((8, 1024, D), np.float32)
    new_state = np.zeros((16, DEPTH, 2, D), np.float32)
    for i in range(NCORES):
        r = res.results[i]
        yp = np.asarray(r["yP"]).transpose(2, 1, 0).reshape(512, D)
        y_prompt[2 * i] = yp[0:256]
        y_prompt[2 * i + 1] = yp[256:512]
        y_sample[i] = np.asarray(r["yS"]).transpose(2, 1, 0).reshape(1024, D)
        st = np.asarray(r["stO"]).reshape(128, DEPTH, 2, 8, 2)
        for s in range(2):
            new_state[2 * i + s] = st[:, :, :, :, s].transpose(1, 2, 3, 0).reshape(DEPTH, 2, D)
    return (y_prompt, y_sample, new_state)
```

```python
import numpy as np
from contextlib import ExitStack
import concourse.bass as bass
import concourse.mybir as mybir
from concourse.bass_utils import run_bass_kernel_spmd

F32 = mybir.dt.float32
BF16 = mybir.dt.bfloat16
AF = mybir.ActivationFunctionType
ALU = mybir.AluOpType

D = 1024
DEPTH = 2
D_IN = 9216
D_FF = 2816
NCORES = 8
EPS = 1e-6

_PLIST = [("norm1_g", 8), ("norm2_g", 8), ("b_mod", 48), ("b_gate", 32), ("lru_conv_w", 32), ("lru_conv_b", 8),
          ("lru_ba", 16), ("lru_bx", 16), ("lru_lam", 16), ("conf_dw_w", 124), ("conf_dw_b", 4), ("conf_ln_g", 4),
          ("conf_ln_b", 4), ("sc_conv_w", 12), ("sc_conv_b", 4), ("ffn_conv_w", 396), ("ffn_conv_b", 44)]
PL = {}
_o = 0
for _n, _c in _PLIST:
    PL[_n] = (_o, _c)
    _o += _c
NP = _o
_DLIST = [("hba", 16), ("hbx", 16), ("hkk", 16), ("hbg", 32), ("cwh", 124)]
DL = {}
_o = 0
for _n, _c in _DLIST:
    DL[_n] = (_o, _c)
    _o += _c
ND = _o
NG = 56


class Buf:
    __slots__ = ("name", "w", "r", "sem", "semv")

    def __init__(self, name):
        self.name = name
        self.w = None
        self.r = []
        self.sem = None
        self.semv = 0


class Sched:
    ENG = ("pe", "act", "dve", "pool", "sp")

    def __init__(self, nc, ctx):
        self.nc = nc
        self.ctx = ctx
        self.q = {e: [] for e in self.ENG}
        self.cnt = {e: 0 for e in self.ENG}
        self.esem = {e: ctx.enter_context(nc.semaphore("s_" + e)) for e in self.ENG}
        self.waited = {}
        self.dma_bufs = []
        self.pending = {e: [] for e in self.ENG}
        self.phase = "init"
        self.pe_log = []

    def _need(self, eng, tok, waits):
        sem, val, teng = tok
        if teng == eng and eng == "pe":
            return
        key = (eng, sem.name)
        if self.waited.get(key, 0) >= val:
            return
        self.waited[key] = val
        waits.append((sem, val))

    def op(self, eng, fn, reads=(), writes=(), dma_buf=None, n_dma=1):
        waits = self.pending[eng]
        self.pending[eng] = []
        for b in reads:
            if b.w is not None:
                self._need(eng, b.w, waits)
        for b in writes:
            if b.w is not None:
                self._need(eng, b.w, waits)
            for t in b.r:
                self._need(eng, t, waits)
        if dma_buf is not None:
            if dma_buf.sem is None:
                dma_buf.sem = self.ctx.enter_context(self.nc.semaphore("d_" + dma_buf.name))
                self.dma_bufs.append(dma_buf)
            dma_buf.semv += 16 * n_dma
            tok = (dma_buf.sem, dma_buf.semv, "dma")
            self.q[eng].append((waits, fn, dma_buf.sem))
        else:
            self.cnt[eng] += 1
            tok = (self.esem[eng], self.cnt[eng], eng)
            self.q[eng].append((waits, fn, None))
        for b in writes:
            b.w = tok
            b.r = []
        for b in reads:
            b.r.append(tok)
        return tok

    def fence(self):
        for e in self.ENG:
            for e2 in self.ENG:
                if self.cnt[e2] > 0:
                    self._need(e, (self.esem[e2], self.cnt[e2], e2 if e2 != "pe" else "x"), self.pending[e])
            for b in self.dma_bufs:
                self._need(e, (b.sem, b.semv, "dma"), self.pending[e])

    def emit(self):
        nc = self.nc
        handles = {"pe": "tensor", "act": "scalar", "dve": "vector", "pool": "gpsimd", "sp": "sync"}
        self.fence()
        with nc.Block() as block:
            for e in self.ENG:
                def body(h, e=e):
                    sem_e = self.esem[e]
                    for waits, fn, dsem in self.q[e]:
                        for (s, v) in waits:
                            h.wait_ge(s, v)
                        if dsem is not None:
                            for ins in fn(h):
                                ins.then_inc(dsem, 16)
                        else:
                            fn(h).then_inc(sem_e, 1)
                    for (s, v) in self.pending[e]:
                        h.wait_ge(s, v)
                getattr(block, handles[e])(body)


class Slot:
    __slots__ = ("b", "t", "name")

    def __init__(self, b, t, name):
        self.b = b
        self.t = t
        self.name = name


class FifoPool:
    def __init__(self, items):
        self.free_list = list(items)

    def alloc(self):
        assert self.free_list, "pool exhausted"
        return self.free_list.pop(0)

    def free(self, it):
        self.free_list.append(it)


def build_nc(order=None):
    nc = bass.Bass("TRN2", target_bir_lowering=False)
    dram = {}

    def din(name, shape):
        dram[name] = nc.dram_tensor(name, list(shape), F32, kind="ExternalInput").ap()
        return dram[name]

    xP = din("xP", [D, 512])
    xS = din("xS", [D, 1024])
    pp_d = din("pp", [DEPTH, 128, NP])
    pg_d = din("pg", [128, NG])
    w_mod = din("w_mod", [DEPTH, D, 6 * D])
    w_in = din("w_in", [DEPTH, D, D_IN])
    lru_wa = din("lru_wa", [DEPTH, 2, 8, 128, 128])
    lru_wx = din("lru_wx", [DEPTH, 2, 8, 128, 128])
    lru_out = din("lru_out", [DEPTH, D, D])
    fourier_out = din("fourier_out", [DEPTH, 512, D])
    conf_out = din("conf_out", [DEPTH, 512, D])
    sc_out = din("sc_out", [DEPTH, 512, D])
    w_o = din("w_o", [DEPTH, D, D])
    ffn_up = din("ffn_up", [DEPTH, D, 2 * D_FF])
    ffn_down = din("ffn_down", [DEPTH, D_FF, D])
    cs128_d = din("cs128", [128, 256])
    dft256_d = din("dft256", [256, 512])
    dftc_d = din("dftc", [1024, 512])
    dfts_d = din("dfts", [1024, 512])
    ident_d = din("ident", [128, 128])
    yP = nc.dram_tensor("yP", [128, 8, 512], F32, kind="ExternalOutput").ap()
    yS = nc.dram_tensor("yS", [128, 8, 1024], F32, kind="ExternalOutput").ap()
    stO = nc.dram_tensor("stO", [128, 64], F32, kind="ExternalOutput").ap()

    with ExitStack() as ctx:
        S = Sched(nc, ctx)

        def sb(name, shape, dt):
            return ctx.enter_context(nc.sbuf_tensor(name, list(shape), dt))

        xT = sb("xT", [128, 8, 1024], F32)
        hT = sb("hT", [128, 8, 1024], BF16)
        R1 = sb("R1", [128, 22 * 1024], BF16)
        merged = R1[:, 0:8192].rearrange("p (c t) -> p c t", c=8)
        act1 = R1[:, 8192:16384].rearrange("p (c t) -> p c t", c=8)
        act1f = R1[:, 8192:16384].bitcast(F32).rearrange("p (c t) -> p c t", c=4)
        act2 = R1[:, 16384:20480].rearrange("p (c t) -> p c t", c=4)
        ffact = R1[:, :].rearrange("p (c t) -> p c t", c=22)
        NWS = 5
        wsl = [sb("ws%d" % i, [128, 8, 512], BF16) for i in range(NWS)]
        NTMP = 18
        tmps = [sb("tmp%d" % i, [128, 512], F32) for i in range(NTMP)]
        NPAD = 5
        pads = [sb("pad%d" % i, [128, 1200], BF16) for i in range(NPAD)]
        NDG = 4
        dgs = [sb("dg%d" % i, [128, 9, 128], BF16) for i in range(NDG)]
        UTs = [sb("ut%d" % i, [128, 8, 256], BF16) for i in range(1)]
        pp = sb("ppt", [128, DEPTH, NP], F32)
        dpl = sb("dpl", [128, DEPTH, ND], F32)
        pg = sb("pgt", [128, NG], F32)
        der = sb("der", [128, 32], F32)
        scond = sb("scond", [128, 8, 2], BF16)
        modTall = sb("modTall", [128, DEPTH, 2, 48], F32)
        consts = sb("consts", [128, 8], F32)
        identf = sb("identf", [128, 128], F32)
        ident = sb("identb", [128, 128], BF16)
        onesb = sb("onesb", [128, 128], BF16)
        cs128 = sb("cs128t", [128, 256], BF16)
        d256 = sb("d256t", [128, 2, 512], BF16)
        stT = sb("stT", [128, 64], F32)
        small = sb("small", [128, 64], F32)
        psb = [ctx.enter_context(nc.psum_tensor("ps%d" % i, [128, 512], F32)) for i in range(8)]

        ws_pool = FifoPool([Slot(Buf("ws%d" % i), wsl[i], "ws%d" % i) for i in range(NWS)])
        tmp_pool = FifoPool([Slot(Buf("tmp%d" % i), tmps[i], "tmp%d" % i) for i in range(NTMP)])
        pad_pool = FifoPool([Slot(Buf("pad%d" % i), pads[i], "pad%d" % i) for i in range(NPAD)])
        dg_pool = FifoPool([Slot(Buf("dg%d" % i), dgs[i], "dg%d" % i) for i in range(NDG)])
        ut_pool = FifoPool([Slot(Buf("ut%d" % i), UTs[i], "ut%d" % i) for i in range(1)])
        ps_pool = FifoPool([Slot(Buf("ps%d" % i), psb[i], "ps%d" % i) for i in range(8)])
        _xr = [R1[:, 1024 * j:1024 * (j + 1)].bitcast(F32) for j in list(range(8)) + list(range(20, 22))]
        lru_extra = [Slot(Buf("xtmp%d" % i), _xr[i], "xtmp%d" % i) for i in range(len(_xr))]
        b_xT = [[Buf("xT%d_%d" % (c, t)) for t in range(2)] for c in range(8)]
        b_hT = [[Buf("hT%d_%d" % (c, t)) for t in range(2)] for c in range(8)]
        b_mg = [[Buf("mg%d_%d" % (c, t)) for t in range(2)] for c in range(8)]
        b_a1 = [[Buf("a1%d_%d" % (c, t)) for t in range(2)] for c in range(8)]
        b_a2 = [[Buf("a2%d_%d" % (c, t)) for t in range(2)] for c in range(4)]
        b_ffx = [[Buf("ff%d_%d" % (c, t)) for t in range(2)] for c in range(2)]
        b_ff = [b_mg[j] if j < 8 else (b_a1[j - 8] if j < 16 else (b_a2[j - 16] if j < 20 else b_ffx[j - 20]))
                for j in range(22)]

        _xal = [b_mg[j] for j in range(8)] + [b_ffx[j] for j in range(2)]

        def alias_import():
            for sl, bufs in zip(lru_extra, _xal):
                toks = []
                for b in bufs:
                    if b.w is not None:
                        toks.append(b.w)
                    toks += b.r
                sl.b.w = None
                sl.b.r = toks

        def alias_export():
            for sl, bufs in zip(lru_extra, _xal):
                toks = list(sl.b.r) + ([sl.b.w] if sl.b.w is not None else [])
                for b in bufs:
                    b.r = b.r + toks


        def r1f_bufs(c, t):
            idx = 2 * c + t
            lst = b_mg[idx] if idx < 8 else b_a1[idx - 8]
            return [lst[0], lst[1]]
        b_pp, b_dpl, b_pg, b_der, b_scond = Buf("pp"), Buf("dpl"), Buf("pg"), Buf("der"), Buf("scond")
        b_modl = [Buf("mod0"), Buf("mod1")]
        b_const, b_ident, b_identf, b_cs, b_d256, b_st, b_small = (Buf("const"), Buf("ident"), Buf("identf"), Buf("cs"),
                                                                    Buf("d256"), Buf("st"), Buf("small"))
        b_out = Buf("out")
        b_st2 = Buf("st2")

        def act(out, in_, func, bias=None, scale=1.0, R=(), W=()):
            def f(h):
                if bias is None:
                    return h.activation(out=out, in_=in_, func=func, scale=scale)
                return h.activation(out=out, in_=in_, func=func, bias=bias, scale=scale)
            S.op("act", f, reads=R, writes=W)

        def tt(out, in0, in1, op, R=(), W=(), eng="dve"):
            S.op(eng, lambda h: h.tensor_tensor(out=out, in0=in0, in1=in1, op=op), reads=R, writes=W)

        def ts(out, in0, s1, s2, op0, op1=None, R=(), W=(), eng="dve"):
            def f(h):
                if op1 is None:
                    return h.tensor_scalar(out=out, in0=in0, scalar1=s1, scalar2=None, op0=op0)
                return h.tensor_scalar(out=out, in0=in0, scalar1=s1, scalar2=s2, op0=op0, op1=op1)
            S.op(eng, f, reads=R, writes=W)

        def stt(out, in0, scalar, in1, op0, op1, R=(), W=()):
            S.op("dve", lambda h: h.scalar_tensor_tensor(out=out, in0=in0, scalar=scalar, in1=in1, op0=op0, op1=op1),
                 reads=R, writes=W)

        def mm(ps, pairs, R, start=True, stop=True):
            def f(h):
                n = len(pairs)
                ins = None
                for i, (l, r) in enumerate(pairs):
                    ins = h.matmul(ps.t[:, :] if not isinstance(ps, tuple) else ps[1], lhsT=l, rhs=r,
                                   start=(start and i == 0), stop=(stop and i == n - 1))
                return ins
            b = ps.b if not isinstance(ps, tuple) else ps[0].b
            S.pe_log.append((S.phase, len(pairs)))
            S.op("pe", f, reads=R, writes=[b])

        def dma_load(eng, dst_buf, pairs):
            def f(h):
                return [h.dma_start(out=o, in_=i) for (o, i) in pairs]
            S.op(eng, f, writes=[dst_buf], dma_buf=dst_buf, n_dma=len(pairs))

        cz, c1, ceps, cq, chalf = (consts[:, 0:1], consts[:, 1:2], consts[:, 2:3], consts[:, 3:4], consts[:, 4:5])

        for i, v in enumerate([0.0, 1.0, EPS, 0.25, 0.5]):
            S.op("dve", lambda h, i=i, v=v: h.memset(consts[:, i:i + 1], v), writes=[b_const])
        S.op("dve", lambda h: h.memset(onesb[:, :], 1.0), writes=[b_ident])
        dma_load("sp", b_identf, [(identf[:, :], ident_d)])
        S.op("dve", lambda h: h.tensor_copy(out=ident[:, :], in_=identf[:, :]), reads=[b_identf], writes=[b_ident])
        dma_load("sp", b_pp, [(pp[:, l, :], pp_d[l]) for l in range(DEPTH)])
        dma_load("sp", b_pg, [(pg[:, :], pg_d)])
        dma_load("pool", b_cs, [(cs128[:, :], cs128_d)])
        dma_load("pool", b_d256, [(d256[:, :, :], dft256_d.rearrange("(t p) n -> p t n", p=128))])
        for i in range(NPAD):
            S.op("pool", lambda h, i=i: h.memset(pads[i][:, :], 0.0), writes=[pad_pool.free_list[i].b])

        def pcol(l, name, a=0, n=None):
            o, c = PL[name]
            if n is None:
                n = c - a
            return pp[:, l, o + a:o + a + n]

        def dcol(l, name, a=0, n=None):
            o, c = DL[name]
            if n is None:
                n = c - a
            return dpl[:, l, o + a:o + a + n]

        for l in range(DEPTH):
            ts(dcol(l, "hba"), pcol(l, "lru_ba"), 0.5, None, ALU.mult, R=[b_pp], W=[b_dpl])
            ts(dcol(l, "hbx"), pcol(l, "lru_bx"), 0.5, None, ALU.mult, R=[b_pp], W=[b_dpl])
            ts(dcol(l, "hbg"), pcol(l, "b_gate"), 0.5, None, ALU.mult, R=[b_pp], W=[b_dpl])
            ts(dcol(l, "cwh"), pcol(l, "conf_dw_w"), 0.5, None, ALU.mult, R=[b_pp], W=[b_dpl])
            e_ = small[:, 0:16]
            p_ = small[:, 16:32]
            act(e_, pcol(l, "lru_lam"), AF.Exp, scale=-1.0, R=[b_pp], W=[b_small])
            ts(p_, e_, -0.2, 0.25, ALU.mult, ALU.add, R=[b_small], W=[b_small])
            for cst in (1.0 / 3.0, 0.5, 1.0):
                tt(p_, p_, e_, ALU.mult, R=[b_small], W=[b_small])
                ts(p_, p_, -1.0, cst, ALU.mult, ALU.add, R=[b_small], W=[b_small])
            tt(p_, p_, e_, ALU.mult, R=[b_small], W=[b_small])
            ts(dcol(l, "hkk"), p_, -4.0, None, ALU.mult, R=[b_small], W=[b_dpl])

        for ci in range(2):
            act(scond[:, :, ci], pg[:, 8 + 8 * ci:16 + 8 * ci], AF.Silu, R=[b_pg], W=[b_scond])

        wq = []
        wstate = {"next_issue": 0, "next_get": 0, "loaded": {}}

        def w_issue():
            while wstate["next_issue"] < len(wq) and ws_pool.free_list and \
                    wstate["next_issue"] - wstate["next_get"] < NWS:
                i = wstate["next_issue"]
                slot = ws_pool.alloc()
                name, pf = wq[i]
                dma_load("pool", slot.b, pf(slot.t))
                wstate["loaded"][i] = slot
                wstate["next_issue"] += 1

        def w_get(name):
            recorded.append(name)
            if order is None:
                slot = ws_pool.alloc()
                dma_load("pool", slot.b, wq_defs[name](slot.t))
                wstate["next_get"] += 1
                return slot
            i = wstate["next_get"]
            assert wq[i][0] == name, (wq[i][0], name)
            if i not in wstate["loaded"]:
                w_issue()
            assert i in wstate["loaded"], "no free weight slot for " + name
            wstate["next_get"] += 1
            slot = wstate["loaded"].pop(i)
            w_issue()
            return slot

        def w_free(slot):
            ws_pool.free(slot)
            if order is not None:
                w_issue()

        def kview(w2d):
            return w2d.rearrange("(k p) n -> p k n", p=128)

        def q_std(name, w2d, c0, ncols, nk=8, k0=0):
            def pf(t, w2d=w2d):
                return [(t[:, 0:nk, 0:ncols], kview(w2d)[:, k0:k0 + nk, c0:c0 + ncols])]
            wq.append((name, pf))

        def q_wide(name, w2d):
            def pf(t, w2d=w2d):
                tv = t[:, :, :].rearrange("p a b -> p (a b)").rearrange("p (k n) -> p k n", k=4)
                return [(tv, kview(w2d))]
            wq.append((name, pf))

        def build_wq(l, kind):
            tag = "%s%d_" % (kind, l)
            if kind == "P" and l == 0:
                for g in range(12):
                    q_std("mod0_%d" % g, w_mod[0], g * 512, 512)
            for g in (2, 3, 0, 1):
                q_std(tag + "in%d" % g, w_in[l], g * 512, 512)

            def pf_bd(t, l=l):
                tv = t[:, :, :].rearrange("p a b -> p (a b)").rearrange("p (k n) -> p k n", k=32)
                return [(tv[:, 0:16, :], lru_wa[l].rearrange("d h i j -> i (d h) j")),
                        (tv[:, 16:32, :], lru_wx[l].rearrange("d h i j -> i (d h) j"))]
            wq.insert(len(wq) - 2, (tag + "bd", pf_bd))
            q_std(tag + "in4", w_in[l], 4 * 512, 512)
            if kind == "S":
                q_std(tag + "dfc", dftc_d, 0, 512)
                q_std(tag + "dfs", dfts_d, 0, 512)
            for hh in range(2):
                q_std(tag + "lruout%d" % hh, lru_out[l], hh * 512, 512)
                q_std(tag + "in%d" % (10 + hh), w_in[l], (10 + hh) * 512, 512)
            q_wide(tag + "fourier_out", fourier_out[l])
            for hh in range(2):
                q_std(tag + "in%d" % (12 + hh), w_in[l], (12 + hh) * 512, 512)
            for g in (6, 5):
                q_std(tag + "in%d" % g, w_in[l], g * 512, 512)
            for g in (9, 7):
                q_std(tag + "in%d" % g, w_in[l], g * 512, 512)
            q_wide(tag + "conf_out", conf_out[l])
            for hh in range(2):
                q_std(tag + "in%d" % (14 + hh), w_in[l], (14 + hh) * 512, 512)
            q_std(tag + "in8", w_in[l], 8 * 512, 512)
            q_wide(tag + "sc_out", sc_out[l])
            for hh in range(2):
                q_std(tag + "in%d" % (16 + hh), w_in[l], (16 + hh) * 512, 512)
            for hh in range(2):
                q_std(tag + "wo%d" % hh, w_o[l], hh * 512, 512)
            for q in range(11):
                def pf_up(t, l=l, q=q):
                    v = kview(ffn_up[l])
                    return [(t[:, :, 0:256], v[:, :, 256 * q:256 * q + 256]),
                            (t[:, :, 256:512], v[:, :, D_FF + 256 * q:D_FF + 256 * q + 256])]
                wq.append((tag + "up%d" % q, pf_up))
                if kind == "P" and l == 0:
                    q_std("mod1_%d" % q, w_mod[1], q * 512, 512)
            if kind == "P" and l == 0:
                q_std("mod1_11", w_mod[1], 11 * 512, 512)
            for hh in range(2):
                for kg in range(3):
                    nk = 8 if kg < 2 else 6
                    q_std(tag + "down%d_%d" % (hh, kg), ffn_down[l], hh * 512, 512, nk=nk, k0=8 * kg)

        for kind in ("P", "S"):
            for l in range(DEPTH):
                build_wq(l, kind)
        wq_defs = dict(wq)
        recorded = []
        if order is not None:
            wq[:] = [(n_, wq_defs[n_]) for n_ in order]
            assert len(wq) == len(wq_defs)

        def mod_steps(l):
            tag = "mod%d_" % l

            def finish(row, g):
                ps2 = ps_pool.alloc()

                def f(h, row=row, ps2=ps2):
                    ins = None
                    for j in range(4):
                        ins = h.matmul(ps2.t[:, 2 * j:2 * j + 2], lhsT=row.t[0:2, j * 128:(j + 1) * 128],
                                       rhs=identf[0:2, 0:2], start=True, stop=True)
                    return ins
                S.pe_log.append((S.phase, 4))
                S.op("pe", f, reads=[row.b, b_identf], writes=[ps2.b])
                tt(modTall[:, l, :, 4 * g:4 * g + 4], ps2.t[:, 0:8].rearrange("p (j n) -> p n j", n=2),
                   pcol(l, "b_mod", 4 * g, 4).unsqueeze(1).to_broadcast([128, 2, 4]), ALU.add,
                   R=[ps2.b, b_pp], W=[b_modl[l]])
                ps_pool.free(ps2)
                tmp_pool.free(row)

            prev = None
            for g in range(12):
                slot = w_get(tag + "%d" % g)
                ps = ps_pool.alloc()
                mm((ps, ps.t[0:2, :]), [(scond[:, k, :], slot.t[:, k, :]) for k in range(8)], R=[slot.b, b_scond])
                w_free(slot)
                row = tmp_pool.alloc()
                act(row.t[0:2, :], ps.t[0:2, :], AF.Copy, R=[ps.b], W=[row.b])
                ps_pool.free(ps)
                if prev is not None:
                    finish(*prev)
                prev = (row, g)
                if g == 11:
                    finish(*prev)
                yield g

        def run_pass(kind):
            P = (kind == "P")
            NT = 1 if P else 2
            NTOK = 512 * NT
            L = 256 if P else 1024
            nseq = 2 if P else 1
            xsrc = xP if P else xS

            def v(ap):
                return ap.rearrange("p (s l) -> p s l", s=2) if P else ap

            def padv(pad_t, pl, pr, t, k):
                W = pl + L + pr
                if P:
                    return pad_t[:, 0:2 * W].rearrange("p (s w) -> p s w", s=2)[:, :, k:k + 256]
                return pad_t[:, t * 512 + k:t * 512 + k + 512]

            def v2(ap):
                if P:
                    return ap.rearrange("p (s l) -> p s l", s=2)
                return ap.rearrange("p (r c) -> p r c", c=64)

            def padv2(pad_t, t, kr, kc):
                if P:
                    return pad_t[:, 0:2 * 258].rearrange("p (s w) -> p s w", s=2)[:, :, kc:kc + 256]
                return pad_t[:, 0:18 * 66].rearrange("p (r c) -> p r c", c=66)[:, 8 * t + kr:8 * t + kr + 8, kc:kc + 64]

            ffn_taps = [(1, kc) for kc in range(3)] if P else [(kr, kc) for kr in range(3) for kc in range(3)]

            def tsl(t):
                return slice(t * 512, (t + 1) * 512)

            for c in range(8):
                for t in range(NT):
                    if (not P) and t == 1:
                        continue
                    dma_load("sp", b_xT[c][t], [(xT[:, c, tsl(t)], xsrc[c * 128:(c + 1) * 128, tsl(t)])])
            if P:
                for c in range(8):
                    dma_load("sp", b_xT[c][1], [(xT[:, c, tsl(1)], xS[c * 128:(c + 1) * 128, tsl(1)])])

            def zmm(ps, wslot, jj, t):
                mm(ps, [(wslot.t[:, k, jj * 128:(jj + 1) * 128], hT[:, k, tsl(t)]) for k in range(8)],
                   R=[wslot.b] + [b_hT[k][t] for k in range(8)])

            def build_diag(dg, wap, ntaps):
                S.op("pool", lambda h: h.tensor_tensor(
                    out=dg.t[:, 0:ntaps, :],
                    in0=ident[:, :].unsqueeze(1).to_broadcast([128, ntaps, 128]),
                    in1=wap.unsqueeze(2).to_broadcast([128, ntaps, 128]), op=ALU.mult),
                    reads=[b_ident, b_pp, b_dpl], writes=[dg.b])

            def stats_begin():
                return {"ps": [ps_pool.alloc() for _ in range(NT)], "n": [0] * NT}

            def stats_add(st, c, t):
                sq = tmp_pool.alloc()
                sqb = sq.t[:, :].bitcast(BF16)[:, 0:512]
                act(sqb, xT[:, c, tsl(t)], AF.Square, R=[b_xT[c][t]], W=[sq.b])
                mm(st["ps"][t], [(onesb[:, :], sqb)], R=[b_ident, sq.b], start=(st["n"][t] == 0), stop=(st["n"][t] == 7))
                st["n"][t] += 1
                tmp_pool.free(sq)

            def stats_add_delayed(st, c, t, lag=3):
                q_ = st.setdefault("q", [])
                q_.append((c, t))
                while len(q_) > lag:
                    stats_add(st, *q_.pop(0))

            def stats_flush(st):
                for ct in st.pop("q", []):
                    stats_add(st, *ct)

            def rmsnorm_to(l, gcol, shcol, dst_fn, dst_bufs, dst_buf_fn=None, torder=None, st=None, per_tile_cb=None):
                order_ = list(torder) if torder is not None else list(range(NT))
                if st is None:
                    st = stats_begin()
                    for t in order_:
                        for c in range(8):
                            stats_add(st, c, t)
                else:
                    stats_flush(st)
                assert all(n_ == 8 for n_ in st["n"])
                rss = {}
                for t in order_:
                    ps = st["ps"][t]
                    rs = tmp_pool.alloc()
                    act(rs.t[:, :], ps.t[:, :], AF.Ln, bias=ceps, scale=1.0 / D, R=[ps.b, b_const], W=[rs.b])
                    ps_pool.free(ps)
                    act(rs.t[:, :], rs.t[:, :], AF.Exp, scale=-0.5, R=[rs.b], W=[rs.b])
                    rss[t] = rs
                for t in order_:
                    rs = rss[t]
                    for c in range(8):
                        tm = tmp_pool.alloc()
                        tt(tm.t[:, :], xT[:, c, tsl(t)], rs.t[:, :], ALU.mult, R=[b_xT[c][t], rs.b], W=[tm.b])
                        if shcol is None:
                            act(dst_fn(c, t), tm.t[:, :], AF.Identity, bias=cz, scale=gcol(c),
                                R=[tm.b, b_pg, b_der, b_const], W=dst_buf_fn(c, t))
                        else:
                            act(dst_fn(c, t), tm.t[:, :], AF.Identity, bias=shcol(c), scale=gcol(c),
                                R=[tm.b, b_der, b_modl[l]], W=[dst_bufs[c][t]])
                        tmp_pool.free(tm)
                    tmp_pool.free(rs)
                    if per_tile_cb is not None:
                        per_tile_cb(t)

            def out_and_gate(l, tag, b, get_out, free_out, wcol, kchunks, act_ap, act_bufs):
                for hh in range(2):
                    wout = get_out(hh)
                    gslot = w_get(tag + "in%d" % (10 + 2 * b + hh))
                    wflat = wout.t[:, :, :].rearrange("p a b -> p (a b)")
                    tgs = {}
                    for t in range(NT):
                        for mq in range(4):
                            m = hh * 4 + mq
                            psg = ps_pool.alloc()
                            zmm(psg, gslot, mq, t)
                            tg = tmp_pool.alloc()
                            act(tg.t[:, :], psg.t[:, :], AF.Tanh, bias=dcol(l, "hbg", b * 8 + m, 1), scale=0.5,
                                R=[psg.b, b_dpl], W=[tg.b])
                            ps_pool.free(psg)
                            tgs[(mq, t)] = tg
                    w_free(gslot)
                    for t in range(NT):
                        for mq in range(4):
                            m = hh * 4 + mq
                            tg = tgs.pop((mq, t))
                            psy = ps_pool.alloc()
                            mm(psy, [(wflat[:, wcol(k, m):wcol(k, m) + 128], act_ap(k, t)) for k in range(kchunks)],
                               R=[wout.b] + [act_bufs[k][t] for k in range(kchunks)])
                            if b == 0:
                                stt(merged[:, m, tsl(t)], tg.t[:, :], 1.0, psy.t[:, :], ALU.add, ALU.mult,
                                    R=[tg.b, psy.b], W=[b_mg[m][t]])
                            else:
                                stt(tg.t[:, :], tg.t[:, :], 1.0, psy.t[:, :], ALU.add, ALU.mult,
                                    R=[tg.b, psy.b], W=[tg.b])
                                tt(merged[:, m, tsl(t)], merged[:, m, tsl(t)], tg.t[:, :], ALU.add,
                                   R=[tg.b, b_mg[m][t]], W=[b_mg[m][t]])
                            ps_pool.free(psy)
                            tmp_pool.free(tg)
                    free_out(hh)

            def wide_out(tag, name):
                hold = {}

                def get_out(hh):
                    if "w" not in hold:
                        hold["w"] = w_get(tag + name)
                    return hold["w"]

                def free_out(hh):
                    if hh == 1:
                        w_free(hold.pop("w"))
                return get_out, free_out, (lambda k, m: k * 1024 + m * 128)

            def layer(l):
                tag = "%s%d_" % (kind, l)
                if P and l == 0:
                    S.phase = "P0:mod"
                    for _ in mod_steps(0):
                        pass
                modT = modTall[:, l, 0 if P else 1, :]
                mgen = mod_steps(1) if (P and l == 0) else iter(())
                stt(der[:, 0:8], modT[:, 8:16], 1.0, pcol(l, "norm1_g"), ALU.add, ALU.mult, R=[b_modl[l], b_pp], W=[b_der])
                stt(der[:, 8:16], modT[:, 32:40], 1.0, pcol(l, "norm2_g"), ALU.add, ALU.mult, R=[b_modl[l], b_pp], W=[b_der])
                ts(der[:, 16:24], modT[:, 16:24], 0.5, None, ALU.mult, R=[b_modl[l]], W=[b_der])

                S.phase = kind + str(l) + ":norm1"
                rmsnorm_to(l, lambda c: der[:, c:c + 1], lambda c: modT[:, c:c + 1],
                           lambda c, t: hT[:, c, tsl(t)], b_hT,
                           torder=((1, 0) if (not P and l == 0) else None), st=pend["st"])
                pend["st"] = None

                S.phase = kind + str(l) + ":geluzg"
                for g in (2, 3):
                    slot = w_get(tag + "in%d" % g)
                    for jj in range(4):
                        c = (g - 2) * 4 + jj
                        for t in range(NT):
                            ps = ps_pool.alloc()
                            zmm(ps, slot, jj, t)
                            act(act1[:, c, tsl(t)], ps.t[:, :], AF.Gelu, R=[ps.b], W=[b_a1[c][t]])
                            ps_pool.free(ps)
                    w_free(slot)
                def fft_gen():
                    ph_ = [None]

                    def enter():
                        ph_[0] = S.phase
                        S.phase = kind + str(l) + ":fft"

                    def leave():
                        S.phase = ph_[0]
                    enter()
                    slot = w_get(tag + "in4")
                    zfs = []
                    for g in range(4):
                        zf = tmp_pool.alloc()
                        zfb = zf.t[:, :].bitcast(BF16)
                        for t in range(NT):
                            ps = ps_pool.alloc()
                            zmm(ps, slot, g, t)
                            act(zfb[:, tsl(t)], ps.t[:, :], AF.Copy, R=[ps.b], W=[zf.b])
                            ps_pool.free(ps)
                        zfs.append(zf)
                    w_free(slot)
                    leave()
                    yield
                    nlt = NTOK // 128
                    st_ = {}

                    def stage1(g):
                        zf = zfs[g]
                        zfb = zf.t[:, :].bitcast(BF16)
                        ut = ut_pool.alloc()
                        for lp in range(nlt // 2):
                            ps = ps_pool.alloc()

                            def f(h, ps=ps, lp=lp, zfb=zfb):
                                ins = None
                                for q in range(2):
                                    lt = 2 * lp + q
                                    if P:
                                        lh = zfb[:, lt * 128:(lt + 1) * 128]
                                    else:
                                        par_, blk_ = divmod(lt, 4)
                                        lh = zfb[:, 256 * blk_ + par_:256 * blk_ + 256:2]
                                    ins = h.matmul(ps.t[:, q * 256:(q + 1) * 256], lhsT=lh,
                                                   rhs=cs128[:, :], start=True, stop=True)
                                return ins
                            S.pe_log.append((S.phase, 2))
                            S.op("pe", f, reads=[zf.b, b_cs], writes=[ps.b])
                            act(ut.t[:, 2 * lp:2 * lp + 2, :], ps.t[:, :].rearrange("p (a b) -> p a b", a=2), AF.Copy,
                                R=[ps.b], W=[ut.b])
                            ps_pool.free(ps)
                        st_["ut"] = ut

                    def stage2(k):
                        ut = st_.pop("ut")
                        if P:
                            g = k
                            ps = ps_pool.alloc()
                            for sq in range(2):
                                pairs = []
                                for lt in range(2):
                                    pairs.append((ut.t[:, 2 * sq + lt, 0:128], d256[:, lt, 0:256]))
                                    pairs.append((ut.t[:, 2 * sq + lt, 128:256], d256[:, lt, 256:512]))
                                mm((ps, ps.t[:, sq * 256:(sq + 1) * 256]), pairs, R=[ut.b, b_d256])
                            act(act2[:, g, 0:512], ps.t[:, :], AF.Copy, R=[ps.b], W=[b_a2[g][0]])
                            ps_pool.free(ps)
                        else:
                            g = k
                            if g == 0:
                                st_["dc"] = w_get(tag + "dfc")
                                st_["ds"] = w_get(tag + "dfs")
                            dc, ds = st_["dc"], st_["ds"]
                            psE = ps_pool.alloc()
                            psO = ps_pool.alloc()
                            for ps_, base_ in ((psE, 0), (psO, 4)):
                                pairs = []
                                for lt in range(base_, base_ + 4):
                                    pairs.append((ut.t[:, lt, 0:128], dc.t[:, lt, :]))
                                    pairs.append((ut.t[:, lt, 128:256], ds.t[:, lt, :]))
                                mm(ps_, pairs, R=[ut.b, dc.b, ds.b])
                            et = tmp_pool.alloc()
                            ot = tmp_pool.alloc()
                            act(et.t[:, :], psE.t[:, :], AF.Copy, R=[psE.b], W=[et.b])
                            ps_pool.free(psE)
                            act(ot.t[:, :], psO.t[:, :], AF.Copy, R=[psO.b], W=[ot.b])
                            ps_pool.free(psO)
                            tt(act2[:, g, tsl(0)], et.t[:, :], ot.t[:, :], ALU.add, R=[et.b, ot.b], W=[b_a2[g][0]], eng="pool")
                            tt(act2[:, g, tsl(1)], et.t[:, :], ot.t[:, :], ALU.subtract, R=[et.b, ot.b], W=[b_a2[g][1]], eng="pool")
                            tmp_pool.free(et)
                            tmp_pool.free(ot)
                            if g == 3:
                                w_free(st_.pop("dc"))
                                w_free(st_.pop("ds"))
                        ut_pool.free(ut)

                    nk = 4
                    if P:
                        for i in range(1, nk + 2):
                            enter()
                            if 0 <= i - 2 < nk:
                                stage2(i - 2)
                            if i - 1 < nk:
                                stage1((i - 1) % 4)
                            leave()
                            yield
                    else:
                        for g in range(4):
                            enter()
                            stage1(g)
                            leave()
                            yield
                            enter()
                            stage2(g)
                            leave()
                            yield
                    for g in range(4):
                        tmp_pool.free(zfs[g])

                S.phase = kind + str(l) + ":lru"
                fg = fft_gen()
                alias_import()
                tmp_pool.free_list.extend(lru_extra)
                bpads = [pad_pool.alloc() for _ in range(3)]
                bslots = [Slot(p_.b, p_.t[:, 0:1024].bitcast(F32), "bp") for p_ in bpads]
                tmp_pool.free_list.extend(bslots)
                bd = w_get(tag + "bd")
                bdv = bd.t[:, :, :].rearrange("p a b -> p (a b)").rearrange("p (k n) -> p k n", k=32)
                def lru_front(c, slot, jj):
                    pad = pad_pool.alloc()
                    S.op("pool", lambda h, pad=pad: h.memset(pad.t[:, :], 0.0), writes=[pad.b])
                    dg = dg_pool.alloc()
                    build_diag(dg, pcol(l, "lru_conv_w")[:, c::8], 4)
                    for t in range(NT):
                        ps = ps_pool.alloc()
                        zmm(ps, slot, jj, t)
                        act(padv(pad.t, 2, 1, t, 2), v(ps.t[:, :]), AF.Copy, R=[ps.b], W=[pad.b])
                        ps_pool.free(ps)
                    return (c, pad, dg)

                def lru_front_b(sa_):
                    c, pad, dg = sa_
                    xc32 = []
                    xcbs = tmp_pool.alloc()
                    xcbv = xcbs.t[:, :].bitcast(BF16)
                    for t in range(NT):
                        ps = ps_pool.alloc()
                        mm((ps, v(ps.t[:, :])), [(dg.t[:, k, :], padv(pad.t, 2, 1, t, k)) for k in range(4)],
                           R=[dg.b, pad.b])
                        x32 = tmp_pool.alloc()
                        ts(x32.t[:, :], ps.t[:, :], pcol(l, "lru_conv_b", c, 1), None, ALU.add,
                           R=[ps.b, b_pp], W=[x32.b])
                        S.op("dve", lambda h, x32=x32, t=t: h.tensor_copy(out=xcbv[:, tsl(t)], in_=x32.t[:, :]),
                             reads=[x32.b], writes=[xcbs.b])
                        ps_pool.free(ps)
                        xc32.append(x32)
                    pad_pool.free(pad)
                    dg_pool.free(dg)
                    return (c, xc32, xcbs, xcbv)

                def lru_back(state_):
                    c, xc32, xcbs, xcbv = state_
                    A_, S_, I_ = {}, {}, {}
                    for d in (0, 1):
                        for t in range(NT):
                            xbv = xcbv[:, tsl(t)]
                            psa = ps_pool.alloc()
                            mm(psa, [(bdv[:, d * 8 + c, :], xbv)], R=[bd.b, xcbs.b])
                            psx = ps_pool.alloc()
                            mm(psx, [(bdv[:, 16 + d * 8 + c, :], xbv)], R=[bd.b, xcbs.b])
                            a_ = tmp_pool.alloc()
                            act(a_.t[:, :], psa.t[:, :], AF.Tanh, bias=dcol(l, "hba", d * 8 + c, 1), scale=0.5,
                                R=[psa.b, b_dpl], W=[a_.b])
                            ps_pool.free(psa)
                            act(a_.t[:, :], a_.t[:, :], AF.Exp, bias=dcol(l, "hkk", d * 8 + c, 1),
                                scale=dcol(l, "hkk", d * 8 + c, 1), R=[a_.b, b_dpl], W=[a_.b])
                            s_ = tmp_pool.alloc()
                            tt(s_.t[:, :], a_.t[:, :], a_.t[:, :], ALU.mult, R=[a_.b], W=[s_.b], eng="pool")
                            i_ = tmp_pool.alloc()
                            act(i_.t[:, :], psx.t[:, :], AF.Tanh, bias=dcol(l, "hbx", d * 8 + c, 1), scale=0.5,
                                R=[psx.b, b_dpl], W=[i_.b])
                            ps_pool.free(psx)
                            A_[(d, t)], S_[(d, t)], I_[(d, t)] = a_, s_, i_
                    for d in (0, 1):
                        for t in range(NT):
                            s_ = S_[(d, t)]
                            act(s_.t[:, :], s_.t[:, :], AF.Sqrt, bias=cq, scale=-0.25, R=[s_.b, b_const], W=[s_.b])
                    hf = [None] * NT
                    hb = []
                    for d in (0, 1):
                        order = list(range(NT)) if d == 0 else list(range(NT - 1, -1, -1))
                        prev_h = None
                        for t in order:
                            a_, s_, i_ = A_[(d, t)], S_[(d, t)], I_[(d, t)]
                            stt(i_.t[:, :], i_.t[:, :], 1.0, xc32[t].t[:, :], ALU.add, ALU.mult,
                                R=[i_.b, xc32[t].b], W=[i_.b])
                            tt(i_.t[:, :], i_.t[:, :], s_.t[:, :], ALU.mult, R=[i_.b, s_.b], W=[i_.b])
                            tmp_pool.free(s_)
                            h_ = tmp_pool.alloc()
                            for sq in range(nseq):
                                lo, hi = (sq * 256, (sq + 1) * 256) if P else (0, 512)
                                if P:
                                    init = 0.0
                                    rd = []
                                elif prev_h is None:
                                    init = pg[:, 24 + (l * 2 + d) * 8 + c:24 + (l * 2 + d) * 8 + c + 1]
                                    rd = [b_pg]
                                else:
                                    init = prev_h.t[:, 511:512] if d == 0 else prev_h.t[:, 0:1]
                                    rd = [prev_h.b]
                                if d == 0:
                                    o_, a0_, a1_ = h_.t[:, lo:hi], a_.t[:, lo:hi], i_.t[:, lo:hi]
                                else:
                                    o_, a0_, a1_ = (h_.t[:, lo:hi][:, ::-1], a_.t[:, lo:hi][:, ::-1],
                                                    i_.t[:, lo:hi][:, ::-1])
                                S.op("dve", lambda h, o_=o_, a0_=a0_, a1_=a1_, init=init: h.tensor_tensor_scan(
                                    out=o_, data0=a0_, data1=a1_, initial=init, op0=ALU.mult, op1=ALU.add),
                                    reads=[a_.b, i_.b] + rd, writes=[h_.b])
                                if P:
                                    col = ((l * 2 + d) * 8 + c) * 2 + sq
                                    src = h_.t[:, hi - 1:hi] if d == 0 else h_.t[:, lo:lo + 1]
                                    S.op("dve", lambda h, col=col, src=src: h.tensor_copy(out=stT[:, col:col + 1], in_=src),
                                         reads=[h_.b], writes=[b_st])
                            tmp_pool.free(a_)
                            tmp_pool.free(i_)
                            if d == 0:
                                hf[t] = h_
                            else:
                                hb.append(h_)
                                tt(hf[t].t[:, :], hf[t].t[:, :], h_.t[:, :], ALU.add, R=[h_.b, hf[t].b], W=[hf[t].b])
                                tt(act1[:, c, tsl(t)], act1[:, c, tsl(t)], hf[t].t[:, :], ALU.mult,
                                   R=[hf[t].b, b_a1[c][t]], W=[b_a1[c][t]])
                            prev_h = h_
                    for t in range(NT):
                        tmp_pool.free(xc32[t])
                        tmp_pool.free(hf[t])
                    tmp_pool.free(xcbs)
                    for h__ in hb:
                        tmp_pool.free(h__)

                slots_ = {}

                def front_a(c):
                    g_, jj_ = divmod(c, 4)
                    if jj_ == 0:
                        slots_[g_] = w_get(tag + "in%d" % g_)
                    sa_ = lru_front(c, slots_[g_], jj_)
                    if jj_ == 3:
                        w_free(slots_.pop(g_))
                    return sa_
                sa = {0: front_a(0)}
                sb = {0: lru_front_b(sa.pop(0))}
                sa[1] = front_a(1)
                for c in range(8):
                    if c + 1 < 8:
                        sb[c + 1] = lru_front_b(sa.pop(c + 1))
                    lru_back(sb.pop(c))
                    if c + 2 < 8:
                        sa[c + 2] = front_a(c + 2)
                    next(fg, None)
                w_free(bd)
                for _ in fg:
                    pass
                for x_ in lru_extra + bslots:
                    tmp_pool.free_list.remove(x_)
                for p_ in bpads:
                    pad_pool.free(p_)
                alias_export()

                S.phase = kind + str(l) + ":outA"
                holder = {}

                def get_A(hh):
                    holder[hh] = w_get(tag + "lruout%d" % hh)
                    return holder[hh]

                def free_A(hh):
                    w_free(holder.pop(hh))
                out_and_gate(l, tag, 0, get_A, free_A, (lambda k, m: k * 512 + (m % 4) * 128), 8,
                             lambda k, t: act1[:, k, tsl(t)], b_a1)

                next(mgen, None)
                S.phase = kind + str(l) + ":outB"
                go, fo, wc = wide_out(tag, "fourier_out")
                out_and_gate(l, tag, 1, go, fo, wc, 4, lambda k, t: act2[:, k, tsl(t)], b_a2)

                next(mgen, None)
                S.phase = kind + str(l) + ":conf"
                cpads = [pad_pool.alloc() for _ in range(4)]
                slot = w_get(tag + "in6")
                for c in range(4):
                    S.op("pool", lambda h, c=c: h.memset(cpads[c].t[:, :], 0.0), writes=[cpads[c].b])
                    for t in range(NT):
                        ps = ps_pool.alloc()
                        zmm(ps, slot, c, t)
                        act(padv(cpads[c].t, 15, 15, t, 15), v(ps.t[:, :]), AF.Tanh, scale=0.5, R=[ps.b], W=[cpads[c].b])
                        ps_pool.free(ps)
                w_free(slot)
                slot = w_get(tag + "in5")
                for c in range(4):
                    for t in range(NT):
                        ps = ps_pool.alloc()
                        zmm(ps, slot, c, t)
                        stt(padv(cpads[c].t, 15, 15, t, 15), padv(cpads[c].t, 15, 15, t, 15), 1.0, v(ps.t[:, :]),
                            ALU.add, ALU.mult, R=[ps.b, cpads[c].b], W=[cpads[c].b])
                        ps_pool.free(ps)
                w_free(slot)
                b_uc = [[r1f_bufs(4 + c, t) for t in range(NT)] for c in range(4)]
                for c in range(4):
                    pss = [ps_pool.alloc() for _ in range(NT)]
                    groups = list(range(0, 31, 9))
                    for gi, k0 in enumerate(groups):
                        n_ = min(9, 31 - k0)
                        dg = dg_pool.alloc()
                        build_diag(dg, dcol(l, "cwh")[:, c::4][:, k0:k0 + n_], n_)
                        for t in range(NT):
                            mm((pss[t], v(pss[t].t[:, :])),
                               [(dg.t[:, k, :], padv(cpads[c].t, 15, 15, t, k0 + k)) for k in range(n_)],
                               R=[cpads[c].b, dg.b], start=(gi == 0), stop=(gi == len(groups) - 1))
                        dg_pool.free(dg)
                    for t in range(NT):
                        act(act1f[:, c, tsl(t)], pss[t].t[:, :], AF.Identity, bias=pcol(l, "conf_dw_b", c, 1),
                            R=[pss[t].b, b_pp], W=b_uc[c][t])
                        ps_pool.free(pss[t])
                for c in range(4):
                    pad_pool.free(cpads[c])
                def ln_gen():
                    ph0 = S.phase
                    for t in range(NT):
                        psm = ps_pool.alloc()
                        psq = ps_pool.alloc()
                        ub = []
                        for c in range(4):
                            u_ = tmp_pool.alloc()
                            uv = u_.t[:, :].bitcast(BF16)
                            act(uv[:, 0:512], act1f[:, c, tsl(t)], AF.Copy, R=b_uc[c][t], W=[u_.b])
                            act(uv[:, 512:1024], act1f[:, c, tsl(t)], AF.Square, R=b_uc[c][t], W=[u_.b])
                            ub.append(u_)
                        mm(psm, [(onesb[:, :], ub[c].t[:, :].bitcast(BF16)[:, 0:512]) for c in range(4)],
                           R=[b_ident] + [u_.b for u_ in ub])
                        mm(psq, [(onesb[:, :], ub[c].t[:, :].bitcast(BF16)[:, 512:1024]) for c in range(4)],
                           R=[b_ident] + [u_.b for u_ in ub])
                        for u_ in ub:
                            tmp_pool.free(u_)
                        yield
                        mean = tmp_pool.alloc()
                        act(mean.t[:, :], psm.t[:, :], AF.Copy, scale=1.0 / 512, R=[psm.b], W=[mean.b])
                        ps_pool.free(psm)
                        var = tmp_pool.alloc()
                        tt(var.t[:, :], mean.t[:, :], mean.t[:, :], ALU.mult, R=[mean.b], W=[var.b])
                        stt(var.t[:, :], psq.t[:, :], 1.0 / 512, var.t[:, :], ALU.mult, ALU.subtract,
                            R=[psq.b, var.b], W=[var.b])
                        ps_pool.free(psq)
                        yield
                        act(var.t[:, :], var.t[:, :], AF.Ln, bias=ceps, scale=1.0, R=[var.b, b_const], W=[var.b])
                        act(var.t[:, :], var.t[:, :], AF.Exp, scale=-0.5, R=[var.b], W=[var.b])
                        yield
                        for c in range(4):
                            d_ = tmp_pool.alloc()
                            tt(d_.t[:, :], act1f[:, c, tsl(t)], mean.t[:, :], ALU.subtract, R=b_uc[c][t] + [mean.b], W=[d_.b])
                            tt(d_.t[:, :], d_.t[:, :], var.t[:, :], ALU.mult, R=[d_.b, var.b], W=[d_.b])
                            act(act2[:, c, tsl(t)], d_.t[:, :], AF.Silu, bias=pcol(l, "conf_ln_b", c, 1),
                                scale=pcol(l, "conf_ln_g", c, 1), R=[d_.b, b_pp], W=[b_a2[c][t]])
                            tmp_pool.free(d_)
                            if c % 2 == 1:
                                yield
                        tmp_pool.free(mean)
                        tmp_pool.free(var)

                S.phase = kind + str(l) + ":scfront"
                lg = ln_gen()
                next(lg, None)
                spads = [pad_pool.alloc() for _ in range(4)]
                slot = w_get(tag + "in9")
                for c in range(4):
                    S.op("pool", lambda h, c=c: h.memset(spads[c].t[:, :], 0.0), writes=[spads[c].b])
                    for t in range(NT):
                        ps = ps_pool.alloc()
                        zmm(ps, slot, c, t)
                        act(padv(spads[c].t, 1, 1, t, 1), v(ps.t[:, :]), AF.Copy, R=[ps.b], W=[spads[c].b])
                        ps_pool.free(ps)
                        next(lg, None)
                w_free(slot)
                slot = w_get(tag + "in7")
                for c in range(4):
                    for t in range(NT):
                        ps = ps_pool.alloc()
                        zmm(ps, slot, c, t)
                        tt(padv(spads[c].t, 1, 1, t, 1), padv(spads[c].t, 1, 1, t, 1), v(ps.t[:, :]), ALU.mult,
                           R=[ps.b, spads[c].b], W=[spads[c].b])
                        ps_pool.free(ps)
                        next(lg, None)
                w_free(slot)
                for _ in lg:
                    pass
                next(mgen, None)
                S.phase = kind + str(l) + ":outC"
                go, fo, wc = wide_out(tag, "conf_out")
                out_and_gate(l, tag, 2, go, fo, wc, 4, lambda k, t: act2[:, k, tsl(t)], b_a2)

                next(mgen, None)
                S.phase = kind + str(l) + ":sc"
                for c in range(4):
                    dg = dg_pool.alloc()
                    build_diag(dg, pcol(l, "sc_conv_w")[:, c::4], 3)
                    for t in range(NT):
                        ps = ps_pool.alloc()
                        mm((ps, v(ps.t[:, :])), [(dg.t[:, k, :], padv(spads[c].t, 1, 1, t, k)) for k in range(3)],
                           R=[dg.b, spads[c].b])
                        act(act1f[:, c, tsl(t)], ps.t[:, :], AF.Identity, bias=pcol(l, "sc_conv_b", c, 1),
                            R=[ps.b, b_pp], W=b_uc[c][t])
                        ps_pool.free(ps)
                    dg_pool.free(dg)
                    pad_pool.free(spads[c])
                slot = w_get(tag + "in8")
                for c in range(4):
                    for t in range(NT):
                        ps = ps_pool.alloc()
                        zmm(ps, slot, c, t)
                        tt(act2[:, c, tsl(t)], act1f[:, c, tsl(t)], ps.t[:, :], ALU.mult,
                           R=[ps.b] + b_uc[c][t], W=[b_a2[c][t]])
                        ps_pool.free(ps)
                w_free(slot)
                next(mgen, None)
                S.phase = kind + str(l) + ":outD"
                go, fo, wc = wide_out(tag, "sc_out")
                out_and_gate(l, tag, 3, go, fo, wc, 4, lambda k, t: act2[:, k, tsl(t)], b_a2)

                next(mgen, None)
                S.phase = kind + str(l) + ":wo"
                st2 = stats_begin()
                for hh in range(2):
                    slot = w_get(tag + "wo%d" % hh)
                    for t in range(NT):
                        for mq in range(4):
                            m = hh * 4 + mq
                            ps = ps_pool.alloc()
                            mm(ps, [(slot.t[:, k, mq * 128:(mq + 1) * 128], merged[:, k, tsl(t)]) for k in range(8)],
                               R=[slot.b] + [b_mg[k][t] for k in range(8)])
                            stt(xT[:, m, tsl(t)], ps.t[:, :], der[:, 16 + m:17 + m], xT[:, m, tsl(t)], ALU.mult, ALU.add,
                                R=[ps.b, b_der, b_xT[m][t]], W=[b_xT[m][t]])
                            ps_pool.free(ps)
                            stats_add_delayed(st2, m, t)
                    w_free(slot)

                S.phase = kind + str(l) + ":norm2"
                rmsnorm_to(l, lambda c: der[:, 8 + c:9 + c], lambda c: modT[:, 24 + c:25 + c],
                           lambda c, t: hT[:, c, tsl(t)], b_hT, st=st2)
                S.phase = kind + str(l) + ":ffnup"
                ntap = len(ffn_taps)
                fstate = {}

                def ffn_front(j, slot, jq):
                    pa = pad_pool.alloc()
                    pv = pad_pool.alloc()
                    S.op("pool", lambda h, pa=pa: h.memset(pa.t[:, :], 0.0), writes=[pa.b])
                    S.op("pool", lambda h, pv=pv: h.memset(pv.t[:, :], 0.0), writes=[pv.b])
                    dga = dg_pool.alloc()
                    dgv = dg_pool.alloc()
                    wa_ = pcol(l, "ffn_conv_w")[:, j::44]
                    wv_ = pcol(l, "ffn_conv_w")[:, 22 + j::44]
                    if P:
                        wa_, wv_ = wa_[:, 3:6], wv_[:, 3:6]
                    if taps_for(j)[0]:
                        build_diag(dga, wa_, ntap)
                        build_diag(dgv, wv_, ntap)
                    for t in range(NT):
                        ps = ps_pool.alloc()
                        zmm(ps, slot, jq, t)
                        act(padv2(pa.t, t, 1, 1), v2(ps.t[:, :]), AF.Copy, R=[ps.b], W=[pa.b])
                        ps_pool.free(ps)
                        ps = ps_pool.alloc()
                        zmm(ps, slot, 2 + jq, t)
                        S.op("dve", lambda h, ps=ps, pv=pv, t=t: h.tensor_copy(out=padv2(pv.t, t, 1, 1), in_=v2(ps.t[:, :])),
                             reads=[ps.b], writes=[pv.b])
                        ps_pool.free(ps)
                    fstate[j] = (pa, pv, dga, dgv)

                def taps_for(j):
                    all_pe = (j >= 20) or (P and j % 3 == 2)
                    if all_pe:
                        return list(enumerate(ffn_taps)), []
                    if P:
                        return [], [(3 + kc_, (1, kc_)) for kc_ in range(3)]
                    return ([(i, kk_) for i, kk_ in enumerate(ffn_taps) if kk_[0] < 2],
                            [(i, kk_) for i, kk_ in enumerate(ffn_taps) if kk_[0] == 2])

                def ffn_back(j):
                    pa, pv, dga, dgv = fstate.pop(j)
                    pe_taps, ve_taps = taps_for(j)
                    wcol = pcol(l, "ffn_conv_w")
                    accs = {}
                    for t in range(NT):
                        for nm, pd, joff in (("a", pa, j), ("v", pv, 22 + j)):
                            if not ve_taps:
                                continue
                            acc = tmp_pool.alloc()
                            (i0, (r0, c0)), (i1, (r1, c1)), (i2, (r2, c2)) = ve_taps
                            act(v2(acc.t[:, :]), padv2(pd.t, t, r0, c0), AF.Copy, scale=wcol[:, i0 * 44 + joff:i0 * 44 + joff + 1],
                                R=[pd.b, b_pp], W=[acc.b])
                            stt(v2(acc.t[:, :]), padv2(pd.t, t, r1, c1), wcol[:, i1 * 44 + joff:i1 * 44 + joff + 1], v2(acc.t[:, :]),
                                ALU.mult, ALU.add, R=[pd.b, b_pp, acc.b], W=[acc.b])
                            stt(v2(acc.t[:, :]), padv2(pd.t, t, r2, c2), wcol[:, i2 * 44 + joff:i2 * 44 + joff + 1], v2(acc.t[:, :]),
                                ALU.mult, ALU.add, R=[pd.b, b_pp, acc.b], W=[acc.b])
                            accs[(nm, t)] = acc
                    for t in range(NT):
                        if not pe_taps:
                            aa, av = accs.pop(("a", t)), accs.pop(("v", t))
                            ga = tmp_pool.alloc()
                            act(ga.t[:, :], aa.t[:, :], AF.Gelu, bias=pcol(l, "ffn_conv_b", j, 1), R=[aa.b, b_pp], W=[ga.b])
                            tmp_pool.free(aa)
                            stt(ffact[:, j, tsl(t)], av.t[:, :], pcol(l, "ffn_conv_b", 22 + j, 1), ga.t[:, :], ALU.add, ALU.mult,
                                R=[av.b, ga.b, b_pp], W=[b_ff[j][t]])
                            tmp_pool.free(av)
                            tmp_pool.free(ga)
                            continue
                        psa = ps_pool.alloc()
                        mm((psa, v2(psa.t[:, :])), [(dga.t[:, i, :], padv2(pa.t, t, kr, kc)) for i, (kr, kc) in pe_taps],
                           R=[dga.b, pa.b])
                        psv = ps_pool.alloc()
                        mm((psv, v2(psv.t[:, :])), [(dgv.t[:, i, :], padv2(pv.t, t, kr, kc)) for i, (kr, kc) in pe_taps],
                           R=[dgv.b, pv.b])
                        ga = tmp_pool.alloc()
                        if ve_taps:
                            aa, av = accs.pop(("a", t)), accs.pop(("v", t))
                            tt(aa.t[:, :], aa.t[:, :], psa.t[:, :], ALU.add, R=[aa.b, psa.b], W=[aa.b])
                            ps_pool.free(psa)
                            act(ga.t[:, :], aa.t[:, :], AF.Gelu, bias=pcol(l, "ffn_conv_b", j, 1), R=[aa.b, b_pp], W=[ga.b])
                            tmp_pool.free(aa)
                            tt(av.t[:, :], av.t[:, :], psv.t[:, :], ALU.add, R=[av.b, psv.b], W=[av.b])
                            ps_pool.free(psv)
                            stt(ffact[:, j, tsl(t)], av.t[:, :], pcol(l, "ffn_conv_b", 22 + j, 1), ga.t[:, :], ALU.add, ALU.mult,
                                R=[av.b, ga.b, b_pp], W=[b_ff[j][t]])
                            tmp_pool.free(av)
                        else:
                            act(ga.t[:, :], psa.t[:, :], AF.Gelu, bias=pcol(l, "ffn_conv_b", j, 1), R=[psa.b, b_pp], W=[ga.b])
                            ps_pool.free(psa)
                            stt(ffact[:, j, tsl(t)], psv.t[:, :], pcol(l, "ffn_conv_b", 22 + j, 1), ga.t[:, :], ALU.add, ALU.mult,
                                R=[psv.b, ga.b, b_pp], W=[b_ff[j][t]])
                            ps_pool.free(psv)
                        tmp_pool.free(ga)
                    pad_pool.free(pa)
                    pad_pool.free(pv)
                    dg_pool.free(dga)
                    dg_pool.free(dgv)

                prevj = None
                for j in range(22):
                    q, jq = divmod(j, 2)
                    if jq == 0:
                        slot = w_get(tag + "up%d" % q)
                    ffn_front(j, slot, jq)
                    if jq == 1:
                        w_free(slot)
                        if q % 2 == 1:
                            next(mgen, None)
                    if prevj is not None:
                        ffn_back(prevj)
                    prevj = j
                ffn_back(prevj)
                for _ in mgen:
                    pass
                S.phase = kind + str(l) + ":ffndown"
                pend["st"] = stats_begin()
                for hh in range(2):
                    dsl = [w_get(tag + "down%d_%d" % (hh, kg)) for kg in range(3)]
                    for t in range(NT):
                        for mq in range(4):
                            m = hh * 4 + mq
                            ps = ps_pool.alloc()
                            mm(ps, [(dsl[k // 8].t[:, k % 8, mq * 128:(mq + 1) * 128], ffact[:, k, tsl(t)]) for k in range(22)],
                               R=[s_.b for s_ in dsl] + [b_ff[k][t] for k in range(22)])
                            stt(xT[:, m, tsl(t)], ps.t[:, :], modT[:, 40 + m:41 + m], xT[:, m, tsl(t)], ALU.mult, ALU.add,
                                R=[ps.b, b_modl[l], b_xT[m][t]], W=[b_xT[m][t]])
                            ps_pool.free(ps)
                            stats_add_delayed(pend["st"], m, t)
                    for s_ in dsl:
                        w_free(s_)

            pend = {"st": None}
            for l in range(DEPTH):
                layer(l)
            S.phase = kind + ":final"
            yview = R1[:, 0:16384].bitcast(F32).rearrange("p (c t) -> p c t", c=8)
            ydst = yP if P else yS

            def out_tile(t):
                S.op("sp", lambda h, t=t: [h.dma_start(out=ydst[:, :, tsl(t)], in_=yview[:, :, tsl(t)])],
                     reads=[b for c in range(8) for b in r1f_bufs(c, t)], dma_buf=b_out)
            rmsnorm_to(0, lambda c: pg[:, c:c + 1], None, lambda c, t: yview[:, c, tsl(t)], None,
                       dst_buf_fn=r1f_bufs, st=pend["st"], per_tile_cb=out_tile)
            if P:
                S.op("sp", lambda h: [h.dma_start(out=stO, in_=stT[:, :])], reads=[b_st], dma_buf=b_st2)

        run_pass("P")
        run_pass("S")
        assert wstate["next_get"] == len(wq), (wstate["next_get"], len(wq))
        build_nc.pe_log = S.pe_log
        build_nc.order = recorded
        S.emit()
    return nc


def _fm(v):
    v = np.asarray(v, dtype=np.float32).reshape(-1)
    return np.ascontiguousarray(v.reshape(-1, 128).T)


_NC_CACHE = {}


def kernel(x_prompt, x_sample, state_lru, c, c_ctx, norm1_g, norm2_g, w_mod, b_mod, w_in, b_gate,
           lru_conv_w, lru_conv_b, lru_wa, lru_ba, lru_wx, lru_bx, lru_lam, lru_out, fourier_out,
           conf_dw_w, conf_dw_b, conf_ln_g, conf_ln_b, conf_out, sc_conv_w, sc_conv_b, sc_out, w_o,
           ffn_up, ffn_conv_w, ffn_conv_b, ffn_down, final_g):
    f32 = lambda a: np.ascontiguousarray(np.asarray(a, dtype=np.float32))
    loc = dict(norm1_g=norm1_g, norm2_g=norm2_g, b_mod=b_mod, b_gate=b_gate, lru_conv_w=lru_conv_w,
               lru_conv_b=lru_conv_b, lru_ba=lru_ba, lru_bx=lru_bx, lru_lam=lru_lam, conf_dw_w=conf_dw_w,
               conf_dw_b=conf_dw_b, conf_ln_g=conf_ln_g, conf_ln_b=conf_ln_b, sc_conv_w=sc_conv_w,
               sc_conv_b=sc_conv_b, ffn_conv_w=ffn_conv_w, ffn_conv_b=ffn_conv_b)
    pp = np.zeros((DEPTH, 128, NP), np.float32)
    for l in range(DEPTH):
        for name, (o, n) in PL.items():
            pp[l, :, o:o + n] = _fm(np.asarray(loc[name])[l])
    m = np.arange(128)
    ang = 2.0 * np.pi * np.outer(m, m) / 128.0
    cs128 = np.concatenate([np.cos(ang), np.sin(ang)], axis=1).astype(np.float32)

    def dft(Lh):
        i = np.arange(Lh)
        a = 2.0 * np.pi * (np.outer(i, i) % Lh) / Lh
        s = 1.0 / np.sqrt(Lh * 128.0)
        return (np.cos(a) * s).astype(np.float32), (-np.sin(a) * s).astype(np.float32)
    c256, s256 = dft(256)
    dft256 = np.ascontiguousarray(np.concatenate([c256, s256], axis=1))
    li = np.arange(1024)
    ang_ = 2.0 * np.pi * (np.outer(li, np.arange(512)) % 1024) / 1024.0
    sc_ = 1.0 / np.sqrt(1024 * 128.0)
    cfull = (np.cos(ang_) * sc_).astype(np.float32)
    sfull = (-np.sin(ang_) * sc_).astype(np.float32)
    dftc = np.concatenate([cfull[0::2], cfull[1::2]], axis=0)
    dfts = np.concatenate([sfull[0::2], sfull[1::2]], axis=0)
    ident = np.eye(128, dtype=np.float32)

    shared = dict(pp=pp, w_mod=f32(w_mod), w_in=f32(w_in), lru_wa=f32(lru_wa), lru_wx=f32(lru_wx),
                  lru_out=f32(lru_out), fourier_out=f32(fourier_out), conf_out=f32(conf_out), sc_out=f32(sc_out),
                  w_o=f32(w_o), ffn_up=f32(ffn_up), ffn_down=f32(ffn_down), cs128=cs128, dft256=dft256,
                  dftc=np.ascontiguousarray(dftc), dfts=np.ascontiguousarray(dfts), ident=ident)
    x_prompt = np.asarray(x_prompt, np.float32)
    x_sample = np.asarray(x_sample, np.float32)
    state_lru = np.asarray(state_lru, np.float32)
    c = np.asarray(c, np.float32)
    in_maps = []
    for i in range(NCORES):
        pg = np.zeros((128, NG), np.float32)
        pg[:, 0:8] = _fm(final_g)
        pg[:, 8:16] = _fm(c_ctx)
        pg[:, 16:24] = _fm(c[i])
        pg[:, 24:56] = _fm(state_lru[i])
        d = dict(shared)
        d["xP"] = np.ascontiguousarray(x_prompt[2 * i:2 * i + 2].reshape(512, D).T)
        d["xS"] = np.ascontiguousarray(x_sample[i].T)
        d["pg"] = pg
        in_maps.append(d)
    if "nc" not in _NC_CACHE:
        build_nc()
        _NC_CACHE["nc"] = build_nc(order=list(build_nc.order))
    nc = _NC_CACHE["nc"]
    res = run_bass_kernel_spmd(nc, in_maps, core_ids=list(range(NCORES)))
    y_prompt = np.zeros((16, 256, D), np.float32)
    y_sample = np.zeros((8, 1024, D), np.float32)
    new_state = np.zeros((16, DEPTH, 2, D), np.float32)
    for i in range(NCORES):
        r = res.results[i]
        yp = np.asarray(r["yP"]).transpose(2, 1, 0).reshape(512, D)
        y_prompt[2 * i] = yp[0:256]
        y_prompt[2 * i + 1] = yp[256:512]
        y_sample[i] = np.asarray(r["yS"]).transpose(2, 1, 0).reshape(1024, D)
        st = np.asarray(r["stO"]).reshape(128, DEPTH, 2, 8, 2)
        for s in range(2):
            new_state[2 * i + s] = st[:, :, :, :, s].transpose(1, 2, 3, 0).reshape(DEPTH, 2, D)
    return (y_prompt, y_sample, new_state)
```
